# Optimizing a Trainium2 kernel written in Bass

```python
import math
import jax, jax.numpy as jnp
from jax import lax
import numpy as np

D_MODEL = 2048
BATCH = 8
SEQ = 2048
DEPTH = 2

N_A_LAYERS = DEPTH // 2
N_B_LAYERS = DEPTH - N_A_LAYERS
MEM_LEN = 256
MEM_HEADS = 4
MEM_HEAD_DIM = 128
MEM_WIDTH = MEM_HEADS * MEM_HEAD_DIM
MIX_WIDTH = D_MODEL
POOL_WIDTH = MIX_WIDTH - MEM_WIDTH
POOL_WINDOWS = (2, 4, 8, 16)
N_POOL_GROUPS = len(POOL_WINDOWS)
POOL_GROUP = POOL_WIDTH // N_POOL_GROUPS
DIFF_HEAD_DIM = 64
DIFF_V_DIM = 2 * DIFF_HEAD_DIM
DIFF_HEADS = POOL_WIDTH // DIFF_V_DIM
DIFF_QK_WIDTH = DIFF_HEADS * DIFF_HEAD_DIM
DIFF_V_WIDTH = DIFF_HEADS * DIFF_V_DIM
KV_WIDTH = 2 * DIFF_QK_WIDTH + DIFF_V_WIDTH
ROPE_THETA = 500000.0
ROPE_DIM = DIFF_HEAD_DIM // 4
D_FF = 4 * D_MODEL
Q_BLOCK = 128
EPS = 1e-6
NEG_INF = -1e30

kernel_name = 'yoco_pool_diffattn_memory_hybrid'


def rms_norm(x, g):
    xf = x.astype(jnp.float32)
    y = xf * lax.rsqrt(jnp.mean(xf * xf, axis=-1, keepdims=True) + EPS)
    return (y * g.astype(jnp.float32)).astype(x.dtype)


def rope_tables(seq):
    pos = jnp.arange(seq, dtype=jnp.float32)
    inv = ROPE_THETA ** (-(jnp.arange(ROPE_DIM // 2, dtype=jnp.float32) * 2.0) / ROPE_DIM)
    ang = pos[:, None] * inv[None, :]
    return jnp.cos(ang), jnp.sin(ang)


def partial_rope(t, cos, sin):
    half = ROPE_DIM // 2
    c = cos[None, :, None, :].astype(t.dtype)
    s = sin[None, :, None, :].astype(t.dtype)
    r1, r2, rest = t[..., :half], t[..., half:ROPE_DIM], t[..., ROPE_DIM:]
    return jnp.concatenate([r1 * c - r2 * s, r1 * s + r2 * c, rest], axis=-1)


def causal_multiscale_pool(u):
    B, S, _ = u.shape
    groups = u.astype(jnp.float32).reshape(B, S, N_POOL_GROUPS, POOL_GROUP)
    csum = jnp.cumsum(groups, axis=1)
    t = jnp.arange(S)
    outs = []
    for g, w in enumerate(POOL_WINDOWS):
        cg = csum[:, :, g]
        lag = jnp.pad(cg, ((0, 0), (w, 0), (0, 0)))[:, :S]
        count = jnp.minimum(t + 1, w).astype(jnp.float32)[None, :, None]
        outs.append((cg - lag) / count)
    mean = jnp.stack(outs, axis=2)
    return (mean - groups).astype(u.dtype)


def memory_cross_attention(q, mem, mem_norm_g, w_mem_kv, mq_g, mk_g):
    B, S, _ = q.shape
    M = mem.shape[1]
    mkv = rms_norm(mem, mem_norm_g) @ w_mem_kv
    k = mkv[..., :MEM_WIDTH].reshape(B, M, MEM_HEADS, MEM_HEAD_DIM)
    v = mkv[..., MEM_WIDTH:].reshape(B, M, MEM_HEADS, MEM_HEAD_DIM)
    q = rms_norm(q.reshape(B, S, MEM_HEADS, MEM_HEAD_DIM), mq_g)
    k = rms_norm(k, mk_g)
    s = jnp.einsum('bqhd,bkhd->bhqk', q, k).astype(jnp.float32) * (MEM_HEAD_DIM ** -0.5)
    p = jax.nn.softmax(s, axis=-1).astype(v.dtype)
    o = jnp.einsum('bhqk,bkhd->bqhd', p, v)
    return o.reshape(B, S, MEM_WIDTH)


def differential_attention(q1, q2, k1, k2, v, lam):
    B, S, H, dk = q1.shape
    dv = v.shape[-1]
    nb = S // Q_BLOCK
    scale = dk ** -0.5
    key_pos = jnp.arange(S)

    def blockify(t):
        return t.reshape(B, nb, Q_BLOCK, H, dk).transpose(1, 0, 2, 3, 4)

    def one_block(args):
        q1b, q2b, start = args
        q_pos = start + jnp.arange(Q_BLOCK)
        mask = key_pos[None, :] <= q_pos[:, None]

        def probs(qb, k):
            s = jnp.einsum('bqhd,bkhd->bhqk', qb, k).astype(jnp.float32) * scale
            return jax.nn.softmax(jnp.where(mask, s, NEG_INF), axis=-1)

        a = probs(q1b, k1) - lam * probs(q2b, k2)
        return jnp.einsum('bhqk,bkhd->bqhd', a.astype(v.dtype), v)

    starts = jnp.arange(nb) * Q_BLOCK
    out = lax.map(one_block, (blockify(q1), blockify(q2), starts))
    return out.transpose(1, 0, 2, 3, 4).reshape(B, S, H, dv)


def shared_kv(x, kv_norm, w_kv, k_norm, cos, sin):
    B, S, _ = x.shape
    kv = rms_norm(x, kv_norm) @ w_kv
    k1 = kv[..., :DIFF_QK_WIDTH].reshape(B, S, DIFF_HEADS, DIFF_HEAD_DIM)
    k2 = kv[..., DIFF_QK_WIDTH:2 * DIFF_QK_WIDTH].reshape(B, S, DIFF_HEADS, DIFF_HEAD_DIM)
    v = kv[..., 2 * DIFF_QK_WIDTH:].reshape(B, S, DIFF_HEADS, DIFF_V_DIM)
    k1 = partial_rope(rms_norm(k1, k_norm), cos, sin)
    k2 = partial_rope(rms_norm(k2, k_norm), cos, sin)
    return k1, k2, v


def squared_relu_mlp(h, w1, w2):
    z = jax.nn.relu(h @ w1)
    return (z * z) @ w2


def setup_inputs(seed: int = 0) -> dict:
    key = jax.random.key(seed)
    ks = jax.random.split(key, 20)
    f32 = jnp.float32

    def nrm(k, shape, scale):
        return jax.random.normal(k, shape, f32) * scale

    def gain(k, shape, noise=0.02):
        return 1.0 + noise * jax.random.normal(k, shape, f32)

    return {
        'x': nrm(ks[0], (BATCH, SEQ, D_MODEL), 1.0),
        'mem': nrm(ks[1], (BATCH, MEM_LEN, D_MODEL), 1.0),
        'mix_norm': gain(ks[2], (DEPTH, D_MODEL)),
        'w_in': nrm(ks[3], (DEPTH, D_MODEL, MIX_WIDTH), D_MODEL ** -0.5),
        'w_out': nrm(ks[4], (DEPTH, MIX_WIDTH, D_MODEL), 0.5 * MIX_WIDTH ** -0.5),
        'mem_norm': gain(ks[5], (DEPTH, D_MODEL)),
        'w_mem_kv': nrm(ks[6], (DEPTH, D_MODEL, 2 * MEM_WIDTH), D_MODEL ** -0.5),
        'mem_q_norm': gain(ks[7], (DEPTH, MEM_HEAD_DIM)),
        'mem_k_norm': gain(ks[8], (DEPTH, MEM_HEAD_DIM)),
        'ffn_norm': gain(ks[9], (DEPTH, D_MODEL)),
        'w_ff1': nrm(ks[10], (DEPTH, D_MODEL, D_FF), D_MODEL ** -0.5),
        'w_ff2': nrm(ks[11], (DEPTH, D_FF, D_MODEL), 0.5 * D_FF ** -0.5),
        'pool_w': nrm(ks[12], (N_A_LAYERS, N_POOL_GROUPS, POOL_GROUP, POOL_GROUP), POOL_GROUP ** -0.5),
        'pool_scale': gain(ks[13], (N_A_LAYERS, POOL_WIDTH), 0.1),
        'kv_norm': gain(ks[14], (D_MODEL,)),
        'w_kv': nrm(ks[15], (D_MODEL, KV_WIDTH), D_MODEL ** -0.5),
        'k_norm': gain(ks[16], (DIFF_HEAD_DIM,)),
        'q_norm': gain(ks[17], (N_B_LAYERS, DIFF_HEAD_DIM)),
        'diff_lambda': nrm(ks[18], (N_B_LAYERS, 4, DIFF_HEAD_DIM), 0.1),
        'subln_norm': gain(ks[19], (N_B_LAYERS, DIFF_V_DIM)),
    }


def reference(x, mem, mix_norm, w_in, w_out, mem_norm, w_mem_kv, mem_q_norm, mem_k_norm,
              ffn_norm, w_ff1, w_ff2, pool_w, pool_scale, kv_norm, w_kv, k_norm, q_norm,
              diff_lambda, subln_norm):
    B, S, _ = x.shape
    cos, sin = rope_tables(S)
    sk1 = sk2 = sv = None
    for i in range(DEPTH):
        if i == N_A_LAYERS:
            sk1, sk2, sv = shared_kv(x, kv_norm, w_kv, k_norm, cos, sin)
        u = rms_norm(x, mix_norm[i]) @ w_in[i]
        if i < N_A_LAYERS:
            a = i
            pooled = causal_multiscale_pool(u[..., :POOL_WIDTH])
            token_out = jnp.einsum('bsgc,gcd->bsgd', pooled, pool_w[a]).reshape(B, S, POOL_WIDTH) * pool_scale[a]
            mem_q = u[..., POOL_WIDTH:]
        else:
            b = i - N_A_LAYERS
            lam_init = 0.8 - 0.6 * math.exp(-0.3 * i)
            q1 = u[..., :DIFF_QK_WIDTH].reshape(B, S, DIFF_HEADS, DIFF_HEAD_DIM)
            q2 = u[..., DIFF_QK_WIDTH:2 * DIFF_QK_WIDTH].reshape(B, S, DIFF_HEADS, DIFF_HEAD_DIM)
            q1 = partial_rope(rms_norm(q1, q_norm[b]), cos, sin)
            q2 = partial_rope(rms_norm(q2, q_norm[b]), cos, sin)
            lq = diff_lambda[b].astype(jnp.float32)
            lam = jnp.exp(jnp.sum(lq[0] * lq[1])) - jnp.exp(jnp.sum(lq[2] * lq[3])) + lam_init
            o = differential_attention(q1, q2, sk1, sk2, sv, lam)
            token_out = (rms_norm(o, subln_norm[b]) * (1.0 - lam_init)).reshape(B, S, DIFF_V_WIDTH)
            mem_q = u[..., 2 * DIFF_QK_WIDTH:]
        mem_out = memory_cross_attention(mem_q, mem, mem_norm[i], w_mem_kv[i], mem_q_norm[i], mem_k_norm[i])
        x = x + jnp.concatenate([token_out, mem_out], axis=-1) @ w_out[i]
        x = x + squared_relu_mlp(rms_norm(x, ffn_norm[i]), w_ff1[i], w_ff2[i])
    return x
```

```python
import math
import os
from contextlib import ExitStack

import ml_dtypes
import numpy as np

import concourse.bass as bass
import concourse.mybir as mybir
from concourse.bass_utils import run_bass_kernel_spmd

F32 = mybir.dt.float32
BF16 = mybir.dt.bfloat16
U8 = mybir.dt.uint8
AF = mybir.ActivationFunctionType
ALU = mybir.AluOpType
AX = mybir.AxisListType

S_ = 2048
D = 2048
KC = 16
MEM = 256
EPS = 1e-6
LAM_INIT = 0.8 - 0.6 * math.exp(-0.3 * 1)
ROPE_THETA = 500000.0


class Tile:
    __slots__ = ("ap", "w", "r", "rd", "name", "off", "size")

    def __init__(self, ap, name="", off=-1, size=0):
        self.ap = ap
        self.w = None
        self.r = {}
        self.rd = []
        self.name = name
        self.off = off
        self.size = size


class Op:
    __slots__ = ("eng", "fn", "deps", "marked", "semval", "is_dma", "slot", "dval", "idx")

    def __init__(self, eng, fn, is_dma=False):
        self.eng = eng
        self.fn = fn
        self.deps = set()
        self.marked = False
        self.semval = 0
        self.is_dma = is_dma
        self.slot = -1
        self.dval = 0
        self.idx = 0


ENGS = ("pe", "act", "dve", "pool", "sp")
RING = {"sp": 16, "pool": 8}


class Sched:
    def __init__(self):
        self.ops = {e: [] for e in ENGS}
        self.dma_count = {q: 0 for q in RING}
        self.dma_last = {q: [None] * RING[q] for q in RING}

    def add(self, eng, fn, reads=(), writes=(), dma=False):
        o = Op(eng, fn, dma)
        deps = o.deps
        for t in reads:
            if t.w is not None:
                deps.add(t.w)
        for t in writes:
            if t.w is not None:
                deps.add(t.w)
            deps.update(t.r.values())
            deps.update(t.rd)
        for t in reads:
            if dma:
                t.rd.append(o)
            else:
                t.r[eng] = o
        for t in writes:
            t.w = o
            t.r = {}
            t.rd = []
        if dma:
            k = self.dma_count[eng]
            R = RING[eng]
            o.slot = k % R
            o.dval = 16 * (k // R + 1)
            prev = self.dma_last[eng][o.slot]
            if prev is not None:
                deps.add(prev)
            self.dma_last[eng][o.slot] = o
            self.dma_count[eng] = k + 1
        if eng == "pe" and not dma:
            o.deps = {d for d in deps if d.is_dma or d.eng != "pe"}
        o.deps.discard(o)
        o.idx = len(self.ops[eng])
        self.ops[eng].append(o)
        return o

    def T(self, fn, reads=(), writes=()):
        return self.add("pe", fn, reads, writes)

    def A(self, fn, reads=(), writes=()):
        return self.add("act", fn, reads, writes)

    def V(self, fn, reads=(), writes=()):
        return self.add("dve", fn, reads, writes)

    def G(self, fn, reads=(), writes=()):
        return self.add("pool", fn, reads, writes)

    def dma(self, q, out_ap, in_ap, reads=(), writes=()):
        return self.add(q, lambda e: e.dma_start(out=out_ap, in_=in_ap), reads, writes, dma=True)

    def finalize(self):
        for e in ENGS:
            for o in self.ops[e]:
                for d in o.deps:
                    if not d.is_dma:
                        d.marked = True
        for e in ENGS:
            c = 0
            for o in self.ops[e]:
                if (not o.is_dma) and o.marked:
                    c += 1
                    o.semval = c

    def emit(self, block, engsem, rings):
        self.finalize()
        sched = self

        def run(ename, e):
            seen = {}
            for o in sched.ops[ename]:
                for d in sorted(o.deps, key=lambda d: (d.eng, d.idx)):
                    if d.is_dma:
                        key = (d.eng, d.slot)
                        sem = rings[d.eng][d.slot]
                        val = d.dval
                    else:
                        key = d.eng
                        sem = engsem[d.eng]
                        val = d.semval
                    if seen.get(key, 0) >= val:
                        continue
                    seen[key] = val
                    e.wait_ge(sem, val)
                ins = o.fn(e)
                if o.is_dma:
                    ins.then_inc(rings[ename][o.slot], 16)
                elif o.marked:
                    ins.then_inc(engsem[ename], 1)
            if ename in RING:
                for s, last in enumerate(sched.dma_last[ename]):
                    if last is not None and seen.get((ename, s), 0) < last.dval:
                        e.wait_ge(rings[ename][s], last.dval)

        @block.tensor
        def _(e):
            run("pe", e)

        @block.scalar
        def _(e):
            run("act", e)

        @block.vector
        def _(e):
            run("dve", e)

        @block.gpsimd
        def _(e):
            run("pool", e)

        @block.sync
        def _(e):
            run("sp", e)


class Arena:
    def __init__(self, ap, nbytes, sched):
        self.ap = ap
        self.nbytes = nbytes
        self.live = []
        self.S = sched
        self.top = 0

    def reset(self, top=0):
        self.top = top

    def alloc(self, shape, dt, name=""):
        isz = 4 if dt == F32 else 2
        n = int(np.prod(shape)) * isz
        off = (self.top + 31) // 32 * 32
        assert off + n <= self.nbytes, (name, off, n, self.nbytes)
        self.top = off + n
        return self.at(off, shape, dt, name)

    def at(self, off, shape, dt, name=""):
        isz = 4 if dt == F32 else 2
        n = int(np.prod(shape)) * isz
        assert off + n <= self.nbytes, (name, off, n, self.nbytes)
        ap = self.ap[:, off:off + n].bitcast(dt)
        if len(shape) == 2:
            ap = ap.rearrange("p (a b) -> p a b", a=shape[0])
        elif len(shape) == 3:
            ap = ap.rearrange("p (a b c) -> p a b c", a=shape[0], b=shape[1])
        t = Tile(ap, name, off, n)
        keep = []
        ops = self.S.ops
        for o in self.live:
            if o.off < off + n and off < o.off + o.size:
                cands = list(o.r.values())
                if o.w is not None:
                    if o.w.is_dma:
                        t.rd.append(o.w)
                    else:
                        cands.append(o.w)
                for c in cands:
                    cur = t.r.get(c.eng)
                    if cur is None or cur.idx < c.idx:
                        t.r[c.eng] = c
                t.rd.extend(o.rd)
            else:
                keep.append(o)
        keep.append(t)
        self.live = keep
        return t


P_MIX = 0
P_FFN = 32
P_MEMN = 64
P_KVN = 96
P_PSC = 112
P_MQ = 124
P_MK = 126
P_KN = 128
P_QN = 129
P_SUB = 130
P_DL = 131
P_INVC = 387
P_IDENT = 451
PC = 579

C_IDENT, C_ONESD, C_ONESHD, C_ONES1, C_BLK, C_RT, C_TRI, C_TRIB = range(8)
NCB = 8


def _vec16(v):
    return np.asarray(v, np.float32).reshape(16, 128).T


def pack_params(inp):
    p = np.zeros((128, PC), np.float32)
    for l in range(2):
        p[:, P_MIX + 16 * l:P_MIX + 16 * l + 16] = _vec16(inp["mix_norm"][l])
        p[:, P_FFN + 16 * l:P_FFN + 16 * l + 16] = _vec16(inp["ffn_norm"][l])
        p[:, P_MEMN + 16 * l:P_MEMN + 16 * l + 16] = _vec16(inp["mem_norm"][l])
        p[:, P_MQ + l] = inp["mem_q_norm"][l]
        p[:, P_MK + l] = inp["mem_k_norm"][l]
    p[:, P_KVN:P_KVN + 16] = _vec16(inp["kv_norm"])
    p[:, P_PSC:P_PSC + 12] = np.asarray(inp["pool_scale"][0], np.float32).reshape(12, 128).T
    p[:, P_KN] = np.tile(np.asarray(inp["k_norm"], np.float32), 2)
    p[:, P_QN] = np.tile(np.asarray(inp["q_norm"][0], np.float32), 2)
    p[:, P_SUB] = inp["subln_norm"][0]
    p[:, P_DL:P_DL + 256] = np.asarray(inp["diff_lambda"][0], np.float32).reshape(1, 256)
    for g, w in enumerate((2, 4, 8, 16)):
        t = np.arange(16)
        p[:, P_INVC + 16 * g:P_INVC + 16 * g + 16] = (1.0 / np.minimum(t + 1, w)).astype(np.float32)[None, :]
    p[:, P_IDENT:P_IDENT + 128] = np.eye(128, dtype=np.float32)
    return p


def const_bf16():
    c = np.zeros((128, NCB, 128), np.float32)
    c[:, C_IDENT] = np.eye(128)
    c[:, C_ONESD] = 1.0 / 2048.0
    c[:, C_ONESHD] = 1.0 / 128.0
    c[:, C_ONES1] = 1.0
    blk = np.zeros((128, 128))
    blk[:64, :64] = 1.0 / 64.0
    blk[64:, 64:] = 1.0 / 64.0
    c[:, C_BLK] = blk
    Rm = np.zeros((128, 128))
    for hb in (0, 64):
        for j in range(8):
            Rm[hb + j, hb + j + 8] = -1.0
            Rm[hb + j + 8, hb + j] = 1.0
    c[:, C_RT] = Rm.T
    k = np.arange(128)[:, None]
    q = np.arange(128)[None, :]
    c[:, C_TRI] = (k <= q).astype(np.float32)
    c[:, C_TRIB] = np.where(k <= q, 0.0, -30000.0)
    return c.reshape(128, NCB * 128).astype(ml_dtypes.bfloat16)


def rope_cs():
    pos = np.arange(S_, dtype=np.float32)
    inv = (np.float32(ROPE_THETA) ** (-(np.arange(8, dtype=np.float32) * np.float32(2.0)) / np.float32(16))).astype(np.float32)
    ang = (pos[:, None] * inv[None, :]).astype(np.float32)
    cs = np.zeros((128, 2, S_), np.float32)
    cs[:, 0, :] = 1.0
    for hb in (0, 64):
        for j in range(16):
            cs[hb + j, 0, :] = np.cos(ang[:, j % 8])
            cs[hb + j, 1, :] = np.sin(ang[:, j % 8])
    return cs


ARENA_BYTES = 187 * 1024
SUB = int(os.environ.get('MK_SUB', '99'))
SUB2 = int(os.environ.get('MK_SUB2', '99'))
NBW = 4


def build_nc(stop_after=99, dbg=False):
    nc = bass.Bass("TRN2", target_bir_lowering=False)
    okind = "ExternalOutput" if dbg else "Internal"

    def din(name, shape, dt=F32):
        return nc.dram_tensor(name, list(shape), dt, kind="ExternalInput").ap()

    x_in = din("x", [S_, D])
    mem_in = din("mem", [MEM, D])
    w_in = [din(f"w_in{l}", [D, D]) for l in range(2)]
    w_out = [din(f"w_out{l}", [D, D]) for l in range(2)]
    w_mkv = [din(f"w_mkv{l}", [D, 1024]) for l in range(2)]
    w_ff1 = [din(f"w_ff1{l}", [D, 4 * D]) for l in range(2)]
    w_ff2 = [din(f"w_ff2{l}", [4 * D, D]) for l in range(2)]
    pool_w = din("pool_w", [4, 384, 384])
    w_kv = din("w_kv", [D, 3072])
    params_d = din("params", [128, PC])
    cbf_d = din("cbf", [128, NCB * 128], BF16)
    cs_d = din("ropecs", [128, 2, S_])
    out_d = nc.dram_tensor("out", [S_, D], F32, kind="ExternalOutput").ap()

    xs = nc.dram_tensor("xs", [D, S_], F32, kind=okind).ap()
    memT = nc.dram_tensor("memT", [D, MEM], F32, kind=okind).ap()
    dbg_b = nc.dram_tensor("dbg_b", [128, 2048], BF16, kind=okind).ap()
    dbg_h = nc.dram_tensor("dbg_h", [128, 4096], BF16, kind=okind).ap()
    cat_d = nc.dram_tensor("cat_d", [16, 128, S_], BF16, kind=okind).ap()
    kT_d = nc.dram_tensor("kT_d", [12, 128, S_], BF16, kind=okind).ap()
    qT_d = nc.dram_tensor("qT_d", [12, 128, S_], BF16, kind=okind).ap()
    vT_d = nc.dram_tensor("vT_d", [12, 128, S_], BF16, kind=okind).ap()

    S = Sched()
    with ExitStack() as ctx:
        engsem = {e: ctx.enter_context(nc.semaphore("sem_" + e)) for e in ENGS}
        rings = {q: [ctx.enter_context(nc.semaphore(f"r_{q}_{i}")) for i in range(RING[q])] for q in RING}
        params_t = ctx.enter_context(nc.sbuf_tensor("sb_params", [128, PC], F32))
        cbf_t = ctx.enter_context(nc.sbuf_tensor("sb_cbf", [128, NCB, 128], BF16))
        small_t = ctx.enter_context(nc.sbuf_tensor("sb_small", [128, 16], F32))
        wring_t = ctx.enter_context(nc.sbuf_tensor("sb_wring", [128, NBW, 16, 128], BF16))
        arena_t = ctx.enter_context(nc.sbuf_tensor("sb_arena", [128, ARENA_BYTES], U8))
        ps_t = ctx.enter_context(nc.psum_tensor("ps_all", [128, 8, 512], F32))
        ps = ps_t[:, :, :]
        PB = [Tile(ps[:, b, :], f"ps{b}") for b in range(8)]
        params = Tile(params_t[:, :], "params")
        cbf = Tile(cbf_t[:, :, :], "cbf")
        small = Tile(small_t[:, :], "small")
        wring = [Tile(wring_t[:, i, :, :], f"w{i}") for i in range(NBW)]
        AR = Arena(arena_t[:, :], ARENA_BYTES, S)
        XS = [[Tile(None, f"xs{dc}_{tb}") for tb in range(4)] for dc in range(KC)]
        MEMT = Tile(None, "memT")
        CAT = [Tile(None, f"cat{i}") for i in range(16)]
        KT = [Tile(None, f"kT{i}") for i in range(12)]
        QT = [Tile(None, f"qT{i}") for i in range(12)]
        VT = [Tile(None, f"vT{i}") for i in range(12)]

        def xs_cols(c0, n):
            return [XS[dc][tb] for dc in range(KC) for tb in range(c0 // 512, (c0 + n + 511) // 512)]

        def xs_rows(dc, c0, n):
            return [XS[dc][tb] for tb in range(c0 // 512, (c0 + n + 511) // 512)]
        st = {"w": 0, "bank": 0}

        def pcol(c, n=1):
            return params.ap[:, c:c + n]

        def cm(i):
            return cbf.ap[:, i, :]

        ident_f = params.ap[:, P_IDENT:P_IDENT + 128]
        EPSC = small.ap[:, 0:1]
        NEGLAM = small.ap[:, 1:2]
        SUBG = small.ap[:, 2:3]

        def next_group(n):
            res = st.get("reserved", ())
            for _ in range(9):
                b = (st["bank"] + n - 1) // n * n
                if b + n > 8:
                    b = 0
                st["bank"] = b + n
                if not any((x in res) for x in range(b, b + n)):
                    return list(range(b, b + n))
            raise AssertionError("no free PSUM group")

        def next_bank():
            return next_group(1)[0]

        def psg(g):
            return ps[:, g[0]:g[0] + len(g), :]

        def pbs(g):
            return [PB[b] for b in g]

        def load_w(src, nk):
            wt = wring[st["w"] % NBW]
            st["w"] += 1
            S.dma("pool", wt.ap[:, 0:nk, :], src.rearrange("(kc p) c -> p kc c", p=128), writes=[wt])
            return wt

        def rsqrt_act(out_ap, in_ap, rd, wr):
            S.A(lambda e: e.activation(out_ap, in_ap, AF.Ln, bias=EPSC, scale=1.0), reads=rd + [small], writes=[wr])
            S.A(lambda e: e.activation(out_ap, out_ap, AF.Exp, scale=-0.5), reads=[wr], writes=[wr])

        def recip_act(out_ap, in_ap, rd, wr):
            S.A(lambda e: e.activation(out_ap, in_ap, AF.Ln), reads=rd, writes=[wr])
            S.A(lambda e: e.activation(out_ap, out_ap, AF.Exp, scale=-1.0), reads=[wr], writes=[wr])

        def gemm_fm(wsrc, n_oc, nk, rhs_tiles_fn, rhs_fn, ntb, evac, ncols=512, use_hooks=False):
            pending = []
            hooks = {max(0, nk // 4 - 1): 0, max(0, (5 * nk) // 8 - 1): 1} if (nk >= 8 and use_hooks) else {}

            def run_stage(h, oc):
                for p in list(pending):
                    if p[1] == h and p[2] < oc:
                        p[0][p[1]]()
                        p[1] += 1
                        p[2] = oc
                        if p[1] >= len(p[0]):
                            pending.remove(p)

            for oc in range(n_oc):
                gk = "g%d" % ntb
                gi = st.get(gk, 0)
                st[gk] = gi + 1
                ng = 8 // ntb
                g = list(range((gi % ng) * ntb, (gi % ng) * ntb + ntb))
                st["bank"] = g[-1] + 1
                st["reserved"] = set(g)
                nq = (nk + 15) // 16
                for kq in range(nq):
                    nkk = min(16, nk - kq * 16)
                    wt = load_w(wsrc(oc, kq), nkk)
                    for k16 in range(nkk):
                        kc = kq * 16 + k16
                        for tb in range(ntb):
                            S.T(lambda e, b=g[tb], wt=wt, k16=k16, kc=kc, tb=tb: e.matmul(
                                ps[:, b, 0:ncols], wt.ap[:, k16, :], rhs_fn(kc, tb),
                                start=(kc == 0), stop=(kc == nk - 1)),
                                reads=[wt] + rhs_tiles_fn(kc), writes=[PB[g[tb]]])
                        if kc in hooks:
                            run_stage(hooks[kc], oc)
                st["reserved"] = ()
                cur = evac(oc, g)
                if not hooks:
                    run_stage(0, oc + 1)
                    run_stage(1, oc + 1)
                if cur is not None:
                    pending.append([list(cur) if isinstance(cur, (list, tuple)) else [cur], 0, oc])
            k = n_oc
            while pending:
                k += 1
                run_stage(0, k)
                run_stage(1, k)

        def transpose_in(src, dst, ntok, dtiles):
            TW = min(512, ntok)
            ntt = TW // 128
            AR.reset()
            xin = [AR.alloc([ntt, D], F32, f"xin{i}") for i in range(2)]
            stg = [AR.alloc([TW], F32, f"stg{i}") for i in range(4)]
            k = 0
            for tb in range(ntok // TW):
                xt = xin[tb % 2]
                S.dma("sp", xt.ap, src[tb * TW:(tb + 1) * TW, :].rearrange("(tt p) d -> p tt d", p=128), writes=[xt])
                for dc in range(KC):
                    b = next_bank()
                    for tt in range(ntt):
                        S.T(lambda e, b=b, tt=tt, dc=dc, xt=xt: e.transpose(
                            ps[:, b, tt * 128:(tt + 1) * 128], xt.ap[:, tt, dc * 128:(dc + 1) * 128], ident_f),
                            reads=[xt, params], writes=[PB[b]])
                    sg = stg[k % 4]
                    eng = S.A if k % 2 == 0 else S.V
                    if k % 2 == 0:
                        S.A(lambda e, sg=sg, b=b: e.copy(sg.ap, ps[:, b, 0:TW]), reads=[PB[b]], writes=[sg])
                    else:
                        S.V(lambda e, sg=sg, b=b: e.tensor_copy(sg.ap, ps[:, b, 0:TW]), reads=[PB[b]], writes=[sg])
                    S.dma("sp", dst[dc * 128:(dc + 1) * 128, tb * TW:(tb + 1) * TW], sg.ap, reads=[sg], writes=dtiles(dc, tb))
                    k += 1

        def transpose_out():
            AR.reset()
            xin = [AR.alloc([KC, 512], F32, f"xo_in{i}") for i in range(2)]
            ost = [AR.alloc([D], F32, f"ost{i}") for i in range(2)]
            k = 0
            j = 0
            xsv = xs.rearrange("(dc p) t -> p dc t", p=128)
            for tb in range(4):
                xt = xin[tb % 2]
                S.dma("sp", xt.ap, xsv[:, :, tb * 512:(tb + 1) * 512], reads=xs_cols(tb * 512, 512), writes=[xt])
                for tt in range(4):
                    o = ost[j % 2]
                    j += 1
                    for dc4 in range(4):
                        b = next_bank()
                        for i in range(4):
                            S.T(lambda e, b=b, i=i, dc4=dc4, tt=tt, xt=xt: e.transpose(
                                ps[:, b, i * 128:(i + 1) * 128], xt.ap[:, dc4 * 4 + i, tt * 128:(tt + 1) * 128], ident_f),
                                reads=[xt, params], writes=[PB[b]])
                        if k % 2 == 0:
                            S.A(lambda e, o=o, b=b, dc4=dc4: e.copy(o.ap[:, dc4 * 512:(dc4 + 1) * 512], ps[:, b, :]),
                                reads=[PB[b]], writes=[o])
                        else:
                            S.V(lambda e, o=o, b=b, dc4=dc4: e.tensor_copy(o.ap[:, dc4 * 512:(dc4 + 1) * 512], ps[:, b, :]),
                                reads=[PB[b]], writes=[o])
                        k += 1
                    r0 = tb * 512 + tt * 128
                    S.dma("sp", out_d[r0:r0 + 128, :], o.ap, reads=[o])

        def norm_fm(src, gcol, h, tok0, T, TW, tmp_base, stiles=None):
            AR.reset(tmp_base)
            xts = [AR.alloc([KC, TW], F32, f"nx{i}") for i in range(2)]
            sq = AR.alloc([KC, TW], BF16, "nsq")
            rstd = AR.alloc([TW], F32, "nrstd")
            srcv = src.rearrange("(dc p) t -> p dc t", p=128)
            for tb in range(T // TW):
                xt = xts[tb % 2]
                c0 = tok0 + tb * TW
                S.dma("sp", xt.ap, srcv[:, :, c0:c0 + TW], reads=(stiles if stiles is not None else xs_cols(c0, TW)), writes=[xt])
                hk = KC // 2
                S.A(lambda e, xt=xt: e.activation(sq.ap[:, 0:hk, :], xt.ap[:, 0:hk, :], AF.Square), reads=[xt], writes=[sq])
                S.A(lambda e, xt=xt: e.activation(sq.ap[:, hk:KC, :], xt.ap[:, hk:KC, :], AF.Square), reads=[xt], writes=[sq])
                b = next_bank()
                for dc in range(KC):
                    S.T(lambda e, b=b, dc=dc: e.matmul(ps[:, b, 0:TW], cm(C_ONESD), sq.ap[:, dc, :],
                                                      start=(dc == 0), stop=(dc == KC - 1)),
                        reads=[sq, cbf], writes=[PB[b]])
                rsqrt_act(rstd.ap, ps[:, b, 0:TW], [PB[b]], rstd)
                for dc in range(KC):
                    S.V(lambda e, dc=dc, xt=xt, tb=tb: e.scalar_tensor_tensor(
                        h.ap[:, dc, tb * TW:(tb + 1) * TW], xt.ap[:, dc, :], pcol(gcol + dc), rstd.ap, ALU.mult, ALU.mult),
                        reads=[xt, rstd, params], writes=[h])

        def mem_prep(l, base):
            AR.reset(base)
            kT = AR.alloc([4, MEM], BF16, "mkT")
            vsb = AR.alloc([4, 2, 128], BF16, "mvsb")
            vsb.ap = AR.ap[:, vsb.off:vsb.off + vsb.size].bitcast(BF16).rearrange("p (h m d) -> p h m d", h=4, m=2)
            hm = AR.alloc([KC, MEM], BF16, "hm")
            kf = [AR.alloc([MEM], F32, f"mkf{i}") for i in range(2)]
            sqm = [AR.alloc([MEM], BF16, f"msq{i}") for i in range(2)]
            vf = [AR.alloc([MEM], BF16, f"mvf{i}") for i in range(2)]
            rs = AR.alloc([MEM], F32, "mrs")
            tmp_base = AR.top
            norm_fm(memT, P_MEMN + 16 * l, hm, 0, MEM, MEM, tmp_base, stiles=[MEMT])

            def evac(oc, g):
                b = g[0]
                if oc < 4:
                    k_f = kf[oc % 2]
                    s_q = sqm[oc % 2]
                    S.A(lambda e: e.copy(k_f.ap, ps[:, b, 0:MEM]), reads=[PB[b]], writes=[k_f])
                    S.A(lambda e: e.activation(s_q.ap, ps[:, b, 0:MEM], AF.Square), reads=[PB[b]], writes=[s_q])

                    def post():
                        b2 = next_bank()
                        S.T(lambda e: e.matmul(ps[:, b2, 0:MEM], cm(C_ONESHD), s_q.ap, start=True, stop=True),
                            reads=[s_q, cbf], writes=[PB[b2]])
                        rsqrt_act(rs.ap, ps[:, b2, 0:MEM], [PB[b2]], rs)
                        S.V(lambda e: e.scalar_tensor_tensor(kT.ap[:, oc, :], k_f.ap, pcol(P_MK + l), rs.ap, ALU.mult, ALU.mult),
                            reads=[k_f, rs, params], writes=[kT])
                    return post
                else:
                    hh = oc - 4
                    v_f = vf[oc % 2]
                    S.A(lambda e: e.copy(v_f.ap, ps[:, b, 0:MEM]), reads=[PB[b]], writes=[v_f])

                    def post():
                        b2 = next_bank()
                        pb16 = ps[:, b2, :].bitcast(BF16)
                        for mt in range(2):
                            S.T(lambda e, mt=mt: e.transpose(pb16[:, mt * 128:(mt + 1) * 128], v_f.ap[:, mt * 128:(mt + 1) * 128], cm(C_IDENT)),
                                reads=[v_f, cbf], writes=[PB[b2]])
                        S.V(lambda e: e.tensor_copy(vsb.ap[:, hh, :, :], pb16[:, 0:256].rearrange("p (m d) -> p m d", m=2)),
                            reads=[PB[b2]], writes=[vsb])
                    return post

            gemm_fm(lambda oc, kq: w_mkv[l][:, oc * 128:(oc + 1) * 128], 8, KC,
                    lambda kc: [hm], lambda kc, tb: hm.ap[:, kc, :], 1, evac, ncols=MEM)
            if dbg and l == 0:
                S.dma("sp", dbg_b[:, 0:1024], kT.ap.rearrange("p a b -> p (a b)"), reads=[kT])
                S.dma("sp", dbg_b[:, 1024:2048], vsb.ap.rearrange("p h m d -> p (h m d)"), reads=[vsb])
                S.dma("sp", dbg_h, hm.ap.rearrange("p a b -> p (a b)"), reads=[hm])
            return kT, vsb

        def mem_attn_bufs(rstd=None):
            d = {}
            d["rstd"] = rstd if rstd is not None else AR.alloc([S_], F32, "ma_rstd")
            d["qn"] = AR.alloc([S_], BF16, "ma_qn")
            d["E"] = [AR.alloc([512], BF16, f"ma_E{i}") for i in range(4)]
            d["rl"] = AR.alloc([512], F32, "ma_rl")
            d["osb"] = AR.alloc([512], F32, "ma_osb")
            d["ei"] = 0
            return d

        def mem_attn_head(l, hh, qf, sq, kT, vsb, mb, catst):
            g = next_group(4)
            for tb in range(4):
                S.T(lambda e, tb=tb: e.matmul(ps[:, g[tb], :], cm(C_ONESHD), sq.ap[:, tb * 512:(tb + 1) * 512], start=True, stop=True),
                    reads=[sq, cbf], writes=[PB[g[tb]]])
            rstd = mb["rstd"]
            qn = mb["qn"]
            rsqrt_act(rstd.ap.rearrange("p (b n) -> p b n", b=4), psg(g), pbs(g), rstd)
            S.V(lambda e: e.scalar_tensor_tensor(qn.ap, qf, pcol(P_MQ + l), rstd.ap, ALU.mult, ALU.mult),
                reads=[mb["qf_tile"], rstd, params], writes=[qn])
            sc = 128.0 ** -0.5
            if SUB2 <= 1:
                return
            for tb in range(4):
                Es = []
                for mt in range(2):
                    bS = next_bank()
                    S.T(lambda e, bS=bS, mt=mt, tb=tb: e.matmul(ps[:, bS, :], kT.ap[:, hh, mt * 128:(mt + 1) * 128],
                                                               qn.ap[:, tb * 512:(tb + 1) * 512], start=True, stop=True),
                        reads=[kT, qn], writes=[PB[bS]])
                    E = mb["E"][mb["ei"] % 4]
                    mb["ei"] += 1
                    S.A(lambda e, bS=bS, E=E: e.activation(E.ap, ps[:, bS, :], AF.Exp, scale=sc), reads=[PB[bS]], writes=[E])
                    Es.append(E)
                if SUB2 <= 2:
                    continue
                bo = next_bank()
                bl = next_bank()
                for mt in range(2):
                    S.T(lambda e, mt=mt, E=Es[mt], bo=bo: e.matmul(ps[:, bo, :], vsb.ap[:, hh, mt, :], E.ap, start=(mt == 0), stop=(mt == 1)),
                        reads=[vsb, Es[mt]], writes=[PB[bo]])
                for mt in range(2):
                    S.T(lambda e, mt=mt, E=Es[mt], bl=bl: e.matmul(ps[:, bl, :], cm(C_ONES1), E.ap, start=(mt == 0), stop=(mt == 1)),
                        reads=[cbf, Es[mt]], writes=[PB[bl]])
                if SUB2 <= 3:
                    continue
                rl = mb["rl"]
                osb = mb["osb"]
                recip_act(rl.ap, ps[:, bl, :], [PB[bl]], rl)
                S.V(lambda e, tb=tb, bo=bo: e.tensor_tensor(catst.ap[:, tb * 512:(tb + 1) * 512], ps[:, bo, :], rl.ap, ALU.mult),
                    reads=[PB[bo], rl], writes=[catst])
            if SUB2 <= 4:
                return
            S.dma("sp", cat_d[12 + hh], catst.ap, reads=[catst], writes=[CAT[12 + hh]])

        def out_proj(l):
            AR.reset()
            catq = [AR.alloc([4, S_], BF16, f"catq{i}") for i in range(4)]
            xo = [AR.alloc([S_], F32, f"xo{i}") for i in range(2)]
            cv = cat_d.rearrange("c p t -> p c t")
            for i in range(4):
                S.dma("sp", catq[i].ap, cv[:, 4 * i:4 * i + 4, :], reads=CAT[4 * i:4 * i + 4], writes=[catq[i]])

            def evac(oc, g):
                x_o = xo[oc % 2]
                S.dma("sp", x_o.ap, xs[oc * 128:(oc + 1) * 128, :], reads=xs_rows(oc, 0, S_), writes=[x_o])
                S.V(lambda e: e.tensor_tensor(x_o.ap.rearrange("p (b n) -> p b n", b=4), psg(g),
                                              x_o.ap.rearrange("p (b n) -> p b n", b=4), ALU.add),
                    reads=pbs(g) + [x_o], writes=[x_o])
                S.dma("sp", xs[oc * 128:(oc + 1) * 128, :], x_o.ap, reads=[x_o], writes=xs_rows(oc, 0, S_))
                return None

            gemm_fm(lambda oc, kq: w_out[l][:, oc * 128:(oc + 1) * 128], KC, KC,
                    lambda kc: [catq[kc // 4]], lambda kc, tb: catq[kc // 4].ap[:, kc % 4, tb * 512:(tb + 1) * 512], 4, evac)

        def ffn(l):
            HT = 1024
            for half in range(2):
                AR.reset()
                z = [AR.alloc([HT], BF16, f"z{i}") for i in range(64)]
                hh = AR.alloc([KC, HT], BF16, "ffn_h")
                rr = [AR.alloc([HT], F32, f"ffn_r{i}") for i in range(2)]
                xo = [AR.alloc([HT], F32, f"ffn_xo{i}") for i in range(2)]
                norm_fm(xs, P_FFN + 16 * l, hh, half * HT, HT, 512, 0)

                def evac1(fc, g):
                    r = rr[fc % 2]
                    S.A(lambda e: e.activation(r.ap.rearrange("p (b n) -> p b n", b=2), psg(g), AF.Relu),
                        reads=pbs(g), writes=[r])
                    S.V(lambda e: e.tensor_tensor(z[fc].ap, r.ap, r.ap, ALU.mult), reads=[r], writes=[z[fc]])
                    return None

                for i in range(64):
                    z[i] = AR.at(z[i].off, [HT], BF16, f"z{i}")
                gemm_fm(lambda fc, kq: w_ff1[l][:, fc * 128:(fc + 1) * 128], 64, KC,
                        lambda kc: [hh], lambda kc, tb: hh.ap[:, kc, tb * 512:(tb + 1) * 512], 2, evac1)

                def evac2(dc, g):
                    x_o = xo[dc % 2]
                    S.dma("sp", x_o.ap, xs[dc * 128:(dc + 1) * 128, half * HT:(half + 1) * HT], reads=xs_rows(dc, half * HT, HT), writes=[x_o])
                    S.V(lambda e: e.tensor_tensor(x_o.ap.rearrange("p (b n) -> p b n", b=2), psg(g),
                                                  x_o.ap.rearrange("p (b n) -> p b n", b=2), ALU.add),
                        reads=pbs(g) + [x_o], writes=[x_o])
                    S.dma("sp", xs[dc * 128:(dc + 1) * 128, half * HT:(half + 1) * HT], x_o.ap, reads=[x_o], writes=xs_rows(dc, half * HT, HT))
                    return None

                gemm_fm(lambda dc, kq: w_ff2[l][kq * 2048:(kq + 1) * 2048, dc * 128:(dc + 1) * 128], KC, 64,
                        lambda kc: [z[kc]], lambda kc, tb: z[kc].ap[:, tb * 512:(tb + 1) * 512], 2, evac2)

        def mixer0():
            AR.reset()
            h = AR.alloc([KC, S_], BF16, "h0")
            base = AR.top
            kT, vsb = mem_prep(0, base)
            if SUB <= 1:
                return
            base2 = vsb.off + vsb.size
            norm_fm(xs, P_MIX, h, 0, S_, 512, base2)
            if SUB <= 2:
                return
            AR.reset(base2)
            PADW = S_ + 16
            upad = [AR.alloc([PADW], F32, f"upad{i}") for i in range(2)]
            sb = [AR.alloc([PADW], F32, f"spp{i}") for i in range(2)]
            pooled = [AR.alloc([S_], BF16, f"pooled{i}") for i in range(6)]
            catst = [AR.alloc([S_], BF16, f"catst{i}") for i in range(2)]
            tmp16 = AR.alloc([16], F32, "tmp16")
            sq = [AR.alloc([S_], BF16, f"msq{i}") for i in range(2)]
            mb = mem_attn_bufs()
            pw_tiles = {}
            for t in upad + sb:
                S.V(lambda e, t=t: e.memset(t.ap[:, 0:16], 0.0), writes=[t])
            cst = {"i": 0}

            def token_out(gi):
                for oc2 in range(3):
                    wt = load_w(pool_w[gi, :, oc2 * 128:(oc2 + 1) * 128], 3)
                    g = next_group(4)
                    for ic in range(3):
                        pl = pooled[(gi * 3 + ic) % 6]
                        for tb in range(4):
                            S.T(lambda e, b=g[tb], ic=ic, tb=tb, wt=wt, pl=pl: e.matmul(
                                ps[:, b, :], wt.ap[:, ic, :], pl.ap[:, tb * 512:(tb + 1) * 512], start=(ic == 0), stop=(ic == 2)),
                                reads=[wt, pl], writes=[PB[g[tb]]])
                    cs_ = catst[cst["i"] % 2]
                    cst["i"] += 1
                    oc = gi * 3 + oc2
                    S.A(lambda e, cs_=cs_, g=g, oc=oc: e.activation(cs_.ap.rearrange("p (b n) -> p b n", b=4), psg(g),
                                                                    AF.Identity, scale=pcol(P_PSC + oc)),
                        reads=pbs(g) + [params], writes=[cs_])
                    S.dma("sp", cat_d[oc], cs_.ap, reads=[cs_], writes=[CAT[oc]])

            def evac(oc, g):
                up = upad[oc % 2]
                S.A(lambda e: e.copy(up.ap[:, 16:PADW].rearrange("p (b n) -> p b n", b=4), psg(g)), reads=pbs(g), writes=[up])
                if SUB <= 3:
                    return None
                if oc < 12:
                    gi = oc // 3
                    w = 2 << gi
                    src = up
                    for step in range(gi + 1):
                        sh = 1 << step
                        dst = sb[step % 2]
                        S.V(lambda e, src=src, dst=dst, sh=sh: e.tensor_tensor(
                            dst.ap[:, 16:PADW], src.ap[:, 16:PADW], src.ap[:, 16 - sh:PADW - sh], ALU.add),
                            reads=[src], writes=[dst])
                        src = dst
                    pl = pooled[oc % 6]
                    S.V(lambda e, src=src, pl=pl: e.scalar_tensor_tensor(
                        pl.ap, src.ap[:, 16:PADW], 1.0 / w, up.ap[:, 16:PADW], ALU.mult, ALU.subtract),
                        reads=[src, up], writes=[pl])
                    S.V(lambda e, src=src: e.tensor_tensor(tmp16.ap, src.ap[:, 16:32], pcol(P_INVC + 16 * gi, 16), ALU.mult),
                        reads=[src, params], writes=[tmp16])
                    S.V(lambda e, pl=pl: e.tensor_tensor(pl.ap[:, 0:16], tmp16.ap, up.ap[:, 16:32], ALU.subtract),
                        reads=[tmp16, up], writes=[pl])
                    if oc % 3 == 2 and SUB >= 5:
                        return lambda: token_out(gi)
                    return None
                else:
                    if SUB <= 5:
                        return None
                    hh = oc - 12
                    s_q = sq[oc % 2]
                    S.A(lambda e: e.activation(s_q.ap.rearrange("p (b n) -> p b n", b=4), psg(g), AF.Square),
                        reads=pbs(g), writes=[s_q])
                    cs_ = catst[cst["i"] % 2]
                    cst["i"] += 1

                    def post():
                        mb["qf_tile"] = up
                        mem_attn_head(0, hh, up.ap[:, 16:PADW], s_q, kT, vsb, mb, cs_)
                    return post

            gemm_fm(lambda oc, kq: w_in[0][:, oc * 128:(oc + 1) * 128], KC, KC,
                    lambda kc: [h], lambda kc, tb: h.ap[:, kc, tb * 512:(tb + 1) * 512], 4, evac)

        def hnr_bufs():
            d = {}
            d["cs"] = AR.alloc([2, S_], F32, "ropecs")
            S.dma("sp", d["cs"].ap, cs_d, writes=[d["cs"]])
            d["kf"] = [AR.alloc([S_], F32, f"kf{i}") for i in range(3)]
            d["sq"] = [AR.alloc([S_], BF16, f"ksq{i}") for i in range(2)]
            d["rstd"] = AR.alloc([S_], F32, "krstd")
            d["knb"] = [AR.alloc([S_], BF16, f"knb{i}") for i in range(2)]
            d["t1"] = AR.alloc([S_], F32, "kt1")
            d["t2"] = AR.alloc([S_], F32, "kt2")
            d["ob"] = [AR.alloc([S_], BF16, f"kob{i}") for i in range(2)]
            d["i"] = 0
            return d

        def hnr_evac(hb, g, gcol, dst, dtile):
            i = hb["i"]
            hb["i"] += 1
            kf = hb["kf"][i % 3]
            sq = hb["sq"][i % 2]
            ob = hb["ob"][i % 2]
            rstd, knb, t1, t2, cs = hb["rstd"], hb["knb"][i % 2], hb["t1"], hb["t2"], hb["cs"]
            v4 = lambda t: t.ap.rearrange("p (b n) -> p b n", b=4)
            S.A(lambda e: e.copy(v4(kf), psg(g)), reads=pbs(g), writes=[kf])
            S.A(lambda e: e.activation(v4(sq), psg(g), AF.Square), reads=pbs(g), writes=[sq])

            def postA():
                g2 = next_group(4)
                for tb in range(4):
                    S.T(lambda e, tb=tb: e.matmul(ps[:, g2[tb], :], cm(C_BLK), sq.ap[:, tb * 512:(tb + 1) * 512], start=True, stop=True),
                        reads=[sq, cbf], writes=[PB[g2[tb]]])
                rsqrt_act(v4(rstd), psg(g2), pbs(g2), rstd)
                S.V(lambda e: e.scalar_tensor_tensor(kf.ap, kf.ap, pcol(gcol), rstd.ap, ALU.mult, ALU.mult),
                    reads=[kf, rstd, params], writes=[kf])
                S.A(lambda e: e.copy(knb.ap, kf.ap), reads=[kf], writes=[knb])

            def postB():
                g3 = next_group(4)
                for tb in range(4):
                    S.T(lambda e, tb=tb: e.matmul(ps[:, g3[tb], :], cm(C_RT), knb.ap[:, tb * 512:(tb + 1) * 512], start=True, stop=True),
                        reads=[knb, cbf], writes=[PB[g3[tb]]])
                S.G(lambda e: e.tensor_tensor(t1.ap, kf.ap, cs.ap[:, 0, :], ALU.mult), reads=[kf, cs], writes=[t1])
                S.A(lambda e: e.copy(v4(t2), psg(g3)), reads=pbs(g3), writes=[t2])
                S.V(lambda e: e.tensor_tensor(t2.ap, t2.ap, cs.ap[:, 1, :], ALU.mult), reads=[t2, cs], writes=[t2])
                S.V(lambda e: e.tensor_tensor(ob.ap, t1.ap, t2.ap, ALU.add), reads=[t1, t2], writes=[ob])
                S.dma("sp", dst, ob.ap, reads=[ob], writes=[dtile])
            return [postA, postB]

        def kv_phase():
            AR.reset()
            h = AR.alloc([KC, S_], BF16, "hkv")
            base = AR.top
            norm_fm(xs, P_KVN, h, 0, S_, 512, base)
            AR.reset(base)
            hb = hnr_bufs()
            vst = [AR.alloc([S_], BF16, f"vst{i}") for i in range(2)]

            def evac(oc, g):
                if oc < 12:
                    return hnr_evac(hb, g, P_KN, kT_d[oc], KT[oc])
                v_s = vst[oc % 2]
                S.A(lambda e: e.copy(v_s.ap.rearrange("p (b n) -> p b n", b=4), psg(g)), reads=pbs(g), writes=[v_s])
                S.dma("sp", vT_d[oc - 12], v_s.ap, reads=[v_s], writes=[VT[oc - 12]])
                return None

            gemm_fm(lambda oc, kq: w_kv[:, oc * 128:(oc + 1) * 128], 24, KC,
                    lambda kc: [h], lambda kc, tb: h.ap[:, kc, tb * 512:(tb + 1) * 512], 4, evac, use_hooks=True)

        def q_phase():
            AR.reset()
            h = AR.alloc([KC, S_], BF16, "hq")
            base = AR.top
            kT, vsb = mem_prep(1, base)
            base2 = vsb.off + vsb.size
            norm_fm(xs, P_MIX + 16, h, 0, S_, 512, base2)
            AR.reset(base2)
            hb = hnr_bufs()
            qf = hb["kf"]
            sq = hb["sq"]
            catst = [AR.alloc([S_], BF16, f"qcatst{i}") for i in range(2)]
            mb = mem_attn_bufs(hb["rstd"])

            def evac(oc, g):
                if oc < 12:
                    return hnr_evac(hb, g, P_QN, qT_d[oc], QT[oc])
                hh = oc - 12
                q_f = qf[hb["i"] % 3]
                s_q = sq[hb["i"] % 2]
                hb["i"] += 1
                cs_ = catst[oc % 2]
                S.A(lambda e: e.copy(q_f.ap.rearrange("p (b n) -> p b n", b=4), psg(g)), reads=pbs(g), writes=[q_f])
                S.A(lambda e: e.activation(s_q.ap.rearrange("p (b n) -> p b n", b=4), psg(g), AF.Square), reads=pbs(g), writes=[s_q])

                def post():
                    mb["qf_tile"] = q_f
                    mem_attn_head(1, hh, q_f.ap, s_q, kT, vsb, mb, cs_)
                return post

            gemm_fm(lambda oc, kq: w_in[1][:, oc * 128:(oc + 1) * 128], KC, KC,
                    lambda kc: [h], lambda kc, tb: h.ap[:, kc, tb * 512:(tb + 1) * 512], 4, evac, use_hooks=True)

        def attn_phase():
            AR.reset()
            kk = [[AR.alloc([S_], BF16, f"kk{s}_{i}") for i in range(2)] for s in range(2)]
            zq = [[AR.alloc([S_], BF16, f"zq{m}_{hf}") for hf in range(2)] for m in range(2)]
            for m in range(2):
                for hf in range(2):
                    z0 = 64 * (1 - hf)
                    S.V(lambda e, m=m, hf=hf, z0=z0: e.memset(zq[m][hf].ap[z0:z0 + 64, :], 0.0), writes=[zq[m][hf]])
            vT = [AR.alloc([S_], BF16, f"vT{i}") for i in range(2)]
            vtok = [AR.alloc([16, 128], BF16, f"vtok{i}") for i in range(2)]
            E = [AR.alloc([512], BF16, f"E{i}") for i in range(8)]
            r12 = AR.alloc([1024], F32, "r12")
            aa = AR.alloc([512], F32, "aa")
            bb = AR.alloc([512], F32, "bb")
            oo = AR.alloc([512], F32, "oo")
            osq = AR.alloc([512], BF16, "osq")
            rs = AR.alloc([512], F32, "ars")
            catst = [AR.alloc([S_], BF16, f"acat{i}") for i in range(2)]
            sbank = {"i": 0}
            spair = {"i": 0}
            ei = {"i": 0}
            sc = 64.0 ** -0.5
            BO = (4, 5)
            BL = (6, 7)

            def sb_next():
                b = sbank["i"] % 4
                sbank["i"] += 1
                return b

            for hp in range(6):
                k2t = kk[hp % 2]
                S.dma("sp", k2t[0].ap, kT_d[hp], reads=[KT[hp]], writes=[k2t[0]])
                S.dma("sp", k2t[1].ap, kT_d[6 + hp], reads=[KT[6 + hp]], writes=[k2t[1]])
                for m in range(2):
                    for hf in range(2):
                        q0 = 64 * hf
                        S.dma("sp", zq[m][hf].ap[q0:q0 + 64, :], qT_d[6 * m + hp][q0:q0 + 64, :],
                              reads=[QT[6 * m + hp]], writes=[zq[m][hf]])
                for half in range(2):
                    hd = 2 * hp + half
                    p0 = 64 * half
                    v_T = vT[hd % 2]
                    v_k = vtok[hd % 2]
                    S.dma("sp", v_T.ap, vT_d[hd], reads=[VT[hd]], writes=[v_T])
                    for t4i in range(4):
                        b = sb_next()
                        pb16 = ps[:, b, :].bitcast(BF16)
                        for j in range(4):
                            tt = t4i * 4 + j
                            S.T(lambda e, pb16=pb16, j=j, tt=tt, v_T=v_T: e.transpose(
                                pb16[:, j * 128:(j + 1) * 128], v_T.ap[:, tt * 128:(tt + 1) * 128], cm(C_IDENT)),
                                reads=[v_T, cbf], writes=[PB[b]])
                        S.A(lambda e, pb16=pb16, t4i=t4i, v_k=v_k: e.copy(
                            v_k.ap[:, t4i * 4:(t4i + 1) * 4, :], pb16[:, 0:512].rearrange("p (j d) -> p j d", j=4)),
                            reads=[PB[b]], writes=[v_k])
                    cs_ = catst[hd % 2]
                    DEPTH = 2
                    queue = []
                    tails = []

                    def issue_S(qb, kb, m, half=half, k2t=k2t):
                        nkb = 4 * qb + 4
                        i_d = kb - 4 * qb
                        c0 = 128 * i_d if i_d > 0 else 0
                        qt = zq[m][half]
                        kt = k2t[m]
                        bS = sb_next()
                        diag = i_d >= 0
                        S.T(lambda e: e.matmul(ps[:, bS, c0:512], kt.ap[:, kb * 128:(kb + 1) * 128],
                                               qt.ap[:, qb * 512 + c0:(qb + 1) * 512], start=True, stop=(not diag)),
                            reads=[kt, qt], writes=[PB[bS]])
                        if diag:
                            S.T(lambda e: e.matmul(ps[:, bS, c0:c0 + 128], cm(C_IDENT), cm(C_TRIB), start=False, stop=True),
                                reads=[cbf], writes=[PB[bS]])
                        Et = E[ei["i"] % len(E)]
                        ei["i"] += 1
                        S.A(lambda e: e.activation(Et.ap[:, c0:512], ps[:, bS, c0:512], AF.Exp, scale=sc),
                            reads=[PB[bS]], writes=[Et])
                        return (qb, kb, m, c0, nkb, Et)

                    def post_a(qb):
                        S.V(lambda e: e.tensor_copy(aa.ap, ps[:, BO[0], :]), reads=[PB[BO[0]]], writes=[aa])
                        S.V(lambda e: e.tensor_copy(bb.ap, ps[:, BO[1], :]), reads=[PB[BO[1]]], writes=[bb])
                        recip_act(r12.ap.rearrange("p (b n) -> p b n", b=2), ps[:, BL[0]:BL[1] + 1, :], [PB[BL[0]], PB[BL[1]]], r12)
                        S.V(lambda e: e.tensor_tensor(aa.ap, aa.ap, r12.ap[:, 0:512], ALU.mult), reads=[aa, r12], writes=[aa])
                        S.V(lambda e: e.tensor_tensor(bb.ap, bb.ap, r12.ap[:, 512:1024], ALU.mult), reads=[bb, r12], writes=[bb])
                        S.V(lambda e: e.scalar_tensor_tensor(oo.ap, bb.ap, NEGLAM, aa.ap, ALU.mult, ALU.add),
                            reads=[bb, aa, small], writes=[oo])
                        S.A(lambda e: e.activation(osq.ap, oo.ap, AF.Square), reads=[oo], writes=[osq])

                    def post_b(qb, cs_=cs_):
                        bs2 = sb_next()
                        S.T(lambda e: e.matmul(ps[:, bs2, :], cm(C_ONESHD), osq.ap, start=True, stop=True),
                            reads=[osq, cbf], writes=[PB[bs2]])
                        rsqrt_act(rs.ap, ps[:, bs2, :], [PB[bs2]], rs)
                        S.V(lambda e: e.scalar_tensor_tensor(cs_.ap[:, qb * 512:(qb + 1) * 512], oo.ap, SUBG, rs.ap, ALU.mult, ALU.mult),
                            reads=[oo, rs, small], writes=[cs_])

                    def issue_PV(info, v_k=v_k):
                        qb, kb, m, c0, nkb, Et = info
                        S.T(lambda e: e.matmul(ps[:, BO[m], c0:512], v_k.ap[:, kb, :], Et.ap[:, c0:512],
                                               start=(kb == 0), stop=(kb == nkb - 1)),
                            reads=[v_k, Et], writes=[PB[BO[m]]])
                        S.T(lambda e: e.matmul(ps[:, BL[m], c0:512], cm(C_ONES1), Et.ap[:, c0:512],
                                               start=(kb == 0), stop=(kb == nkb - 1)),
                            reads=[cbf, Et], writes=[PB[BL[m]]])
                        for t in tails:
                            t[0] -= 1
                        while tails and tails[0][0] <= 0:
                            tails.pop(0)[1]()
                        if kb == nkb - 1 and m == 1:
                            post_a(qb)
                            tails.append([5, lambda qb=qb: post_b(qb)])

                    steps = [(qb, kb, m) for qb in range(4) for kb in range(4 * qb + 4) for m in range(2)]
                    for st_ in steps:
                        queue.append(issue_S(*st_))
                        if len(queue) > DEPTH:
                            issue_PV(queue.pop(0))
                    while queue:
                        issue_PV(queue.pop(0))
                    while tails:
                        tails.pop(0)[1]()
                    S.dma("sp", cat_d[hd], cs_.ap, reads=[cs_], writes=[CAT[hd]])

        S.dma("sp", params.ap, params_d, writes=[params])
        S.dma("sp", cbf.ap, cbf_d.rearrange("p (a b) -> p a b", a=NCB), writes=[cbf])
        S.V(lambda e: e.memset(EPSC, EPS), writes=[small])
        AR.reset()
        lt = AR.alloc([64], F32, "lam_tmp")
        l4 = AR.alloc([4], F32, "lam4")
        dl = lambda i: params.ap[:, P_DL + 64 * i:P_DL + 64 * (i + 1)]
        S.V(lambda e: e.tensor_tensor(lt.ap, dl(0), dl(1), ALU.mult), reads=[params], writes=[lt])
        S.V(lambda e: e.reduce_sum(l4.ap[:, 0:1], lt.ap, axis=AX.X), reads=[lt], writes=[l4])
        S.V(lambda e: e.tensor_tensor(lt.ap, dl(2), dl(3), ALU.mult), reads=[params, l4], writes=[lt])
        S.V(lambda e: e.reduce_sum(l4.ap[:, 1:2], lt.ap, axis=AX.X), reads=[lt], writes=[l4])
        S.A(lambda e: e.activation(l4.ap[:, 2:4], l4.ap[:, 0:2], AF.Exp), reads=[l4], writes=[l4])
        S.V(lambda e: e.tensor_tensor(NEGLAM, l4.ap[:, 3:4], l4.ap[:, 2:3], ALU.subtract), reads=[l4], writes=[small])
        S.V(lambda e: e.tensor_scalar_add(NEGLAM, NEGLAM, -LAM_INIT), reads=[small], writes=[small])
        S.V(lambda e: e.tensor_scalar_mul(SUBG, pcol(P_SUB), 1.0 - LAM_INIT), reads=[params], writes=[small])

        phases = [
            lambda: transpose_in(x_in, xs, S_, lambda dc, tb: [XS[dc][tb]]),
            lambda: transpose_in(mem_in, memT, MEM, lambda dc, tb: [MEMT]),
            mixer0,
            lambda: out_proj(0),
            lambda: ffn(0),
            kv_phase,
            q_phase,
            attn_phase,
            lambda: out_proj(1),
            lambda: ffn(1),
        ]
        for i, ph in enumerate(phases):
            if i > stop_after:
                break
            ph()
        transpose_out()

        with nc.Block() as block:
            S.emit(block, engsem, rings)
    return nc


_CACHE = {}


def kernel(**inputs):
    stop_after = int(os.environ.get("MK_STOP", "99"))
    dbg = os.environ.get("MK_DBG", "0") == "1"
    inp = {k: np.asarray(v) for k, v in inputs.items()}
    key = (stop_after, dbg)
    if key not in _CACHE:
        _CACHE[key] = build_nc(stop_after, dbg)
    nc = _CACHE[key]
    params = pack_params(inp)
    cbf = const_bf16()
    cs = rope_cs()
    shared = {
        "pool_w": np.ascontiguousarray(inp["pool_w"][0], np.float32),
        "w_kv": np.ascontiguousarray(inp["w_kv"], np.float32),
        "params": params, "cbf": cbf, "ropecs": cs,
    }
    for l in range(2):
        shared[f"w_in{l}"] = np.ascontiguousarray(inp["w_in"][l], np.float32)
        shared[f"w_out{l}"] = np.ascontiguousarray(inp["w_out"][l], np.float32)
        shared[f"w_mkv{l}"] = np.ascontiguousarray(inp["w_mem_kv"][l], np.float32)
        shared[f"w_ff1{l}"] = np.ascontiguousarray(inp["w_ff1"][l], np.float32)
        shared[f"w_ff2{l}"] = np.ascontiguousarray(inp["w_ff2"][l], np.float32)
    in_maps = []
    for b in range(8):
        m = dict(shared)
        m["x"] = np.ascontiguousarray(inp["x"][b], np.float32)
        m["mem"] = np.ascontiguousarray(inp["mem"][b], np.float32)
        in_maps.append(m)
    ncores = int(os.environ.get("MK_CORES", "8"))
    res = run_bass_kernel_spmd(nc, in_maps[:ncores], core_ids=list(range(ncores)))
    if dbg:
        kernel.last_results = res.results
    out = np.stack([np.asarray(r["out"], np.float32) for r in res.results], axis=0)
    return out
```

```python
import math
import os
from contextlib import ExitStack

import ml_dtypes
import numpy as np

import concourse.bass as bass
import concourse.mybir as mybir
from concourse.bass_utils import run_bass_kernel_spmd

F32 = mybir.dt.float32
BF16 = mybir.dt.bfloat16
U8 = mybir.dt.uint8
AF = mybir.ActivationFunctionType
ALU = mybir.AluOpType
AX = mybir.AxisListType

S_ = 2048
D = 2048
KC = 16
MEM = 256
EPS = 1e-6
LAM_INIT = 0.8 - 0.6 * math.exp(-0.3 * 1)
ROPE_THETA = 500000.0


class Tile:
    __slots__ = ("ap", "w", "r", "rd", "name", "off", "size", "excl")

    def __init__(self, ap, name="", off=-1, size=0):
        self.excl = False
        self.ap = ap
        self.w = None
        self.r = {}
        self.rd = []
        self.name = name
        self.off = off
        self.size = size


class Op:
    __slots__ = ("eng", "fn", "deps", "marked", "semval", "is_dma", "slot", "dval", "idx")

    def __init__(self, eng, fn, is_dma=False):
        self.eng = eng
        self.fn = fn
        self.deps = set()
        self.marked = False
        self.semval = 0
        self.is_dma = is_dma
        self.slot = -1
        self.dval = 0
        self.idx = 0


ENGS = ("pe", "act", "dve", "pool", "sp")
RING = {"sp": 16, "pool": 8}


class Sched:
    def __init__(self):
        self.ops = {e: [] for e in ENGS}
        self.dma_count = {q: 0 for q in RING}
        self.dma_last = {q: [None] * RING[q] for q in RING}

    def add(self, eng, fn, reads=(), writes=(), dma=False):
        o = Op(eng, fn, dma)
        deps = o.deps
        for t in reads:
            if t.w is not None:
                deps.add(t.w)
            if t.excl:
                for en, ro in t.r.items():
                    if en != eng:
                        deps.add(ro)
        for t in writes:
            if t.w is not None:
                deps.add(t.w)
            deps.update(t.r.values())
            deps.update(t.rd)
        for t in reads:
            if dma:
                t.rd.append(o)
            else:
                t.r[eng] = o
        for t in writes:
            t.w = o
            t.r = {}
            t.rd = []
        if dma:
            k = self.dma_count[eng]
            R = RING[eng]
            o.slot = k % R
            o.dval = 16 * (k // R + 1)
            prev = self.dma_last[eng][o.slot]
            if prev is not None:
                deps.add(prev)
            self.dma_last[eng][o.slot] = o
            self.dma_count[eng] = k + 1
        if eng == "pe" and not dma:
            o.deps = {d for d in deps if d.is_dma or d.eng != "pe"}
        o.deps.discard(o)
        o.idx = len(self.ops[eng])
        self.ops[eng].append(o)
        return o

    def T(self, fn, reads=(), writes=()):
        return self.add("pe", fn, reads, writes)

    def A(self, fn, reads=(), writes=()):
        return self.add("act", fn, reads, writes)

    def V(self, fn, reads=(), writes=()):
        return self.add("dve", fn, reads, writes)

    def G(self, fn, reads=(), writes=()):
        return self.add("pool", fn, reads, writes)

    def dma(self, q, out_ap, in_ap, reads=(), writes=()):
        return self.add(q, lambda e: e.dma_start(out=out_ap, in_=in_ap), reads, writes, dma=True)

    def finalize(self):
        for e in ENGS:
            for o in self.ops[e]:
                for d in o.deps:
                    if not d.is_dma:
                        d.marked = True
        for e in ENGS:
            c = 0
            for o in self.ops[e]:
                if (not o.is_dma) and o.marked:
                    c += 1
                    o.semval = c

    def emit(self, block, engsem, rings):
        self.finalize()
        sched = self

        def run(ename, e):
            seen = {}
            for o in sched.ops[ename]:
                for d in sorted(o.deps, key=lambda d: (d.eng, d.idx)):
                    if d.is_dma:
                        key = (d.eng, d.slot)
                        sem = rings[d.eng][d.slot]
                        val = d.dval
                    else:
                        key = d.eng
                        sem = engsem[d.eng]
                        val = d.semval
                    if seen.get(key, 0) >= val:
                        continue
                    seen[key] = val
                    e.wait_ge(sem, val)
                ins = o.fn(e)
                if o.is_dma:
                    ins.then_inc(rings[ename][o.slot], 16)
                elif o.marked:
                    ins.then_inc(engsem[ename], 1)
            if ename in RING:
                for s, last in enumerate(sched.dma_last[ename]):
                    if last is not None and seen.get((ename, s), 0) < last.dval:
                        e.wait_ge(rings[ename][s], last.dval)

        @block.tensor
        def _(e):
            run("pe", e)

        @block.scalar
        def _(e):
            run("act", e)

        @block.vector
        def _(e):
            run("dve", e)

        @block.gpsimd
        def _(e):
            run("pool", e)

        @block.sync
        def _(e):
            run("sp", e)


class Arena:
    def __init__(self, ap, nbytes, sched):
        self.ap = ap
        self.nbytes = nbytes
        self.live = []
        self.S = sched
        self.top = 0

    def reset(self, top=0):
        self.top = top

    def alloc(self, shape, dt, name=""):
        isz = 4 if dt == F32 else 2
        n = int(np.prod(shape)) * isz
        off = (self.top + 31) // 32 * 32
        assert off + n <= self.nbytes, (name, off, n, self.nbytes)
        self.top = off + n
        return self.at(off, shape, dt, name)

    def at(self, off, shape, dt, name=""):
        isz = 4 if dt == F32 else 2
        n = int(np.prod(shape)) * isz
        assert off + n <= self.nbytes, (name, off, n, self.nbytes)
        ap = self.ap[:, off:off + n].bitcast(dt)
        if len(shape) == 2:
            ap = ap.rearrange("p (a b) -> p a b", a=shape[0])
        elif len(shape) == 3:
            ap = ap.rearrange("p (a b c) -> p a b c", a=shape[0], b=shape[1])
        t = Tile(ap, name, off, n)
        keep = []
        ops = self.S.ops
        for o in self.live:
            if o.off < off + n and off < o.off + o.size:
                cands = list(o.r.values())
                if o.w is not None:
                    if o.w.is_dma:
                        t.rd.append(o.w)
                    else:
                        cands.append(o.w)
                for c in cands:
                    cur = t.r.get(c.eng)
                    if cur is None or cur.idx < c.idx:
                        t.r[c.eng] = c
                t.rd.extend(o.rd)
            else:
                keep.append(o)
        keep.append(t)
        self.live = keep
        return t


P_MIX = 0
P_FFN = 32
P_MEMN = 64
P_KVN = 96
P_PSC = 112
P_MQ = 124
P_MK = 126
P_KN = 128
P_QN = 129
P_SUB = 130
P_DL = 131
P_INVC = 387
P_IDENT = 451
PC = 579

C_IDENT, C_ONESD, C_ONESHD, C_ONES1, C_BLK, C_RT, C_TRI, C_TRIB = range(8)
NCB = 8


def _vec16(v):
    return np.asarray(v, np.float32).reshape(16, 128).T


def pack_params(inp):
    p = np.zeros((128, PC), np.float32)
    for l in range(2):
        p[:, P_MIX + 16 * l:P_MIX + 16 * l + 16] = _vec16(inp["mix_norm"][l])
        p[:, P_FFN + 16 * l:P_FFN + 16 * l + 16] = _vec16(inp["ffn_norm"][l])
        p[:, P_MEMN + 16 * l:P_MEMN + 16 * l + 16] = _vec16(inp["mem_norm"][l])
        p[:, P_MQ + l] = inp["mem_q_norm"][l]
        p[:, P_MK + l] = inp["mem_k_norm"][l]
    p[:, P_KVN:P_KVN + 16] = _vec16(inp["kv_norm"])
    p[:, P_PSC:P_PSC + 12] = np.asarray(inp["pool_scale"][0], np.float32).reshape(12, 128).T
    p[:, P_KN] = np.tile(np.asarray(inp["k_norm"], np.float32), 2)
    p[:, P_QN] = np.tile(np.asarray(inp["q_norm"][0], np.float32), 2)
    p[:, P_SUB] = inp["subln_norm"][0]
    p[:, P_DL:P_DL + 256] = np.asarray(inp["diff_lambda"][0], np.float32).reshape(1, 256)
    for g, w in enumerate((2, 4, 8, 16)):
        t = np.arange(16)
        p[:, P_INVC + 16 * g:P_INVC + 16 * g + 16] = (1.0 / np.minimum(t + 1, w)).astype(np.float32)[None, :]
    p[:, P_IDENT:P_IDENT + 128] = np.eye(128, dtype=np.float32)
    return p


def const_bf16():
    c = np.zeros((128, NCB, 128), np.float32)
    c[:, C_IDENT] = np.eye(128)
    c[:, C_ONESD] = 1.0 / 2048.0
    c[:, C_ONESHD] = 1.0 / 128.0
    c[:, C_ONES1] = 1.0
    blk = np.zeros((128, 128))
    blk[:64, :64] = 1.0 / 64.0
    blk[64:, 64:] = 1.0 / 64.0
    c[:, C_BLK] = blk
    Rm = np.zeros((128, 128))
    for hb in (0, 64):
        for j in range(8):
            Rm[hb + j, hb + j + 8] = -1.0
            Rm[hb + j + 8, hb + j] = 1.0
    c[:, C_RT] = Rm.T
    k = np.arange(128)[:, None]
    q = np.arange(128)[None, :]
    c[:, C_TRI] = (k <= q).astype(np.float32)
    c[:, C_TRIB] = np.where(k <= q, 0.0, -30000.0)
    return c.reshape(128, NCB * 128).astype(ml_dtypes.bfloat16)


def rope_cs():
    pos = np.arange(S_, dtype=np.float32)
    inv = (np.float32(ROPE_THETA) ** (-(np.arange(8, dtype=np.float32) * np.float32(2.0)) / np.float32(16))).astype(np.float32)
    ang = (pos[:, None] * inv[None, :]).astype(np.float32)
    cs = np.zeros((128, 2, S_), np.float32)
    cs[:, 0, :] = 1.0
    for hb in (0, 64):
        for j in range(16):
            cs[hb + j, 0, :] = np.cos(ang[:, j % 8])
            cs[hb + j, 1, :] = np.sin(ang[:, j % 8])
    return cs


ARENA_BYTES = 187 * 1024
SUB = int(os.environ.get('MK_SUB', '99'))
SUB2 = int(os.environ.get('MK_SUB2', '99'))
NBW = 4


def build_nc(stop_after=99, dbg=False):
    nc = bass.Bass("TRN2", target_bir_lowering=False)
    okind = "ExternalOutput" if dbg else "Internal"

    def din(name, shape, dt=F32):
        return nc.dram_tensor(name, list(shape), dt, kind="ExternalInput").ap()

    x_in = din("x", [S_, D])
    mem_in = din("mem", [MEM, D])
    w_in = [din(f"w_in{l}", [D, D]) for l in range(2)]
    w_out = [din(f"w_out{l}", [D, D]) for l in range(2)]
    w_mkv = [din(f"w_mkv{l}", [D, 1024]) for l in range(2)]
    w_ff1 = [din(f"w_ff1{l}", [D, 4 * D]) for l in range(2)]
    w_ff2 = [din(f"w_ff2{l}", [4 * D, D]) for l in range(2)]
    pool_w = din("pool_w", [4, 384, 384])
    w_kv = din("w_kv", [D, 3072])
    params_d = din("params", [128, PC])
    cbf_d = din("cbf", [128, NCB * 128], BF16)
    cs_d = din("ropecs", [128, 2, S_])
    out_d = nc.dram_tensor("out", [S_, D], F32, kind="ExternalOutput").ap()

    xs = nc.dram_tensor("xs", [D, S_], F32, kind=okind).ap()
    memT = nc.dram_tensor("memT", [D, MEM], F32, kind=okind).ap()
    dbg_b = nc.dram_tensor("dbg_b", [128, 2048], BF16, kind=okind).ap()
    dbg_h = nc.dram_tensor("dbg_h", [128, 4096], BF16, kind=okind).ap()
    cat_d = nc.dram_tensor("cat_d", [16, 128, S_], BF16, kind=okind).ap()
    kT_d = nc.dram_tensor("kT_d", [12, 128, S_], BF16, kind=okind).ap()
    qT_d = nc.dram_tensor("qT_d", [12, 128, S_], BF16, kind=okind).ap()
    vT_d = nc.dram_tensor("vT_d", [12, 128, S_], BF16, kind=okind).ap()

    S = Sched()
    with ExitStack() as ctx:
        engsem = {e: ctx.enter_context(nc.semaphore("sem_" + e)) for e in ENGS}
        rings = {q: [ctx.enter_context(nc.semaphore(f"r_{q}_{i}")) for i in range(RING[q])] for q in RING}
        params_t = ctx.enter_context(nc.sbuf_tensor("sb_params", [128, PC], F32))
        cbf_t = ctx.enter_context(nc.sbuf_tensor("sb_cbf", [128, NCB, 128], BF16))
        small_t = ctx.enter_context(nc.sbuf_tensor("sb_small", [128, 16], F32))
        wring_t = ctx.enter_context(nc.sbuf_tensor("sb_wring", [128, NBW, 16, 128], BF16))
        arena_t = ctx.enter_context(nc.sbuf_tensor("sb_arena", [128, ARENA_BYTES], U8))
        ps_t = ctx.enter_context(nc.psum_tensor("ps_all", [128, 8, 512], F32))
        ps = ps_t[:, :, :]
        PB = [Tile(ps[:, b, :], f"ps{b}") for b in range(8)]
        for t_ in PB:
            t_.excl = True
        params = Tile(params_t[:, :], "params")
        cbf = Tile(cbf_t[:, :, :], "cbf")
        small = Tile(small_t[:, :], "small")
        wring = [Tile(wring_t[:, i, :, :], f"w{i}") for i in range(NBW)]
        AR = Arena(arena_t[:, :], ARENA_BYTES, S)
        XS = [[Tile(None, f"xs{dc}_{tb}") for tb in range(4)] for dc in range(KC)]
        MEMT = Tile(None, "memT")
        CAT = [Tile(None, f"cat{i}") for i in range(16)]
        KT = [Tile(None, f"kT{i}") for i in range(12)]
        QT = [Tile(None, f"qT{i}") for i in range(12)]
        VT = [Tile(None, f"vT{i}") for i in range(12)]

        def xs_cols(c0, n):
            return [XS[dc][tb] for dc in range(KC) for tb in range(c0 // 512, (c0 + n + 511) // 512)]

        def xs_rows(dc, c0, n):
            return [XS[dc][tb] for tb in range(c0 // 512, (c0 + n + 511) // 512)]
        st = {"w": 0, "bank": 0}

        def pcol(c, n=1):
            return params.ap[:, c:c + n]

        def cm(i):
            return cbf.ap[:, i, :]

        ident_f = params.ap[:, P_IDENT:P_IDENT + 128]
        EPSC = small.ap[:, 0:1]
        NEGLAM = small.ap[:, 1:2]
        SUBG = small.ap[:, 2:3]

        def next_group(n):
            res = st.get("reserved", ())
            for _ in range(9):
                b = (st["bank"] + n - 1) // n * n
                if b + n > 8:
                    b = 0
                st["bank"] = b + n
                if not any((x in res) for x in range(b, b + n)):
                    return list(range(b, b + n))
            raise AssertionError("no free PSUM group")

        def next_bank():
            return next_group(1)[0]

        def psg(g):
            return ps[:, g[0]:g[0] + len(g), :]

        def pbs(g):
            return [PB[b] for b in g]

        def load_w(src, nk):
            wt = wring[st["w"] % NBW]
            st["w"] += 1
            S.dma("pool", wt.ap[:, 0:nk, :], src.rearrange("(kc p) c -> p kc c", p=128), writes=[wt])
            return wt

        def rsqrt_act(out_ap, in_ap, rd, wr):
            S.A(lambda e: e.activation(out_ap, in_ap, AF.Ln, bias=EPSC, scale=1.0), reads=rd + [small], writes=[wr])
            S.A(lambda e: e.activation(out_ap, out_ap, AF.Exp, scale=-0.5), reads=[wr], writes=[wr])

        def recip_act(out_ap, in_ap, rd, wr):
            S.A(lambda e: e.activation(out_ap, in_ap, AF.Ln), reads=rd, writes=[wr])
            S.A(lambda e: e.activation(out_ap, out_ap, AF.Exp, scale=-1.0), reads=[wr], writes=[wr])

        def gemm_fm(wsrc, n_oc, nk, rhs_tiles_fn, rhs_fn, ntb, evac, ncols=512, use_hooks=False):
            pending = []
            hooks = {max(0, (3 * nk) // 8 - 1): 0, max(0, (3 * nk) // 4 - 1): 1} if (nk >= 8 and use_hooks) else {}

            def run_stage(h, oc):
                for p in list(pending):
                    if p[1] == h and p[2] < oc:
                        p[0][p[1]]()
                        p[1] += 1
                        p[2] = oc
                        if p[1] >= len(p[0]):
                            pending.remove(p)

            for oc in range(n_oc):
                gk = "g%d" % ntb
                gi = st.get(gk, 0)
                st[gk] = gi + 1
                ng = 8 // ntb
                g = list(range((gi % ng) * ntb, (gi % ng) * ntb + ntb))
                st["bank"] = g[-1] + 1
                st["reserved"] = set(g)
                nq = (nk + 15) // 16
                for kq in range(nq):
                    nkk = min(16, nk - kq * 16)
                    wt = load_w(wsrc(oc, kq), nkk)
                    for k16 in range(nkk):
                        kc = kq * 16 + k16
                        for tb in range(ntb):
                            S.T(lambda e, b=g[tb], wt=wt, k16=k16, kc=kc, tb=tb: e.matmul(
                                ps[:, b, 0:ncols], wt.ap[:, k16, :], rhs_fn(kc, tb),
                                start=(kc == 0), stop=(kc == nk - 1)),
                                reads=[wt] + rhs_tiles_fn(kc), writes=[PB[g[tb]]])
                        if kc in hooks:
                            run_stage(hooks[kc], oc)
                st["reserved"] = ()
                cur = evac(oc, g)
                if not hooks:
                    run_stage(0, oc + 1)
                    run_stage(1, oc + 1)
                if cur is not None:
                    pending.append([list(cur) if isinstance(cur, (list, tuple)) else [cur], 0, oc])
            k = n_oc
            while pending:
                k += 1
                run_stage(0, k)
                run_stage(1, k)

        def transpose_in(src, dst, ntok, dtiles):
            TW = min(512, ntok)
            ntt = TW // 128
            AR.reset()
            xin = [AR.alloc([ntt, D], F32, f"xin{i}") for i in range(2)]
            stg = [AR.alloc([TW], F32, f"stg{i}") for i in range(4)]
            k = 0
            for tb in range(ntok // TW):
                xt = xin[tb % 2]
                S.dma("sp", xt.ap, src[tb * TW:(tb + 1) * TW, :].rearrange("(tt p) d -> p tt d", p=128), writes=[xt])
                for dc in range(KC):
                    b = next_bank()
                    for tt in range(ntt):
                        S.T(lambda e, b=b, tt=tt, dc=dc, xt=xt: e.transpose(
                            ps[:, b, tt * 128:(tt + 1) * 128], xt.ap[:, tt, dc * 128:(dc + 1) * 128], ident_f),
                            reads=[xt, params], writes=[PB[b]])
                    sg = stg[k % 4]
                    eng = S.A if k % 2 == 0 else S.V
                    if k % 2 == 0:
                        S.A(lambda e, sg=sg, b=b: e.copy(sg.ap, ps[:, b, 0:TW]), reads=[PB[b]], writes=[sg])
                    else:
                        S.V(lambda e, sg=sg, b=b: e.tensor_copy(sg.ap, ps[:, b, 0:TW]), reads=[PB[b]], writes=[sg])
                    S.dma("sp", dst[dc * 128:(dc + 1) * 128, tb * TW:(tb + 1) * TW], sg.ap, reads=[sg], writes=dtiles(dc, tb))
                    k += 1

        def transpose_out():
            AR.reset()
            xin = [AR.alloc([KC, 512], F32, f"xo_in{i}") for i in range(2)]
            ost = [AR.alloc([D], F32, f"ost{i}") for i in range(2)]
            k = 0
            j = 0
            xsv = xs.rearrange("(dc p) t -> p dc t", p=128)
            for tb in range(4):
                xt = xin[tb % 2]
                S.dma("sp", xt.ap, xsv[:, :, tb * 512:(tb + 1) * 512], reads=xs_cols(tb * 512, 512), writes=[xt])
                for tt in range(4):
                    o = ost[j % 2]
                    j += 1
                    for dc4 in range(4):
                        b = next_bank()
                        for i in range(4):
                            S.T(lambda e, b=b, i=i, dc4=dc4, tt=tt, xt=xt: e.transpose(
                                ps[:, b, i * 128:(i + 1) * 128], xt.ap[:, dc4 * 4 + i, tt * 128:(tt + 1) * 128], ident_f),
                                reads=[xt, params], writes=[PB[b]])
                        if k % 2 == 0:
                            S.A(lambda e, o=o, b=b, dc4=dc4: e.copy(o.ap[:, dc4 * 512:(dc4 + 1) * 512], ps[:, b, :]),
                                reads=[PB[b]], writes=[o])
                        else:
                            S.V(lambda e, o=o, b=b, dc4=dc4: e.tensor_copy(o.ap[:, dc4 * 512:(dc4 + 1) * 512], ps[:, b, :]),
                                reads=[PB[b]], writes=[o])
                        k += 1
                    r0 = tb * 512 + tt * 128
                    S.dma("sp", out_d[r0:r0 + 128, :], o.ap, reads=[o])

        def norm_fm(src, gcol, h, tok0, T, TW, tmp_base, stiles=None):
            AR.reset(tmp_base)
            xts = [AR.alloc([KC, TW], F32, f"nx{i}") for i in range(2)]
            sq = AR.alloc([KC, TW], BF16, "nsq")
            rstd = AR.alloc([TW], F32, "nrstd")
            srcv = src.rearrange("(dc p) t -> p dc t", p=128)
            for tb in range(T // TW):
                xt = xts[tb % 2]
                c0 = tok0 + tb * TW
                S.dma("sp", xt.ap, srcv[:, :, c0:c0 + TW], reads=(stiles if stiles is not None else xs_cols(c0, TW)), writes=[xt])
                hk = KC // 2
                S.A(lambda e, xt=xt: e.activation(sq.ap[:, 0:hk, :], xt.ap[:, 0:hk, :], AF.Square), reads=[xt], writes=[sq])
                S.A(lambda e, xt=xt: e.activation(sq.ap[:, hk:KC, :], xt.ap[:, hk:KC, :], AF.Square), reads=[xt], writes=[sq])
                b = next_bank()
                for dc in range(KC):
                    S.T(lambda e, b=b, dc=dc: e.matmul(ps[:, b, 0:TW], cm(C_ONESD), sq.ap[:, dc, :],
                                                      start=(dc == 0), stop=(dc == KC - 1)),
                        reads=[sq, cbf], writes=[PB[b]])
                rsqrt_act(rstd.ap, ps[:, b, 0:TW], [PB[b]], rstd)
                for dc in range(KC):
                    S.V(lambda e, dc=dc, xt=xt, tb=tb: e.scalar_tensor_tensor(
                        h.ap[:, dc, tb * TW:(tb + 1) * TW], xt.ap[:, dc, :], pcol(gcol + dc), rstd.ap, ALU.mult, ALU.mult),
                        reads=[xt, rstd, params], writes=[h])

        def mem_prep(l, base):
            AR.reset(base)
            kT = AR.alloc([4, MEM], BF16, "mkT")
            vsb = AR.alloc([4, 2, 128], BF16, "mvsb")
            vsb.ap = AR.ap[:, vsb.off:vsb.off + vsb.size].bitcast(BF16).rearrange("p (h m d) -> p h m d", h=4, m=2)
            hm = AR.alloc([KC, MEM], BF16, "hm")
            kf = [AR.alloc([MEM], F32, f"mkf{i}") for i in range(2)]
            sqm = [AR.alloc([MEM], BF16, f"msq{i}") for i in range(2)]
            vf = [AR.alloc([MEM], BF16, f"mvf{i}") for i in range(2)]
            rs = AR.alloc([MEM], F32, "mrs")
            tmp_base = AR.top
            norm_fm(memT, P_MEMN + 16 * l, hm, 0, MEM, MEM, tmp_base, stiles=[MEMT])

            def evac(oc, g):
                b = g[0]
                if oc < 4:
                    k_f = kf[oc % 2]
                    s_q = sqm[oc % 2]
                    S.A(lambda e: e.copy(k_f.ap, ps[:, b, 0:MEM]), reads=[PB[b]], writes=[k_f])
                    S.A(lambda e: e.activation(s_q.ap, ps[:, b, 0:MEM], AF.Square), reads=[PB[b]], writes=[s_q])

                    def post():
                        b2 = next_bank()
                        S.T(lambda e: e.matmul(ps[:, b2, 0:MEM], cm(C_ONESHD), s_q.ap, start=True, stop=True),
                            reads=[s_q, cbf], writes=[PB[b2]])
                        rsqrt_act(rs.ap, ps[:, b2, 0:MEM], [PB[b2]], rs)
                        S.V(lambda e: e.scalar_tensor_tensor(kT.ap[:, oc, :], k_f.ap, pcol(P_MK + l), rs.ap, ALU.mult, ALU.mult),
                            reads=[k_f, rs, params], writes=[kT])
                    return post
                else:
                    hh = oc - 4
                    v_f = vf[oc % 2]
                    S.A(lambda e: e.copy(v_f.ap, ps[:, b, 0:MEM]), reads=[PB[b]], writes=[v_f])

                    def post():
                        b2 = next_bank()
                        pb16 = ps[:, b2, :].bitcast(BF16)
                        for mt in range(2):
                            S.T(lambda e, mt=mt: e.transpose(pb16[:, mt * 128:(mt + 1) * 128], v_f.ap[:, mt * 128:(mt + 1) * 128], cm(C_IDENT)),
                                reads=[v_f, cbf], writes=[PB[b2]])
                        S.V(lambda e: e.tensor_copy(vsb.ap[:, hh, :, :], pb16[:, 0:256].rearrange("p (m d) -> p m d", m=2)),
                            reads=[PB[b2]], writes=[vsb])
                    return post

            gemm_fm(lambda oc, kq: w_mkv[l][:, oc * 128:(oc + 1) * 128], 8, KC,
                    lambda kc: [hm], lambda kc, tb: hm.ap[:, kc, :], 1, evac, ncols=MEM)
            if dbg and l == 0:
                S.dma("sp", dbg_b[:, 0:1024], kT.ap.rearrange("p a b -> p (a b)"), reads=[kT])
                S.dma("sp", dbg_b[:, 1024:2048], vsb.ap.rearrange("p h m d -> p (h m d)"), reads=[vsb])
                S.dma("sp", dbg_h, hm.ap.rearrange("p a b -> p (a b)"), reads=[hm])
            return kT, vsb

        def mem_attn_bufs(rstd=None):
            d = {}
            d["rstd"] = rstd if rstd is not None else AR.alloc([S_], F32, "ma_rstd")
            d["qn"] = AR.alloc([S_], BF16, "ma_qn")
            d["E"] = [AR.alloc([512], BF16, f"ma_E{i}") for i in range(4)]
            d["rl"] = AR.alloc([512], F32, "ma_rl")
            d["osb"] = AR.alloc([512], F32, "ma_osb")
            d["ei"] = 0
            return d

        def mem_attn_head(l, hh, qf, sq, kT, vsb, mb, catst):
            g = next_group(4)
            for tb in range(4):
                S.T(lambda e, tb=tb: e.matmul(ps[:, g[tb], :], cm(C_ONESHD), sq.ap[:, tb * 512:(tb + 1) * 512], start=True, stop=True),
                    reads=[sq, cbf], writes=[PB[g[tb]]])
            rstd = mb["rstd"]
            qn = mb["qn"]
            rsqrt_act(rstd.ap.rearrange("p (b n) -> p b n", b=4), psg(g), pbs(g), rstd)
            S.V(lambda e: e.scalar_tensor_tensor(qn.ap, qf, pcol(P_MQ + l), rstd.ap, ALU.mult, ALU.mult),
                reads=[mb["qf_tile"], rstd, params], writes=[qn])
            sc = 128.0 ** -0.5
            if SUB2 <= 1:
                return
            for tb in range(4):
                Es = []
                for mt in range(2):
                    bS = next_bank()
                    S.T(lambda e, bS=bS, mt=mt, tb=tb: e.matmul(ps[:, bS, :], kT.ap[:, hh, mt * 128:(mt + 1) * 128],
                                                               qn.ap[:, tb * 512:(tb + 1) * 512], start=True, stop=True),
                        reads=[kT, qn], writes=[PB[bS]])
                    E = mb["E"][mb["ei"] % 4]
                    mb["ei"] += 1
                    S.A(lambda e, bS=bS, E=E: e.activation(E.ap, ps[:, bS, :], AF.Exp, scale=sc), reads=[PB[bS]], writes=[E])
                    Es.append(E)
                if SUB2 <= 2:
                    continue
                bo = next_bank()
                bl = next_bank()
                for mt in range(2):
                    S.T(lambda e, mt=mt, E=Es[mt], bo=bo: e.matmul(ps[:, bo, :], vsb.ap[:, hh, mt, :], E.ap, start=(mt == 0), stop=(mt == 1)),
                        reads=[vsb, Es[mt]], writes=[PB[bo]])
                for mt in range(2):
                    S.T(lambda e, mt=mt, E=Es[mt], bl=bl: e.matmul(ps[:, bl, :], cm(C_ONES1), E.ap, start=(mt == 0), stop=(mt == 1)),
                        reads=[cbf, Es[mt]], writes=[PB[bl]])
                if SUB2 <= 3:
                    continue
                rl = mb["rl"]
                osb = mb["osb"]
                recip_act(rl.ap, ps[:, bl, :], [PB[bl]], rl)
                S.V(lambda e, tb=tb, bo=bo: e.tensor_tensor(catst.ap[:, tb * 512:(tb + 1) * 512], ps[:, bo, :], rl.ap, ALU.mult),
                    reads=[PB[bo], rl], writes=[catst])
            if SUB2 <= 4:
                return
            S.dma("sp", cat_d[12 + hh], catst.ap, reads=[catst], writes=[CAT[12 + hh]])

        def out_proj(l):
            AR.reset()
            catq = [AR.alloc([4, S_], BF16, f"catq{i}") for i in range(4)]
            xo = [AR.alloc([S_], F32, f"xo{i}") for i in range(2)]
            cv = cat_d.rearrange("c p t -> p c t")
            for i in range(4):
                S.dma("sp", catq[i].ap, cv[:, 4 * i:4 * i + 4, :], reads=CAT[4 * i:4 * i + 4], writes=[catq[i]])

            def evac(oc, g):
                x_o = xo[oc % 2]
                S.dma("sp", x_o.ap, xs[oc * 128:(oc + 1) * 128, :], reads=xs_rows(oc, 0, S_), writes=[x_o])
                S.V(lambda e: e.tensor_tensor(x_o.ap.rearrange("p (b n) -> p b n", b=4), psg(g),
                                              x_o.ap.rearrange("p (b n) -> p b n", b=4), ALU.add),
                    reads=pbs(g) + [x_o], writes=[x_o])
                S.dma("sp", xs[oc * 128:(oc + 1) * 128, :], x_o.ap, reads=[x_o], writes=xs_rows(oc, 0, S_))
                return None

            gemm_fm(lambda oc, kq: w_out[l][:, oc * 128:(oc + 1) * 128], KC, KC,
                    lambda kc: [catq[kc // 4]], lambda kc, tb: catq[kc // 4].ap[:, kc % 4, tb * 512:(tb + 1) * 512], 4, evac)

        def ffn(l):
            HT = 1024
            for half in range(2):
                AR.reset()
                z = [AR.alloc([HT], BF16, f"z{i}") for i in range(64)]
                hh = AR.alloc([KC, HT], BF16, "ffn_h")
                rr = [AR.alloc([HT], F32, f"ffn_r{i}") for i in range(2)]
                xo = [AR.alloc([HT], F32, f"ffn_xo{i}") for i in range(2)]
                norm_fm(xs, P_FFN + 16 * l, hh, half * HT, HT, 512, 0)

                def evac1(fc, g):
                    r = rr[fc % 2]
                    S.A(lambda e: e.activation(r.ap.rearrange("p (b n) -> p b n", b=2), psg(g), AF.Relu),
                        reads=pbs(g), writes=[r])
                    S.V(lambda e: e.tensor_tensor(z[fc].ap, r.ap, r.ap, ALU.mult), reads=[r], writes=[z[fc]])
                    return None

                for i in range(64):
                    z[i] = AR.at(z[i].off, [HT], BF16, f"z{i}")
                gemm_fm(lambda fc, kq: w_ff1[l][:, fc * 128:(fc + 1) * 128], 64, KC,
                        lambda kc: [hh], lambda kc, tb: hh.ap[:, kc, tb * 512:(tb + 1) * 512], 2, evac1)

                def evac2(dc, g):
                    x_o = xo[dc % 2]
                    S.dma("sp", x_o.ap, xs[dc * 128:(dc + 1) * 128, half * HT:(half + 1) * HT], reads=xs_rows(dc, half * HT, HT), writes=[x_o])
                    S.V(lambda e: e.tensor_tensor(x_o.ap.rearrange("p (b n) -> p b n", b=2), psg(g),
                                                  x_o.ap.rearrange("p (b n) -> p b n", b=2), ALU.add),
                        reads=pbs(g) + [x_o], writes=[x_o])
                    S.dma("sp", xs[dc * 128:(dc + 1) * 128, half * HT:(half + 1) * HT], x_o.ap, reads=[x_o], writes=xs_rows(dc, half * HT, HT))
                    return None

                gemm_fm(lambda dc, kq: w_ff2[l][kq * 2048:(kq + 1) * 2048, dc * 128:(dc + 1) * 128], KC, 64,
                        lambda kc: [z[kc]], lambda kc, tb: z[kc].ap[:, tb * 512:(tb + 1) * 512], 2, evac2)

        def mixer0():
            AR.reset()
            h = AR.alloc([KC, S_], BF16, "h0")
            base = AR.top
            kT, vsb = mem_prep(0, base)
            if SUB <= 1:
                return
            base2 = vsb.off + vsb.size
            norm_fm(xs, P_MIX, h, 0, S_, 512, base2)
            if SUB <= 2:
                return
            AR.reset(base2)
            PADW = S_ + 16
            upad = [AR.alloc([PADW], F32, f"upad{i}") for i in range(2)]
            sb = [AR.alloc([PADW], F32, f"spp{i}") for i in range(2)]
            pooled = [AR.alloc([S_], BF16, f"pooled{i}") for i in range(6)]
            catst = [AR.alloc([S_], BF16, f"catst{i}") for i in range(2)]
            tmp16 = AR.alloc([16], F32, "tmp16")
            sq = [AR.alloc([S_], BF16, f"msq{i}") for i in range(2)]
            mb = mem_attn_bufs()
            pw_tiles = {}
            for t in upad + sb:
                S.V(lambda e, t=t: e.memset(t.ap[:, 0:16], 0.0), writes=[t])
            cst = {"i": 0}

            def token_out(gi):
                for oc2 in range(3):
                    wt = load_w(pool_w[gi, :, oc2 * 128:(oc2 + 1) * 128], 3)
                    g = next_group(4)
                    for ic in range(3):
                        pl = pooled[(gi * 3 + ic) % 6]
                        for tb in range(4):
                            S.T(lambda e, b=g[tb], ic=ic, tb=tb, wt=wt, pl=pl: e.matmul(
                                ps[:, b, :], wt.ap[:, ic, :], pl.ap[:, tb * 512:(tb + 1) * 512], start=(ic == 0), stop=(ic == 2)),
                                reads=[wt, pl], writes=[PB[g[tb]]])
                    cs_ = catst[cst["i"] % 2]
                    cst["i"] += 1
                    oc = gi * 3 + oc2
                    S.A(lambda e, cs_=cs_, g=g, oc=oc: e.activation(cs_.ap.rearrange("p (b n) -> p b n", b=4), psg(g),
                                                                    AF.Identity, scale=pcol(P_PSC + oc)),
                        reads=pbs(g) + [params], writes=[cs_])
                    S.dma("sp", cat_d[oc], cs_.ap, reads=[cs_], writes=[CAT[oc]])

            def evac(oc, g):
                up = upad[oc % 2]
                S.A(lambda e: e.copy(up.ap[:, 16:PADW].rearrange("p (b n) -> p b n", b=4), psg(g)), reads=pbs(g), writes=[up])
                if SUB <= 3:
                    return None
                if oc < 12:
                    gi = oc // 3
                    w = 2 << gi
                    src = up
                    for step in range(gi + 1):
                        sh = 1 << step
                        dst = sb[step % 2]
                        S.V(lambda e, src=src, dst=dst, sh=sh: e.tensor_tensor(
                            dst.ap[:, 16:PADW], src.ap[:, 16:PADW], src.ap[:, 16 - sh:PADW - sh], ALU.add),
                            reads=[src], writes=[dst])
                        src = dst
                    pl = pooled[oc % 6]
                    S.V(lambda e, src=src, pl=pl: e.scalar_tensor_tensor(
                        pl.ap, src.ap[:, 16:PADW], 1.0 / w, up.ap[:, 16:PADW], ALU.mult, ALU.subtract),
                        reads=[src, up], writes=[pl])
                    S.V(lambda e, src=src: e.tensor_tensor(tmp16.ap, src.ap[:, 16:32], pcol(P_INVC + 16 * gi, 16), ALU.mult),
                        reads=[src, params], writes=[tmp16])
                    S.V(lambda e, pl=pl: e.tensor_tensor(pl.ap[:, 0:16], tmp16.ap, up.ap[:, 16:32], ALU.subtract),
                        reads=[tmp16, up], writes=[pl])
                    if oc % 3 == 2 and SUB >= 5:
                        return lambda: token_out(gi)
                    return None
                else:
                    if SUB <= 5:
                        return None
                    hh = oc - 12
                    s_q = sq[oc % 2]
                    S.A(lambda e: e.activation(s_q.ap.rearrange("p (b n) -> p b n", b=4), psg(g), AF.Square),
                        reads=pbs(g), writes=[s_q])
                    cs_ = catst[cst["i"] % 2]
                    cst["i"] += 1

                    def post():
                        mb["qf_tile"] = up
                        mem_attn_head(0, hh, up.ap[:, 16:PADW], s_q, kT, vsb, mb, cs_)
                    return post

            gemm_fm(lambda oc, kq: w_in[0][:, oc * 128:(oc + 1) * 128], KC, KC,
                    lambda kc: [h], lambda kc, tb: h.ap[:, kc, tb * 512:(tb + 1) * 512], 4, evac)

        def hnr_bufs():
            d = {}
            d["cs"] = AR.alloc([2, S_], F32, "ropecs")
            S.dma("sp", d["cs"].ap, cs_d, writes=[d["cs"]])
            d["kf"] = [AR.alloc([S_], F32, f"kf{i}") for i in range(3)]
            d["sq"] = [AR.alloc([S_], BF16, f"ksq{i}") for i in range(2)]
            d["rstd"] = AR.alloc([S_], F32, "krstd")
            d["knb"] = [AR.alloc([S_], BF16, f"knb{i}") for i in range(2)]
            d["t1"] = AR.alloc([S_], F32, "kt1")
            d["t2"] = AR.alloc([S_], F32, "kt2")
            d["ob"] = [AR.alloc([S_], BF16, f"kob{i}") for i in range(2)]
            d["i"] = 0
            return d

        def hnr_evac(hb, g, gcol, dst, dtile):
            i = hb["i"]
            hb["i"] += 1
            kf = hb["kf"][i % 3]
            sq = hb["sq"][i % 2]
            ob = hb["ob"][i % 2]
            rstd, knb, t1, t2, cs = hb["rstd"], hb["knb"][i % 2], hb["t1"], hb["t2"], hb["cs"]
            v4 = lambda t: t.ap.rearrange("p (b n) -> p b n", b=4)
            v2 = lambda t, h: t.ap[:, h * 1024:(h + 1) * 1024].rearrange("p (b n) -> p b n", b=2)
            for h_ in range(2):
                ga = g[2 * h_:2 * h_ + 2]
                gd = g[2 * (1 - h_):2 * (1 - h_) + 2]
                S.A(lambda e, h_=h_, ga=ga: e.activation(v2(sq, h_), psg(ga), AF.Square), reads=pbs(ga), writes=[sq])
                S.V(lambda e, h_=h_, gd=gd: e.tensor_copy(v2(kf, 1 - h_), psg(gd)), reads=pbs(gd), writes=[kf])

            def postA():
                g2 = next_group(4)
                for tb in range(4):
                    S.T(lambda e, tb=tb: e.matmul(ps[:, g2[tb], :], cm(C_BLK), sq.ap[:, tb * 512:(tb + 1) * 512], start=True, stop=True),
                        reads=[sq, cbf], writes=[PB[g2[tb]]])
                rsqrt_act(v4(rstd), psg(g2), pbs(g2), rstd)
                S.V(lambda e: e.scalar_tensor_tensor(kf.ap, kf.ap, pcol(gcol), rstd.ap, ALU.mult, ALU.mult),
                    reads=[kf, rstd, params], writes=[kf])
                S.A(lambda e: e.copy(knb.ap, kf.ap), reads=[kf], writes=[knb])

            def postB():
                g3 = next_group(4)
                for tb in range(4):
                    S.T(lambda e, tb=tb: e.matmul(ps[:, g3[tb], :], cm(C_RT), knb.ap[:, tb * 512:(tb + 1) * 512], start=True, stop=True),
                        reads=[knb, cbf], writes=[PB[g3[tb]]])
                S.G(lambda e: e.tensor_tensor(t1.ap, kf.ap, cs.ap[:, 0, :], ALU.mult), reads=[kf, cs], writes=[t1])
                S.V(lambda e: e.tensor_tensor(v4(t2), psg(g3), cs.ap[:, 1, :].rearrange("p (b n) -> p b n", b=4), ALU.mult),
                    reads=pbs(g3) + [cs], writes=[t2])
                S.V(lambda e: e.tensor_tensor(ob.ap, t1.ap, t2.ap, ALU.add), reads=[t1, t2], writes=[ob])
                S.dma("sp", dst, ob.ap, reads=[ob], writes=[dtile])
            return [postA, postB]

        def kv_phase():
            AR.reset()
            h = AR.alloc([KC, S_], BF16, "hkv")
            base = AR.top
            norm_fm(xs, P_KVN, h, 0, S_, 512, base)
            AR.reset(base)
            hb = hnr_bufs()
            vst = [AR.alloc([S_], BF16, f"vst{i}") for i in range(2)]

            def evac(oc, g):
                if oc < 12:
                    return hnr_evac(hb, g, P_KN, kT_d[oc], KT[oc])
                v_s = vst[oc % 2]
                S.A(lambda e: e.copy(v_s.ap.rearrange("p (b n) -> p b n", b=4), psg(g)), reads=pbs(g), writes=[v_s])
                S.dma("sp", vT_d[oc - 12], v_s.ap, reads=[v_s], writes=[VT[oc - 12]])
                return None

            gemm_fm(lambda oc, kq: w_kv[:, oc * 128:(oc + 1) * 128], 24, KC,
                    lambda kc: [h], lambda kc, tb: h.ap[:, kc, tb * 512:(tb + 1) * 512], 4, evac, use_hooks=True)

        def q_phase():
            AR.reset()
            h = AR.alloc([KC, S_], BF16, "hq")
            base = AR.top
            kT, vsb = mem_prep(1, base)
            base2 = vsb.off + vsb.size
            norm_fm(xs, P_MIX + 16, h, 0, S_, 512, base2)
            AR.reset(base2)
            hb = hnr_bufs()
            qf = hb["kf"]
            sq = hb["sq"]
            catst = [AR.alloc([S_], BF16, f"qcatst{i}") for i in range(2)]
            mb = mem_attn_bufs(hb["rstd"])

            def evac(oc, g):
                if oc < 12:
                    return hnr_evac(hb, g, P_QN, qT_d[oc], QT[oc])
                hh = oc - 12
                q_f = qf[hb["i"] % 3]
                s_q = sq[hb["i"] % 2]
                hb["i"] += 1
                cs_ = catst[oc % 2]
                S.A(lambda e: e.copy(q_f.ap.rearrange("p (b n) -> p b n", b=4), psg(g)), reads=pbs(g), writes=[q_f])
                S.A(lambda e: e.activation(s_q.ap.rearrange("p (b n) -> p b n", b=4), psg(g), AF.Square), reads=pbs(g), writes=[s_q])

                def post():
                    mb["qf_tile"] = q_f
                    mem_attn_head(1, hh, q_f.ap, s_q, kT, vsb, mb, cs_)
                return post

            gemm_fm(lambda oc, kq: w_in[1][:, oc * 128:(oc + 1) * 128], KC, KC,
                    lambda kc: [h], lambda kc, tb: h.ap[:, kc, tb * 512:(tb + 1) * 512], 4, evac, use_hooks=True)

        def attn_phase():
            AR.reset()
            kk = [[AR.alloc([S_], BF16, f"kk{s}_{i}") for i in range(2)] for s in range(2)]
            zq = [[AR.alloc([S_], BF16, f"zq{m}_{hf}") for hf in range(2)] for m in range(2)]
            for m in range(2):
                for hf in range(2):
                    z0 = 64 * (1 - hf)
                    S.V(lambda e, m=m, hf=hf, z0=z0: e.memset(zq[m][hf].ap[z0:z0 + 64, :], 0.0), writes=[zq[m][hf]])
            vT = [AR.alloc([S_], BF16, f"vT{i}") for i in range(2)]
            vtok = [AR.alloc([16, 128], BF16, f"vtok{i}") for i in range(2)]
            E = [AR.alloc([512], BF16, f"E{i}") for i in range(8)]
            r12 = AR.alloc([1024], F32, "r12")
            aa = AR.alloc([512], F32, "aa")
            bb = AR.alloc([512], F32, "bb")
            oo = AR.alloc([512], F32, "oo")
            osq = AR.alloc([512], BF16, "osq")
            rs = AR.alloc([512], F32, "ars")
            catst = [AR.alloc([S_], BF16, f"acat{i}") for i in range(2)]
            sbank = {"i": 0}
            spair = {"i": 0}
            ei = {"i": 0}
            sc = 64.0 ** -0.5
            BO = (4, 5)
            BL = (6, 7)

            def sb_next():
                b = sbank["i"] % 4
                sbank["i"] += 1
                return b

            for hp in range(6):
                k2t = kk[hp % 2]
                S.dma("sp", k2t[0].ap, kT_d[hp], reads=[KT[hp]], writes=[k2t[0]])
                S.dma("sp", k2t[1].ap, kT_d[6 + hp], reads=[KT[6 + hp]], writes=[k2t[1]])
                for m in range(2):
                    for hf in range(2):
                        q0 = 64 * hf
                        S.dma("sp", zq[m][hf].ap[q0:q0 + 64, :], qT_d[6 * m + hp][q0:q0 + 64, :],
                              reads=[QT[6 * m + hp]], writes=[zq[m][hf]])
                for half in range(2):
                    hd = 2 * hp + half
                    p0 = 64 * half
                    v_T = vT[hd % 2]
                    v_k = vtok[hd % 2]
                    S.dma("sp", v_T.ap, vT_d[hd], reads=[VT[hd]], writes=[v_T])
                    for t4i in range(4):
                        b = sb_next()
                        pb16 = ps[:, b, :].bitcast(BF16)
                        for j in range(4):
                            tt = t4i * 4 + j
                            S.T(lambda e, pb16=pb16, j=j, tt=tt, v_T=v_T: e.transpose(
                                pb16[:, j * 128:(j + 1) * 128], v_T.ap[:, tt * 128:(tt + 1) * 128], cm(C_IDENT)),
                                reads=[v_T, cbf], writes=[PB[b]])
                        S.V(lambda e, pb16=pb16, t4i=t4i, v_k=v_k: e.tensor_copy(
                            v_k.ap[:, t4i * 4:(t4i + 1) * 4, :], pb16[:, 0:512].rearrange("p (j d) -> p j d", j=4)),
                            reads=[PB[b]], writes=[v_k])
                    cs_ = catst[hd % 2]
                    DEPTH = 2
                    queue = []
                    tails = []

                    def issue_S(qb, kb, m, half=half, k2t=k2t):
                        nkb = 4 * qb + 4
                        i_d = kb - 4 * qb
                        c0 = 128 * i_d if i_d > 0 else 0
                        qt = zq[m][half]
                        kt = k2t[m]
                        bS = sb_next()
                        diag = i_d >= 0
                        S.T(lambda e: e.matmul(ps[:, bS, c0:512], kt.ap[:, kb * 128:(kb + 1) * 128],
                                               qt.ap[:, qb * 512 + c0:(qb + 1) * 512], start=True, stop=(not diag)),
                            reads=[kt, qt], writes=[PB[bS]])
                        if diag:
                            S.T(lambda e: e.matmul(ps[:, bS, c0:c0 + 128], cm(C_IDENT), cm(C_TRIB), start=False, stop=True),
                                reads=[cbf], writes=[PB[bS]])
                        Et = E[ei["i"] % len(E)]
                        ei["i"] += 1
                        S.A(lambda e: e.activation(Et.ap[:, c0:512], ps[:, bS, c0:512], AF.Exp, scale=sc),
                            reads=[PB[bS]], writes=[Et])
                        return (qb, kb, m, c0, nkb, Et)

                    def post_a(qb):
                        S.V(lambda e: e.tensor_copy(aa.ap, ps[:, BO[0], :]), reads=[PB[BO[0]]], writes=[aa])
                        S.V(lambda e: e.tensor_copy(bb.ap, ps[:, BO[1], :]), reads=[PB[BO[1]]], writes=[bb])
                        recip_act(r12.ap.rearrange("p (b n) -> p b n", b=2), ps[:, BL[0]:BL[1] + 1, :], [PB[BL[0]], PB[BL[1]]], r12)
                        S.V(lambda e: e.tensor_tensor(aa.ap, aa.ap, r12.ap[:, 0:512], ALU.mult), reads=[aa, r12], writes=[aa])
                        S.V(lambda e: e.tensor_tensor(bb.ap, bb.ap, r12.ap[:, 512:1024], ALU.mult), reads=[bb, r12], writes=[bb])
                        S.V(lambda e: e.scalar_tensor_tensor(oo.ap, bb.ap, NEGLAM, aa.ap, ALU.mult, ALU.add),
                            reads=[bb, aa, small], writes=[oo])
                        S.V(lambda e: e.tensor_tensor(osq.ap, oo.ap, oo.ap, ALU.mult), reads=[oo], writes=[osq])

                    def post_b(qb, cs_=cs_):
                        bs2 = sb_next()
                        S.T(lambda e: e.matmul(ps[:, bs2, :], cm(C_ONESHD), osq.ap, start=True, stop=True),
                            reads=[osq, cbf], writes=[PB[bs2]])
                        rsqrt_act(rs.ap, ps[:, bs2, :], [PB[bs2]], rs)
                        S.V(lambda e: e.scalar_tensor_tensor(cs_.ap[:, qb * 512:(qb + 1) * 512], oo.ap, SUBG, rs.ap, ALU.mult, ALU.mult),
                            reads=[oo, rs, small], writes=[cs_])

                    def issue_PV(info, v_k=v_k):
                        qb, kb, m, c0, nkb, Et = info
                        S.T(lambda e: e.matmul(ps[:, BO[m], c0:512], v_k.ap[:, kb, :], Et.ap[:, c0:512],
                                               start=(kb == 0), stop=(kb == nkb - 1)),
                            reads=[v_k, Et], writes=[PB[BO[m]]])
                        S.T(lambda e: e.matmul(ps[:, BL[m], c0:512], cm(C_ONES1), Et.ap[:, c0:512],
                                               start=(kb == 0), stop=(kb == nkb - 1)),
                            reads=[cbf, Et], writes=[PB[BL[m]]])
                        for t in tails:
                            t[0] -= 1
                        while tails and tails[0][0] <= 0:
                            tails.pop(0)[1]()
                        if kb == nkb - 1 and m == 1:
                            post_a(qb)
                            tails.append([5, lambda qb=qb: post_b(qb)])

                    steps = [(qb, kb, m) for qb in range(4) for kb in range(4 * qb + 4) for m in range(2)]
                    for st_ in steps:
                        queue.append(issue_S(*st_))
                        if len(queue) > DEPTH:
                            issue_PV(queue.pop(0))
                    while queue:
                        issue_PV(queue.pop(0))
                    while tails:
                        tails.pop(0)[1]()
                    S.dma("sp", cat_d[hd], cs_.ap, reads=[cs_], writes=[CAT[hd]])

        S.dma("sp", params.ap, params_d, writes=[params])
        S.dma("sp", cbf.ap, cbf_d.rearrange("p (a b) -> p a b", a=NCB), writes=[cbf])
        S.V(lambda e: e.memset(EPSC, EPS), writes=[small])
        AR.reset()
        lt = AR.alloc([64], F32, "lam_tmp")
        l4 = AR.alloc([4], F32, "lam4")
        dl = lambda i: params.ap[:, P_DL + 64 * i:P_DL + 64 * (i + 1)]
        S.V(lambda e: e.tensor_tensor(lt.ap, dl(0), dl(1), ALU.mult), reads=[params], writes=[lt])
        S.V(lambda e: e.reduce_sum(l4.ap[:, 0:1], lt.ap, axis=AX.X), reads=[lt], writes=[l4])
        S.V(lambda e: e.tensor_tensor(lt.ap, dl(2), dl(3), ALU.mult), reads=[params, l4], writes=[lt])
        S.V(lambda e: e.reduce_sum(l4.ap[:, 1:2], lt.ap, axis=AX.X), reads=[lt], writes=[l4])
        S.A(lambda e: e.activation(l4.ap[:, 2:4], l4.ap[:, 0:2], AF.Exp), reads=[l4], writes=[l4])
        S.V(lambda e: e.tensor_tensor(NEGLAM, l4.ap[:, 3:4], l4.ap[:, 2:3], ALU.subtract), reads=[l4], writes=[small])
        S.V(lambda e: e.tensor_scalar_add(NEGLAM, NEGLAM, -LAM_INIT), reads=[small], writes=[small])
        S.V(lambda e: e.tensor_scalar_mul(SUBG, pcol(P_SUB), 1.0 - LAM_INIT), reads=[params], writes=[small])

        phases = [
            lambda: transpose_in(x_in, xs, S_, lambda dc, tb: [XS[dc][tb]]),
            lambda: transpose_in(mem_in, memT, MEM, lambda dc, tb: [MEMT]),
            mixer0,
            lambda: out_proj(0),
            lambda: ffn(0),
            kv_phase,
            q_phase,
            attn_phase,
            lambda: out_proj(1),
            lambda: ffn(1),
        ]
        for i, ph in enumerate(phases):
            if i > stop_after:
                break
            ph()
        transpose_out()

        with nc.Block() as block:
            S.emit(block, engsem, rings)
    return nc


_CACHE = {}


def kernel(**inputs):
    stop_after = int(os.environ.get("MK_STOP", "99"))
    dbg = os.environ.get("MK_DBG", "0") == "1"
    inp = {k: np.asarray(v) for k, v in inputs.items()}
    key = (stop_after, dbg)
    if key not in _CACHE:
        _CACHE[key] = build_nc(stop_after, dbg)
    nc = _CACHE[key]
    params = pack_params(inp)
    cbf = const_bf16()
    cs = rope_cs()
    shared = {
        "pool_w": np.ascontiguousarray(inp["pool_w"][0], np.float32),
        "w_kv": np.ascontiguousarray(inp["w_kv"], np.float32),
        "params": params, "cbf": cbf, "ropecs": cs,
    }
    for l in range(2):
        shared[f"w_in{l}"] = np.ascontiguousarray(inp["w_in"][l], np.float32)
        shared[f"w_out{l}"] = np.ascontiguousarray(inp["w_out"][l], np.float32)
        shared[f"w_mkv{l}"] = np.ascontiguousarray(inp["w_mem_kv"][l], np.float32)
        shared[f"w_ff1{l}"] = np.ascontiguousarray(inp["w_ff1"][l], np.float32)
        shared[f"w_ff2{l}"] = np.ascontiguousarray(inp["w_ff2"][l], np.float32)
    in_maps = []
    for b in range(8):
        m = dict(shared)
        m["x"] = np.ascontiguousarray(inp["x"][b], np.float32)
        m["mem"] = np.ascontiguousarray(inp["mem"][b], np.float32)
        in_maps.append(m)
    ncores = int(os.environ.get("MK_CORES", "8"))
    res = run_bass_kernel_spmd(nc, in_maps[:ncores], core_ids=list(range(ncores)))
    if dbg:
        kernel.last_results = res.results
    out = np.stack([np.asarray(r["out"], np.float32) for r in res.results], axis=0)
    return out
```

```python
import math
import os
from contextlib import ExitStack

import ml_dtypes
import numpy as np

import concourse.bass as bass
import concourse.mybir as mybir
from concourse.bass_utils import run_bass_kernel_spmd

F32 = mybir.dt.float32
BF16 = mybir.dt.bfloat16
U8 = mybir.dt.uint8
AF = mybir.ActivationFunctionType
ALU = mybir.AluOpType
AX = mybir.AxisListType

S_ = 2048
D = 2048
KC = 16
MEM = 256
EPS = 1e-6
LAM_INIT = 0.8 - 0.6 * math.exp(-0.3 * 1)
ROPE_THETA = 500000.0


class Tile:
    __slots__ = ("ap", "w", "r", "rd", "name", "off", "size", "excl")

    def __init__(self, ap, name="", off=-1, size=0):
        self.excl = False
        self.ap = ap
        self.w = None
        self.r = {}
        self.rd = []
        self.name = name
        self.off = off
        self.size = size


class Op:
    __slots__ = ("eng", "fn", "deps", "marked", "semval", "is_dma", "slot", "dval", "idx")

    def __init__(self, eng, fn, is_dma=False):
        self.eng = eng
        self.fn = fn
        self.deps = set()
        self.marked = False
        self.semval = 0
        self.is_dma = is_dma
        self.slot = -1
        self.dval = 0
        self.idx = 0


ENGS = ("pe", "act", "dve", "pool", "sp")
RING = {"sp": 16, "pool": 8}


class Sched:
    def __init__(self):
        self.ops = {e: [] for e in ENGS}
        self.dma_count = {q: 0 for q in RING}
        self.dma_last = {q: [None] * RING[q] for q in RING}

    def add(self, eng, fn, reads=(), writes=(), dma=False):
        o = Op(eng, fn, dma)
        deps = o.deps
        for t in reads:
            if t.w is not None:
                deps.add(t.w)
            if t.excl:
                for en, ro in t.r.items():
                    if en != eng:
                        deps.add(ro)
        for t in writes:
            if t.w is not None:
                deps.add(t.w)
            deps.update(t.r.values())
            deps.update(t.rd)
        for t in reads:
            if dma:
                t.rd.append(o)
            else:
                t.r[eng] = o
        for t in writes:
            t.w = o
            t.r = {}
            t.rd = []
        if dma:
            k = self.dma_count[eng]
            R = RING[eng]
            o.slot = k % R
            o.dval = 16 * (k // R + 1)
            prev = self.dma_last[eng][o.slot]
            if prev is not None:
                deps.add(prev)
            self.dma_last[eng][o.slot] = o
            self.dma_count[eng] = k + 1
        if eng == "pe" and not dma:
            o.deps = {d for d in deps if d.is_dma or d.eng != "pe"}
        o.deps.discard(o)
        o.idx = len(self.ops[eng])
        self.ops[eng].append(o)
        return o

    def T(self, fn, reads=(), writes=()):
        return self.add("pe", fn, reads, writes)

    def A(self, fn, reads=(), writes=()):
        return self.add("act", fn, reads, writes)

    def V(self, fn, reads=(), writes=()):
        return self.add("dve", fn, reads, writes)

    def G(self, fn, reads=(), writes=()):
        return self.add("pool", fn, reads, writes)

    def dma(self, q, out_ap, in_ap, reads=(), writes=()):
        return self.add(q, lambda e: e.dma_start(out=out_ap, in_=in_ap), reads, writes, dma=True)

    def finalize(self):
        for e in ENGS:
            for o in self.ops[e]:
                for d in o.deps:
                    if not d.is_dma:
                        d.marked = True
        for e in ENGS:
            c = 0
            for o in self.ops[e]:
                if (not o.is_dma) and o.marked:
                    c += 1
                    o.semval = c

    def emit(self, block, engsem, rings):
        self.finalize()
        sched = self

        def run(ename, e):
            seen = {}
            for o in sched.ops[ename]:
                for d in sorted(o.deps, key=lambda d: (d.eng, d.idx)):
                    if d.is_dma:
                        key = (d.eng, d.slot)
                        sem = rings[d.eng][d.slot]
                        val = d.dval
                    else:
                        key = d.eng
                        sem = engsem[d.eng]
                        val = d.semval
                    if seen.get(key, 0) >= val:
                        continue
                    seen[key] = val
                    e.wait_ge(sem, val)
                ins = o.fn(e)
                if o.is_dma:
                    ins.then_inc(rings[ename][o.slot], 16)
                elif o.marked:
                    ins.then_inc(engsem[ename], 1)
            if ename in RING:
                for s, last in enumerate(sched.dma_last[ename]):
                    if last is not None and seen.get((ename, s), 0) < last.dval:
                        e.wait_ge(rings[ename][s], last.dval)

        @block.tensor
        def _(e):
            run("pe", e)

        @block.scalar
        def _(e):
            run("act", e)

        @block.vector
        def _(e):
            run("dve", e)

        @block.gpsimd
        def _(e):
            run("pool", e)

        @block.sync
        def _(e):
            run("sp", e)


class Arena:
    def __init__(self, ap, nbytes, sched):
        self.ap = ap
        self.nbytes = nbytes
        self.live = []
        self.S = sched
        self.top = 0

    def reset(self, top=0):
        self.top = top

    def alloc(self, shape, dt, name=""):
        isz = 4 if dt == F32 else 2
        n = int(np.prod(shape)) * isz
        off = (self.top + 31) // 32 * 32
        assert off + n <= self.nbytes, (name, off, n, self.nbytes)
        self.top = off + n
        return self.at(off, shape, dt, name)

    def at(self, off, shape, dt, name=""):
        isz = 4 if dt == F32 else 2
        n = int(np.prod(shape)) * isz
        assert off + n <= self.nbytes, (name, off, n, self.nbytes)
        ap = self.ap[:, off:off + n].bitcast(dt)
        if len(shape) == 2:
            ap = ap.rearrange("p (a b) -> p a b", a=shape[0])
        elif len(shape) == 3:
            ap = ap.rearrange("p (a b c) -> p a b c", a=shape[0], b=shape[1])
        t = Tile(ap, name, off, n)
        keep = []
        ops = self.S.ops
        for o in self.live:
            if o.off < off + n and off < o.off + o.size:
                cands = list(o.r.values())
                if o.w is not None:
                    if o.w.is_dma:
                        t.rd.append(o.w)
                    else:
                        cands.append(o.w)
                for c in cands:
                    cur = t.r.get(c.eng)
                    if cur is None or cur.idx < c.idx:
                        t.r[c.eng] = c
                t.rd.extend(o.rd)
            else:
                keep.append(o)
        keep.append(t)
        self.live = keep
        return t


P_MIX = 0
P_FFN = 32
P_MEMN = 64
P_KVN = 96
P_PSC = 112
P_MQ = 124
P_MK = 126
P_KN = 128
P_QN = 129
P_SUB = 130
P_DL = 131
P_INVC = 387
P_IDENT = 451
PC = 579

C_IDENT, C_ONESD, C_ONESHD, C_ONES1, C_BLK, C_RT, C_TRI, C_TRIB = range(8)
NCB = 8


def _vec16(v):
    return np.asarray(v, np.float32).reshape(16, 128).T


def pack_params(inp):
    p = np.zeros((128, PC), np.float32)
    for l in range(2):
        p[:, P_MIX + 16 * l:P_MIX + 16 * l + 16] = _vec16(inp["mix_norm"][l])
        p[:, P_FFN + 16 * l:P_FFN + 16 * l + 16] = _vec16(inp["ffn_norm"][l])
        p[:, P_MEMN + 16 * l:P_MEMN + 16 * l + 16] = _vec16(inp["mem_norm"][l])
        p[:, P_MQ + l] = inp["mem_q_norm"][l]
        p[:, P_MK + l] = inp["mem_k_norm"][l]
    p[:, P_KVN:P_KVN + 16] = _vec16(inp["kv_norm"])
    p[:, P_PSC:P_PSC + 12] = np.asarray(inp["pool_scale"][0], np.float32).reshape(12, 128).T
    p[:, P_KN] = np.tile(np.asarray(inp["k_norm"], np.float32), 2)
    p[:, P_QN] = np.tile(np.asarray(inp["q_norm"][0], np.float32), 2)
    p[:, P_SUB] = inp["subln_norm"][0]
    p[:, P_DL:P_DL + 256] = np.asarray(inp["diff_lambda"][0], np.float32).reshape(1, 256)
    for g, w in enumerate((2, 4, 8, 16)):
        t = np.arange(16)
        p[:, P_INVC + 16 * g:P_INVC + 16 * g + 16] = (1.0 / np.minimum(t + 1, w)).astype(np.float32)[None, :]
    p[:, P_IDENT:P_IDENT + 128] = np.eye(128, dtype=np.float32)
    return p


def const_bf16():
    c = np.zeros((128, NCB, 128), np.float32)
    c[:, C_IDENT] = np.eye(128)
    c[:, C_ONESD] = 1.0 / 2048.0
    c[:, C_ONESHD] = 1.0 / 128.0
    c[:, C_ONES1] = 1.0
    blk = np.zeros((128, 128))
    blk[:64, :64] = 1.0 / 64.0
    blk[64:, 64:] = 1.0 / 64.0
    c[:, C_BLK] = blk
    Rm = np.zeros((128, 128))
    for hb in (0, 64):
        for j in range(8):
            Rm[hb + j, hb + j + 8] = -1.0
            Rm[hb + j + 8, hb + j] = 1.0
    c[:, C_RT] = Rm.T
    k = np.arange(128)[:, None]
    q = np.arange(128)[None, :]
    c[:, C_TRI] = (k <= q).astype(np.float32)
    c[:, C_TRIB] = np.where(k <= q, 0.0, -30000.0)
    return c.reshape(128, NCB * 128).astype(ml_dtypes.bfloat16)


def rope_cs():
    pos = np.arange(S_, dtype=np.float32)
    inv = (np.float32(ROPE_THETA) ** (-(np.arange(8, dtype=np.float32) * np.float32(2.0)) / np.float32(16))).astype(np.float32)
    ang = (pos[:, None] * inv[None, :]).astype(np.float32)
    cs = np.zeros((128, 2, S_), np.float32)
    cs[:, 0, :] = 1.0
    for hb in (0, 64):
        for j in range(16):
            cs[hb + j, 0, :] = np.cos(ang[:, j % 8])
            cs[hb + j, 1, :] = np.sin(ang[:, j % 8])
    return cs


ARENA_BYTES = 187 * 1024
SUB = int(os.environ.get('MK_SUB', '99'))
SUB2 = int(os.environ.get('MK_SUB2', '99'))
NBW = 4


def build_nc(stop_after=99, dbg=False):
    nc = bass.Bass("TRN2", target_bir_lowering=False)
    okind = "ExternalOutput" if dbg else "Internal"

    def din(name, shape, dt=F32):
        return nc.dram_tensor(name, list(shape), dt, kind="ExternalInput").ap()

    x_in = din("x", [S_, D])
    mem_in = din("mem", [MEM, D])
    w_in = [din(f"w_in{l}", [D, D]) for l in range(2)]
    w_out = [din(f"w_out{l}", [D, D]) for l in range(2)]
    w_mkv = [din(f"w_mkv{l}", [D, 1024]) for l in range(2)]
    w_ff1 = [din(f"w_ff1{l}", [D, 4 * D]) for l in range(2)]
    w_ff2 = [din(f"w_ff2{l}", [4 * D, D]) for l in range(2)]
    pool_w = din("pool_w", [4, 384, 384])
    w_kv = din("w_kv", [D, 3072])
    params_d = din("params", [128, PC])
    cbf_d = din("cbf", [128, NCB * 128], BF16)
    cs_d = din("ropecs", [128, 2, S_])
    out_d = nc.dram_tensor("out", [S_, D], F32, kind="ExternalOutput").ap()

    xs = nc.dram_tensor("xs", [D, S_], F32, kind=okind).ap()
    memT = nc.dram_tensor("memT", [D, MEM], F32, kind=okind).ap()
    dbg_b = nc.dram_tensor("dbg_b", [128, 2048], BF16, kind=okind).ap()
    dbg_h = nc.dram_tensor("dbg_h", [128, 4096], BF16, kind=okind).ap()
    cat_d = nc.dram_tensor("cat_d", [16, 128, S_], BF16, kind=okind).ap()
    kT_d = nc.dram_tensor("kT_d", [12, 128, S_], BF16, kind=okind).ap()
    qT_d = nc.dram_tensor("qT_d", [12, 128, S_], BF16, kind=okind).ap()
    vT_d = nc.dram_tensor("vT_d", [12, 128, S_], BF16, kind=okind).ap()

    S = Sched()
    with ExitStack() as ctx:
        engsem = {e: ctx.enter_context(nc.semaphore("sem_" + e)) for e in ENGS}
        rings = {q: [ctx.enter_context(nc.semaphore(f"r_{q}_{i}")) for i in range(RING[q])] for q in RING}
        params_t = ctx.enter_context(nc.sbuf_tensor("sb_params", [128, PC], F32))
        cbf_t = ctx.enter_context(nc.sbuf_tensor("sb_cbf", [128, NCB, 128], BF16))
        small_t = ctx.enter_context(nc.sbuf_tensor("sb_small", [128, 16], F32))
        wring_t = ctx.enter_context(nc.sbuf_tensor("sb_wring", [128, NBW, 16, 128], BF16))
        arena_t = ctx.enter_context(nc.sbuf_tensor("sb_arena", [128, ARENA_BYTES], U8))
        ps_t = ctx.enter_context(nc.psum_tensor("ps_all", [128, 8, 512], F32))
        ps = ps_t[:, :, :]
        PB = [Tile(ps[:, b, :], f"ps{b}") for b in range(8)]
        for t_ in PB:
            t_.excl = True
        params = Tile(params_t[:, :], "params")
        cbf = Tile(cbf_t[:, :, :], "cbf")
        small = Tile(small_t[:, :], "small")
        wring = [Tile(wring_t[:, i, :, :], f"w{i}") for i in range(NBW)]
        AR = Arena(arena_t[:, :], ARENA_BYTES, S)
        XS = [[Tile(None, f"xs{dc}_{tb}") for tb in range(4)] for dc in range(KC)]
        MEMT = Tile(None, "memT")
        CAT = [Tile(None, f"cat{i}") for i in range(16)]
        KT = [Tile(None, f"kT{i}") for i in range(12)]
        QT = [Tile(None, f"qT{i}") for i in range(12)]
        VT = [Tile(None, f"vT{i}") for i in range(12)]

        def xs_cols(c0, n):
            return [XS[dc][tb] for dc in range(KC) for tb in range(c0 // 512, (c0 + n + 511) // 512)]

        def xs_rows(dc, c0, n):
            return [XS[dc][tb] for tb in range(c0 // 512, (c0 + n + 511) // 512)]
        st = {"w": 0, "bank": 0}

        def pcol(c, n=1):
            return params.ap[:, c:c + n]

        def cm(i):
            return cbf.ap[:, i, :]

        ident_f = params.ap[:, P_IDENT:P_IDENT + 128]
        EPSC = small.ap[:, 0:1]
        NEGLAM = small.ap[:, 1:2]
        SUBG = small.ap[:, 2:3]

        def next_group(n):
            res = st.get("reserved", ())
            for _ in range(9):
                b = (st["bank"] + n - 1) // n * n
                if b + n > 8:
                    b = 0
                st["bank"] = b + n
                if not any((x in res) for x in range(b, b + n)):
                    return list(range(b, b + n))
            raise AssertionError("no free PSUM group")

        def next_bank():
            return next_group(1)[0]

        def psg(g):
            return ps[:, g[0]:g[0] + len(g), :]

        def pbs(g):
            return [PB[b] for b in g]

        def load_w(src, nk):
            wt = wring[st["w"] % NBW]
            st["w"] += 1
            S.dma("pool", wt.ap[:, 0:nk, :], src.rearrange("(kc p) c -> p kc c", p=128), writes=[wt])
            return wt

        def rsqrt_act(out_ap, in_ap, rd, wr):
            S.A(lambda e: e.activation(out_ap, in_ap, AF.Ln, bias=EPSC, scale=1.0), reads=rd + [small], writes=[wr])
            S.A(lambda e: e.activation(out_ap, out_ap, AF.Exp, scale=-0.5), reads=[wr], writes=[wr])

        def recip_act(out_ap, in_ap, rd, wr):
            S.A(lambda e: e.activation(out_ap, in_ap, AF.Ln), reads=rd, writes=[wr])
            S.A(lambda e: e.activation(out_ap, out_ap, AF.Exp, scale=-1.0), reads=[wr], writes=[wr])

        def gemm_fm(wsrc, n_oc, nk, rhs_tiles_fn, rhs_fn, ntb, evac, ncols=512, use_hooks=False):
            pending = []
            hooks = {max(0, (3 * nk) // 8 - 1): 0, max(0, (3 * nk) // 4 - 1): 1} if (nk >= 8 and use_hooks) else {}

            def run_stage(h, oc):
                for p in list(pending):
                    if p[1] == h and p[2] < oc:
                        p[0][p[1]]()
                        p[1] += 1
                        p[2] = oc
                        if p[1] >= len(p[0]):
                            pending.remove(p)

            for oc in range(n_oc):
                gk = "g%d" % ntb
                gi = st.get(gk, 0)
                st[gk] = gi + 1
                ng = 8 // ntb
                g = list(range((gi % ng) * ntb, (gi % ng) * ntb + ntb))
                st["bank"] = g[-1] + 1
                st["reserved"] = set(g)
                nq = (nk + 15) // 16
                for kq in range(nq):
                    nkk = min(16, nk - kq * 16)
                    wt = load_w(wsrc(oc, kq), nkk)
                    for k16 in range(nkk):
                        kc = kq * 16 + k16
                        for tb in range(ntb):
                            S.T(lambda e, b=g[tb], wt=wt, k16=k16, kc=kc, tb=tb: e.matmul(
                                ps[:, b, 0:ncols], wt.ap[:, k16, :], rhs_fn(kc, tb),
                                start=(kc == 0), stop=(kc == nk - 1)),
                                reads=[wt] + rhs_tiles_fn(kc), writes=[PB[g[tb]]])
                        if kc in hooks:
                            run_stage(hooks[kc], oc)
                st["reserved"] = ()
                cur = evac(oc, g)
                if not hooks:
                    run_stage(0, oc + 1)
                    run_stage(1, oc + 1)
                if cur is not None:
                    pending.append([list(cur) if isinstance(cur, (list, tuple)) else [cur], 0, oc])
            k = n_oc
            while pending:
                k += 1
                run_stage(0, k)
                run_stage(1, k)

        def transpose_in(src, dst, ntok, dtiles):
            TW = min(512, ntok)
            ntt = TW // 128
            AR.reset()
            xin = [AR.alloc([ntt, D], F32, f"xin{i}") for i in range(2)]
            stg = [AR.alloc([TW], F32, f"stg{i}") for i in range(4)]
            k = 0
            for tb in range(ntok // TW):
                xt = xin[tb % 2]
                S.dma("sp", xt.ap, src[tb * TW:(tb + 1) * TW, :].rearrange("(tt p) d -> p tt d", p=128), writes=[xt])
                for dc in range(KC):
                    b = next_bank()
                    for tt in range(ntt):
                        S.T(lambda e, b=b, tt=tt, dc=dc, xt=xt: e.transpose(
                            ps[:, b, tt * 128:(tt + 1) * 128], xt.ap[:, tt, dc * 128:(dc + 1) * 128], ident_f),
                            reads=[xt, params], writes=[PB[b]])
                    sg = stg[k % 4]
                    eng = S.A if k % 2 == 0 else S.V
                    if k % 2 == 0:
                        S.A(lambda e, sg=sg, b=b: e.copy(sg.ap, ps[:, b, 0:TW]), reads=[PB[b]], writes=[sg])
                    else:
                        S.V(lambda e, sg=sg, b=b: e.tensor_copy(sg.ap, ps[:, b, 0:TW]), reads=[PB[b]], writes=[sg])
                    S.dma("sp", dst[dc * 128:(dc + 1) * 128, tb * TW:(tb + 1) * TW], sg.ap, reads=[sg], writes=dtiles(dc, tb))
                    k += 1

        def transpose_out():
            AR.reset()
            xin = [AR.alloc([KC, 512], F32, f"xo_in{i}") for i in range(2)]
            ost = [AR.alloc([D], F32, f"ost{i}") for i in range(2)]
            k = 0
            j = 0
            xsv = xs.rearrange("(dc p) t -> p dc t", p=128)
            for tb in range(4):
                xt = xin[tb % 2]
                S.dma("sp", xt.ap, xsv[:, :, tb * 512:(tb + 1) * 512], reads=xs_cols(tb * 512, 512), writes=[xt])
                for tt in range(4):
                    o = ost[j % 2]
                    j += 1
                    for dc4 in range(4):
                        b = next_bank()
                        for i in range(4):
                            S.T(lambda e, b=b, i=i, dc4=dc4, tt=tt, xt=xt: e.transpose(
                                ps[:, b, i * 128:(i + 1) * 128], xt.ap[:, dc4 * 4 + i, tt * 128:(tt + 1) * 128], ident_f),
                                reads=[xt, params], writes=[PB[b]])
                        if k % 2 == 0:
                            S.A(lambda e, o=o, b=b, dc4=dc4: e.copy(o.ap[:, dc4 * 512:(dc4 + 1) * 512], ps[:, b, :]),
                                reads=[PB[b]], writes=[o])
                        else:
                            S.V(lambda e, o=o, b=b, dc4=dc4: e.tensor_copy(o.ap[:, dc4 * 512:(dc4 + 1) * 512], ps[:, b, :]),
                                reads=[PB[b]], writes=[o])
                        k += 1
                    r0 = tb * 512 + tt * 128
                    S.dma("sp", out_d[r0:r0 + 128, :], o.ap, reads=[o])

        def norm_fm(src, gcol, h, tok0, T, TW, tmp_base, stiles=None):
            AR.reset(tmp_base)
            xts = [AR.alloc([KC, TW], F32, f"nx{i}") for i in range(2)]
            sq = AR.alloc([KC, TW], BF16, "nsq")
            rstd = AR.alloc([TW], F32, "nrstd")
            srcv = src.rearrange("(dc p) t -> p dc t", p=128)
            for tb in range(T // TW):
                xt = xts[tb % 2]
                c0 = tok0 + tb * TW
                S.dma("sp", xt.ap, srcv[:, :, c0:c0 + TW], reads=(stiles if stiles is not None else xs_cols(c0, TW)), writes=[xt])
                hk = KC // 2
                S.A(lambda e, xt=xt: e.activation(sq.ap[:, 0:hk, :], xt.ap[:, 0:hk, :], AF.Square), reads=[xt], writes=[sq])
                S.A(lambda e, xt=xt: e.activation(sq.ap[:, hk:KC, :], xt.ap[:, hk:KC, :], AF.Square), reads=[xt], writes=[sq])
                b = next_bank()
                for dc in range(KC):
                    S.T(lambda e, b=b, dc=dc: e.matmul(ps[:, b, 0:TW], cm(C_ONESD), sq.ap[:, dc, :],
                                                      start=(dc == 0), stop=(dc == KC - 1)),
                        reads=[sq, cbf], writes=[PB[b]])
                rsqrt_act(rstd.ap, ps[:, b, 0:TW], [PB[b]], rstd)
                for dc in range(KC):
                    S.V(lambda e, dc=dc, xt=xt, tb=tb: e.scalar_tensor_tensor(
                        h.ap[:, dc, tb * TW:(tb + 1) * TW], xt.ap[:, dc, :], pcol(gcol + dc), rstd.ap, ALU.mult, ALU.mult),
                        reads=[xt, rstd, params], writes=[h])

        def mem_prep(l, base):
            AR.reset(base)
            kT = AR.alloc([4, MEM], BF16, "mkT")
            vsb = AR.alloc([4, 2, 128], BF16, "mvsb")
            vsb.ap = AR.ap[:, vsb.off:vsb.off + vsb.size].bitcast(BF16).rearrange("p (h m d) -> p h m d", h=4, m=2)
            hm = AR.alloc([KC, MEM], BF16, "hm")
            kf = [AR.alloc([MEM], F32, f"mkf{i}") for i in range(2)]
            sqm = [AR.alloc([MEM], BF16, f"msq{i}") for i in range(2)]
            vf = [AR.alloc([MEM], BF16, f"mvf{i}") for i in range(2)]
            rs = AR.alloc([MEM], F32, "mrs")
            tmp_base = AR.top
            norm_fm(memT, P_MEMN + 16 * l, hm, 0, MEM, MEM, tmp_base, stiles=[MEMT])

            def evac(oc, g):
                b = g[0]
                if oc < 4:
                    k_f = kf[oc % 2]
                    s_q = sqm[oc % 2]
                    S.A(lambda e: e.copy(k_f.ap, ps[:, b, 0:MEM]), reads=[PB[b]], writes=[k_f])
                    S.A(lambda e: e.activation(s_q.ap, ps[:, b, 0:MEM], AF.Square), reads=[PB[b]], writes=[s_q])

                    def post():
                        b2 = next_bank()
                        S.T(lambda e: e.matmul(ps[:, b2, 0:MEM], cm(C_ONESHD), s_q.ap, start=True, stop=True),
                            reads=[s_q, cbf], writes=[PB[b2]])
                        rsqrt_act(rs.ap, ps[:, b2, 0:MEM], [PB[b2]], rs)
                        S.V(lambda e: e.scalar_tensor_tensor(kT.ap[:, oc, :], k_f.ap, pcol(P_MK + l), rs.ap, ALU.mult, ALU.mult),
                            reads=[k_f, rs, params], writes=[kT])
                    return post
                else:
                    hh = oc - 4
                    v_f = vf[oc % 2]
                    S.A(lambda e: e.copy(v_f.ap, ps[:, b, 0:MEM]), reads=[PB[b]], writes=[v_f])

                    def post():
                        b2 = next_bank()
                        pb16 = ps[:, b2, :].bitcast(BF16)
                        for mt in range(2):
                            S.T(lambda e, mt=mt: e.transpose(pb16[:, mt * 128:(mt + 1) * 128], v_f.ap[:, mt * 128:(mt + 1) * 128], cm(C_IDENT)),
                                reads=[v_f, cbf], writes=[PB[b2]])
                        S.V(lambda e: e.tensor_copy(vsb.ap[:, hh, :, :], pb16[:, 0:256].rearrange("p (m d) -> p m d", m=2)),
                            reads=[PB[b2]], writes=[vsb])
                    return post

            gemm_fm(lambda oc, kq: w_mkv[l][:, oc * 128:(oc + 1) * 128], 8, KC,
                    lambda kc: [hm], lambda kc, tb: hm.ap[:, kc, :], 1, evac, ncols=MEM)
            if dbg and l == 0:
                S.dma("sp", dbg_b[:, 0:1024], kT.ap.rearrange("p a b -> p (a b)"), reads=[kT])
                S.dma("sp", dbg_b[:, 1024:2048], vsb.ap.rearrange("p h m d -> p (h m d)"), reads=[vsb])
                S.dma("sp", dbg_h, hm.ap.rearrange("p a b -> p (a b)"), reads=[hm])
            return kT, vsb

        def mem_attn_bufs(rstd=None):
            d = {}
            d["rstd"] = rstd if rstd is not None else AR.alloc([S_], F32, "ma_rstd")
            d["qn"] = AR.alloc([S_], BF16, "ma_qn")
            d["E"] = [AR.alloc([512], BF16, f"ma_E{i}") for i in range(4)]
            d["rl"] = AR.alloc([512], F32, "ma_rl")
            d["osb"] = AR.alloc([512], F32, "ma_osb")
            d["ei"] = 0
            return d

        def mem_attn_head(l, hh, qf, sq, kT, vsb, mb, catst):
            g = next_group(4)
            for tb in range(4):
                S.T(lambda e, tb=tb: e.matmul(ps[:, g[tb], :], cm(C_ONESHD), sq.ap[:, tb * 512:(tb + 1) * 512], start=True, stop=True),
                    reads=[sq, cbf], writes=[PB[g[tb]]])
            rstd = mb["rstd"]
            qn = mb["qn"]
            rsqrt_act(rstd.ap.rearrange("p (b n) -> p b n", b=4), psg(g), pbs(g), rstd)
            S.V(lambda e: e.scalar_tensor_tensor(qn.ap, qf, pcol(P_MQ + l), rstd.ap, ALU.mult, ALU.mult),
                reads=[mb["qf_tile"], rstd, params], writes=[qn])
            sc = 128.0 ** -0.5
            if SUB2 <= 1:
                return
            for tb in range(4):
                Es = []
                for mt in range(2):
                    bS = next_bank()
                    S.T(lambda e, bS=bS, mt=mt, tb=tb: e.matmul(ps[:, bS, :], kT.ap[:, hh, mt * 128:(mt + 1) * 128],
                                                               qn.ap[:, tb * 512:(tb + 1) * 512], start=True, stop=True),
                        reads=[kT, qn], writes=[PB[bS]])
                    E = mb["E"][mb["ei"] % 4]
                    mb["ei"] += 1
                    S.A(lambda e, bS=bS, E=E: e.activation(E.ap, ps[:, bS, :], AF.Exp, scale=sc), reads=[PB[bS]], writes=[E])
                    Es.append(E)
                if SUB2 <= 2:
                    continue
                bo = next_bank()
                bl = next_bank()
                for mt in range(2):
                    S.T(lambda e, mt=mt, E=Es[mt], bo=bo: e.matmul(ps[:, bo, :], vsb.ap[:, hh, mt, :], E.ap, start=(mt == 0), stop=(mt == 1)),
                        reads=[vsb, Es[mt]], writes=[PB[bo]])
                for mt in range(2):
                    S.T(lambda e, mt=mt, E=Es[mt], bl=bl: e.matmul(ps[:, bl, :], cm(C_ONES1), E.ap, start=(mt == 0), stop=(mt == 1)),
                        reads=[cbf, Es[mt]], writes=[PB[bl]])
                if SUB2 <= 3:
                    continue
                rl = mb["rl"]
                osb = mb["osb"]
                recip_act(rl.ap, ps[:, bl, :], [PB[bl]], rl)
                S.V(lambda e, tb=tb, bo=bo: e.tensor_tensor(catst.ap[:, tb * 512:(tb + 1) * 512], ps[:, bo, :], rl.ap, ALU.mult),
                    reads=[PB[bo], rl], writes=[catst])
            if SUB2 <= 4:
                return
            S.dma("sp", cat_d[12 + hh], catst.ap, reads=[catst], writes=[CAT[12 + hh]])

        def out_proj(l):
            AR.reset()
            catq = [AR.alloc([4, S_], BF16, f"catq{i}") for i in range(4)]
            xo = [AR.alloc([S_], F32, f"xo{i}") for i in range(2)]
            cv = cat_d.rearrange("c p t -> p c t")
            for i in range(4):
                S.dma("sp", catq[i].ap, cv[:, 4 * i:4 * i + 4, :], reads=CAT[4 * i:4 * i + 4], writes=[catq[i]])

            def evac(oc, g):
                x_o = xo[oc % 2]
                S.dma("sp", x_o.ap, xs[oc * 128:(oc + 1) * 128, :], reads=xs_rows(oc, 0, S_), writes=[x_o])
                S.V(lambda e: e.tensor_tensor(x_o.ap.rearrange("p (b n) -> p b n", b=4), psg(g),
                                              x_o.ap.rearrange("p (b n) -> p b n", b=4), ALU.add),
                    reads=pbs(g) + [x_o], writes=[x_o])
                S.dma("sp", xs[oc * 128:(oc + 1) * 128, :], x_o.ap, reads=[x_o], writes=xs_rows(oc, 0, S_))
                return None

            gemm_fm(lambda oc, kq: w_out[l][:, oc * 128:(oc + 1) * 128], KC, KC,
                    lambda kc: [catq[kc // 4]], lambda kc, tb: catq[kc // 4].ap[:, kc % 4, tb * 512:(tb + 1) * 512], 4, evac)

        def ffn(l, fuse_out=False):
            HT = 1024
            ostc = {"i": 0}
            for half in range(2):
                AR.reset()
                z = [AR.alloc([HT], BF16, f"z{i}") for i in range(64)]
                hh = AR.alloc([KC, HT], BF16, "ffn_h")
                rr = [AR.alloc([HT], F32, f"ffn_r{i}") for i in range(2)]
                xo = [AR.alloc([HT], F32, f"ffn_xo{i}") for i in range(2)]
                ost = [AR.alloc([8, 128], F32, f"ffn_ost{i}") for i in range(2)] if fuse_out else None
                norm_fm(xs, P_FFN + 16 * l, hh, half * HT, HT, 512, 0)

                def evac1(fc, g):
                    r = rr[fc % 2]
                    S.A(lambda e: e.activation(r.ap.rearrange("p (b n) -> p b n", b=2), psg(g), AF.Relu),
                        reads=pbs(g), writes=[r])
                    S.V(lambda e: e.tensor_tensor(z[fc].ap, r.ap, r.ap, ALU.mult), reads=[r], writes=[z[fc]])
                    return None

                for i in range(64):
                    z[i] = AR.at(z[i].off, [HT], BF16, f"z{i}")
                gemm_fm(lambda fc, kq: w_ff1[l][:, fc * 128:(fc + 1) * 128], 64, KC,
                        lambda kc: [hh], lambda kc, tb: hh.ap[:, kc, tb * 512:(tb + 1) * 512], 2, evac1)

                def evac2(dc, g):
                    x_o = xo[dc % 2]
                    S.dma("sp", x_o.ap, xs[dc * 128:(dc + 1) * 128, half * HT:(half + 1) * HT], reads=xs_rows(dc, half * HT, HT), writes=[x_o])
                    S.V(lambda e: e.tensor_tensor(x_o.ap.rearrange("p (b n) -> p b n", b=2), psg(g),
                                                  x_o.ap.rearrange("p (b n) -> p b n", b=2), ALU.add),
                        reads=pbs(g) + [x_o], writes=[x_o])
                    if not fuse_out:
                        S.dma("sp", xs[dc * 128:(dc + 1) * 128, half * HT:(half + 1) * HT], x_o.ap, reads=[x_o], writes=xs_rows(dc, half * HT, HT))
                        return None

                    def tail(half=half):
                        o_t = ost[ostc["i"] % 2]
                        ostc["i"] += 1
                        g2 = next_group(2)
                        for t in range(8):
                            S.T(lambda e, t=t: e.transpose(ps[:, g2[t // 4], (t % 4) * 128:(t % 4 + 1) * 128],
                                                           x_o.ap[:, t * 128:(t + 1) * 128], ident_f),
                                reads=[x_o, params], writes=[PB[g2[t // 4]]])
                        S.A(lambda e: e.copy(o_t.ap[:, 0:4, :], ps[:, g2[0], :].rearrange("p (j d) -> p j d", j=4)),
                            reads=[PB[g2[0]]], writes=[o_t])
                        S.V(lambda e: e.tensor_copy(o_t.ap[:, 4:8, :], ps[:, g2[1], :].rearrange("p (j d) -> p j d", j=4)),
                            reads=[PB[g2[1]]], writes=[o_t])
                        S.dma("sp", out_d[half * HT:(half + 1) * HT, dc * 128:(dc + 1) * 128].rearrange("(t p) d -> p t d", p=128),
                              o_t.ap, reads=[o_t])
                    return tail

                gemm_fm(lambda dc, kq: w_ff2[l][kq * 2048:(kq + 1) * 2048, dc * 128:(dc + 1) * 128], KC, 64,
                        lambda kc: [z[kc]], lambda kc, tb: z[kc].ap[:, tb * 512:(tb + 1) * 512], 2, evac2, use_hooks=fuse_out)

        def mixer0():
            AR.reset()
            h = AR.alloc([KC, S_], BF16, "h0")
            base = AR.top
            kT, vsb = mem_prep(0, base)
            if SUB <= 1:
                return
            base2 = vsb.off + vsb.size
            norm_fm(xs, P_MIX, h, 0, S_, 512, base2)
            if SUB <= 2:
                return
            AR.reset(base2)
            PADW = S_ + 16
            upad = [AR.alloc([PADW], F32, f"upad{i}") for i in range(2)]
            sb = [AR.alloc([PADW], F32, f"spp{i}") for i in range(2)]
            pooled = [AR.alloc([S_], BF16, f"pooled{i}") for i in range(6)]
            catst = [AR.alloc([S_], BF16, f"catst{i}") for i in range(2)]
            tmp16 = AR.alloc([16], F32, "tmp16")
            sq = [AR.alloc([S_], BF16, f"msq{i}") for i in range(2)]
            mb = mem_attn_bufs()
            pw_tiles = {}
            for t in upad + sb:
                S.V(lambda e, t=t: e.memset(t.ap[:, 0:16], 0.0), writes=[t])
            cst = {"i": 0}

            def token_out(gi):
                for oc2 in range(3):
                    wt = load_w(pool_w[gi, :, oc2 * 128:(oc2 + 1) * 128], 3)
                    g = next_group(4)
                    for ic in range(3):
                        pl = pooled[(gi * 3 + ic) % 6]
                        for tb in range(4):
                            S.T(lambda e, b=g[tb], ic=ic, tb=tb, wt=wt, pl=pl: e.matmul(
                                ps[:, b, :], wt.ap[:, ic, :], pl.ap[:, tb * 512:(tb + 1) * 512], start=(ic == 0), stop=(ic == 2)),
                                reads=[wt, pl], writes=[PB[g[tb]]])
                    cs_ = catst[cst["i"] % 2]
                    cst["i"] += 1
                    oc = gi * 3 + oc2
                    S.A(lambda e, cs_=cs_, g=g, oc=oc: e.activation(cs_.ap.rearrange("p (b n) -> p b n", b=4), psg(g),
                                                                    AF.Identity, scale=pcol(P_PSC + oc)),
                        reads=pbs(g) + [params], writes=[cs_])
                    S.dma("sp", cat_d[oc], cs_.ap, reads=[cs_], writes=[CAT[oc]])

            def evac(oc, g):
                up = upad[oc % 2]
                S.A(lambda e: e.copy(up.ap[:, 16:PADW].rearrange("p (b n) -> p b n", b=4), psg(g)), reads=pbs(g), writes=[up])
                if SUB <= 3:
                    return None
                if oc < 12:
                    gi = oc // 3
                    w = 2 << gi
                    src = up
                    for step in range(gi + 1):
                        sh = 1 << step
                        dst = sb[step % 2]
                        S.V(lambda e, src=src, dst=dst, sh=sh: e.tensor_tensor(
                            dst.ap[:, 16:PADW], src.ap[:, 16:PADW], src.ap[:, 16 - sh:PADW - sh], ALU.add),
                            reads=[src], writes=[dst])
                        src = dst
                    pl = pooled[oc % 6]
                    S.V(lambda e, src=src, pl=pl: e.scalar_tensor_tensor(
                        pl.ap, src.ap[:, 16:PADW], 1.0 / w, up.ap[:, 16:PADW], ALU.mult, ALU.subtract),
                        reads=[src, up], writes=[pl])
                    S.V(lambda e, src=src: e.tensor_tensor(tmp16.ap, src.ap[:, 16:32], pcol(P_INVC + 16 * gi, 16), ALU.mult),
                        reads=[src, params], writes=[tmp16])
                    S.V(lambda e, pl=pl: e.tensor_tensor(pl.ap[:, 0:16], tmp16.ap, up.ap[:, 16:32], ALU.subtract),
                        reads=[tmp16, up], writes=[pl])
                    if oc % 3 == 2 and SUB >= 5:
                        return lambda: token_out(gi)
                    return None
                else:
                    if SUB <= 5:
                        return None
                    hh = oc - 12
                    s_q = sq[oc % 2]
                    S.A(lambda e: e.activation(s_q.ap.rearrange("p (b n) -> p b n", b=4), psg(g), AF.Square),
                        reads=pbs(g), writes=[s_q])
                    cs_ = catst[cst["i"] % 2]
                    cst["i"] += 1

                    def post():
                        mb["qf_tile"] = up
                        mem_attn_head(0, hh, up.ap[:, 16:PADW], s_q, kT, vsb, mb, cs_)
                    return post

            gemm_fm(lambda oc, kq: w_in[0][:, oc * 128:(oc + 1) * 128], KC, KC,
                    lambda kc: [h], lambda kc, tb: h.ap[:, kc, tb * 512:(tb + 1) * 512], 4, evac)

        def hnr_bufs():
            d = {}
            d["cs"] = AR.alloc([2, S_], F32, "ropecs")
            S.dma("sp", d["cs"].ap, cs_d, writes=[d["cs"]])
            d["kf"] = [AR.alloc([S_], F32, f"kf{i}") for i in range(3)]
            d["sq"] = [AR.alloc([S_], BF16, f"ksq{i}") for i in range(2)]
            d["rstd"] = AR.alloc([S_], F32, "krstd")
            d["knb"] = [AR.alloc([S_], BF16, f"knb{i}") for i in range(2)]
            d["t1"] = AR.alloc([S_], F32, "kt1")
            d["t2"] = AR.alloc([S_], F32, "kt2")
            d["ob"] = [AR.alloc([S_], BF16, f"kob{i}") for i in range(2)]
            d["i"] = 0
            return d

        def hnr_evac(hb, g, gcol, dst, dtile):
            i = hb["i"]
            hb["i"] += 1
            kf = hb["kf"][i % 3]
            sq = hb["sq"][i % 2]
            ob = hb["ob"][i % 2]
            rstd, knb, t1, t2, cs = hb["rstd"], hb["knb"][i % 2], hb["t1"], hb["t2"], hb["cs"]
            v4 = lambda t: t.ap.rearrange("p (b n) -> p b n", b=4)
            v2 = lambda t, h: t.ap[:, h * 1024:(h + 1) * 1024].rearrange("p (b n) -> p b n", b=2)
            for h_ in range(2):
                ga = g[2 * h_:2 * h_ + 2]
                gd = g[2 * (1 - h_):2 * (1 - h_) + 2]
                S.A(lambda e, h_=h_, ga=ga: e.activation(v2(sq, h_), psg(ga), AF.Square), reads=pbs(ga), writes=[sq])
                S.V(lambda e, h_=h_, gd=gd: e.tensor_copy(v2(kf, 1 - h_), psg(gd)), reads=pbs(gd), writes=[kf])

            def postA():
                g2 = next_group(4)
                for tb in range(4):
                    S.T(lambda e, tb=tb: e.matmul(ps[:, g2[tb], :], cm(C_BLK), sq.ap[:, tb * 512:(tb + 1) * 512], start=True, stop=True),
                        reads=[sq, cbf], writes=[PB[g2[tb]]])
                rsqrt_act(v4(rstd), psg(g2), pbs(g2), rstd)
                S.V(lambda e: e.scalar_tensor_tensor(kf.ap, kf.ap, pcol(gcol), rstd.ap, ALU.mult, ALU.mult),
                    reads=[kf, rstd, params], writes=[kf])
                S.A(lambda e: e.copy(knb.ap, kf.ap), reads=[kf], writes=[knb])

            def postB():
                g3 = next_group(4)
                for tb in range(4):
                    S.T(lambda e, tb=tb: e.matmul(ps[:, g3[tb], :], cm(C_RT), knb.ap[:, tb * 512:(tb + 1) * 512], start=True, stop=True),
                        reads=[knb, cbf], writes=[PB[g3[tb]]])
                S.G(lambda e: e.tensor_tensor(t1.ap, kf.ap, cs.ap[:, 0, :], ALU.mult), reads=[kf, cs], writes=[t1])
                S.V(lambda e: e.tensor_tensor(v4(t2), psg(g3), cs.ap[:, 1, :].rearrange("p (b n) -> p b n", b=4), ALU.mult),
                    reads=pbs(g3) + [cs], writes=[t2])
                S.V(lambda e: e.tensor_tensor(ob.ap, t1.ap, t2.ap, ALU.add), reads=[t1, t2], writes=[ob])
                S.dma("sp", dst, ob.ap, reads=[ob], writes=[dtile])
            return [postA, postB]

        def kv_phase():
            AR.reset()
            h = AR.alloc([KC, S_], BF16, "hkv")
            base = AR.top
            norm_fm(xs, P_KVN, h, 0, S_, 512, base)
            AR.reset(base)
            hb = hnr_bufs()
            vst = [AR.alloc([S_], BF16, f"vst{i}") for i in range(2)]

            def evac(oc, g):
                if oc < 12:
                    return hnr_evac(hb, g, P_KN, kT_d[oc], KT[oc])
                v_s = vst[oc % 2]
                S.A(lambda e: e.copy(v_s.ap.rearrange("p (b n) -> p b n", b=4), psg(g)), reads=pbs(g), writes=[v_s])
                S.dma("sp", vT_d[oc - 12], v_s.ap, reads=[v_s], writes=[VT[oc - 12]])
                return None

            gemm_fm(lambda oc, kq: w_kv[:, oc * 128:(oc + 1) * 128], 24, KC,
                    lambda kc: [h], lambda kc, tb: h.ap[:, kc, tb * 512:(tb + 1) * 512], 4, evac, use_hooks=True)

        def q_phase():
            AR.reset()
            h = AR.alloc([KC, S_], BF16, "hq")
            base = AR.top
            kT, vsb = mem_prep(1, base)
            base2 = vsb.off + vsb.size
            norm_fm(xs, P_MIX + 16, h, 0, S_, 512, base2)
            AR.reset(base2)
            hb = hnr_bufs()
            qf = hb["kf"]
            sq = hb["sq"]
            catst = [AR.alloc([S_], BF16, f"qcatst{i}") for i in range(2)]
            mb = mem_attn_bufs(hb["rstd"])

            def evac(oc, g):
                if oc < 12:
                    return hnr_evac(hb, g, P_QN, qT_d[oc], QT[oc])
                hh = oc - 12
                q_f = qf[hb["i"] % 3]
                s_q = sq[hb["i"] % 2]
                hb["i"] += 1
                cs_ = catst[oc % 2]
                S.A(lambda e: e.copy(q_f.ap.rearrange("p (b n) -> p b n", b=4), psg(g)), reads=pbs(g), writes=[q_f])
                S.A(lambda e: e.activation(s_q.ap.rearrange("p (b n) -> p b n", b=4), psg(g), AF.Square), reads=pbs(g), writes=[s_q])

                def post():
                    mb["qf_tile"] = q_f
                    mem_attn_head(1, hh, q_f.ap, s_q, kT, vsb, mb, cs_)
                return post

            gemm_fm(lambda oc, kq: w_in[1][:, oc * 128:(oc + 1) * 128], KC, KC,
                    lambda kc: [h], lambda kc, tb: h.ap[:, kc, tb * 512:(tb + 1) * 512], 4, evac, use_hooks=True)

        def attn_phase():
            AR.reset()
            kk = [[AR.alloc([S_], BF16, f"kk{s}_{i}") for i in range(2)] for s in range(2)]
            zq = [[AR.alloc([S_], BF16, f"zq{m}_{hf}") for hf in range(2)] for m in range(2)]
            for m in range(2):
                for hf in range(2):
                    z0 = 64 * (1 - hf)
                    S.V(lambda e, m=m, hf=hf, z0=z0: e.memset(zq[m][hf].ap[z0:z0 + 64, :], 0.0), writes=[zq[m][hf]])
            vT = [AR.alloc([S_], BF16, f"vT{i}") for i in range(2)]
            vtok = [AR.alloc([16, 128], BF16, f"vtok{i}") for i in range(2)]
            E = [AR.alloc([512], BF16, f"E{i}") for i in range(8)]
            r12 = AR.alloc([1024], F32, "r12")
            aa = AR.alloc([512], F32, "aa")
            bb = AR.alloc([512], F32, "bb")
            oo = AR.alloc([512], F32, "oo")
            osq = AR.alloc([512], BF16, "osq")
            rs = AR.alloc([512], F32, "ars")
            catst = [AR.alloc([S_], BF16, f"acat{i}") for i in range(2)]
            sbank = {"i": 0}
            spair = {"i": 0}
            ei = {"i": 0}
            sc = 64.0 ** -0.5
            BO = (4, 5)
            BL = (6, 7)

            def sb_next():
                b = sbank["i"] % 4
                sbank["i"] += 1
                return b

            for hp in range(6):
                k2t = kk[hp % 2]
                S.dma("sp", k2t[0].ap, kT_d[hp], reads=[KT[hp]], writes=[k2t[0]])
                S.dma("sp", k2t[1].ap, kT_d[6 + hp], reads=[KT[6 + hp]], writes=[k2t[1]])
                for m in range(2):
                    for hf in range(2):
                        q0 = 64 * hf
                        S.dma("sp", zq[m][hf].ap[q0:q0 + 64, :], qT_d[6 * m + hp][q0:q0 + 64, :],
                              reads=[QT[6 * m + hp]], writes=[zq[m][hf]])
                for half in range(2):
                    hd = 2 * hp + half
                    p0 = 64 * half
                    v_T = vT[hd % 2]
                    v_k = vtok[hd % 2]
                    S.dma("sp", v_T.ap, vT_d[hd], reads=[VT[hd]], writes=[v_T])
                    for t4i in range(4):
                        b = sb_next()
                        pb16 = ps[:, b, :].bitcast(BF16)
                        for j in range(4):
                            tt = t4i * 4 + j
                            S.T(lambda e, pb16=pb16, j=j, tt=tt, v_T=v_T: e.transpose(
                                pb16[:, j * 128:(j + 1) * 128], v_T.ap[:, tt * 128:(tt + 1) * 128], cm(C_IDENT)),
                                reads=[v_T, cbf], writes=[PB[b]])
                        S.V(lambda e, pb16=pb16, t4i=t4i, v_k=v_k: e.tensor_copy(
                            v_k.ap[:, t4i * 4:(t4i + 1) * 4, :], pb16[:, 0:512].rearrange("p (j d) -> p j d", j=4)),
                            reads=[PB[b]], writes=[v_k])
                    cs_ = catst[hd % 2]
                    DEPTH = 2
                    queue = []
                    tails = []

                    def issue_S(qb, kb, m, half=half, k2t=k2t):
                        nkb = 4 * qb + 4
                        i_d = kb - 4 * qb
                        c0 = 128 * i_d if i_d > 0 else 0
                        qt = zq[m][half]
                        kt = k2t[m]
                        bS = sb_next()
                        diag = i_d >= 0
                        S.T(lambda e: e.matmul(ps[:, bS, c0:512], kt.ap[:, kb * 128:(kb + 1) * 128],
                                               qt.ap[:, qb * 512 + c0:(qb + 1) * 512], start=True, stop=(not diag)),
                            reads=[kt, qt], writes=[PB[bS]])
                        if diag:
                            S.T(lambda e: e.matmul(ps[:, bS, c0:c0 + 128], cm(C_IDENT), cm(C_TRIB), start=False, stop=True),
                                reads=[cbf], writes=[PB[bS]])
                        Et = E[ei["i"] % len(E)]
                        ei["i"] += 1
                        S.A(lambda e: e.activation(Et.ap[:, c0:512], ps[:, bS, c0:512], AF.Exp, scale=sc),
                            reads=[PB[bS]], writes=[Et])
                        return (qb, kb, m, c0, nkb, Et)

                    def post_a(qb):
                        S.V(lambda e: e.tensor_copy(aa.ap, ps[:, BO[0], :]), reads=[PB[BO[0]]], writes=[aa])
                        S.V(lambda e: e.tensor_copy(bb.ap, ps[:, BO[1], :]), reads=[PB[BO[1]]], writes=[bb])
                        recip_act(r12.ap.rearrange("p (b n) -> p b n", b=2), ps[:, BL[0]:BL[1] + 1, :], [PB[BL[0]], PB[BL[1]]], r12)
                        S.V(lambda e: e.tensor_tensor(aa.ap, aa.ap, r12.ap[:, 0:512], ALU.mult), reads=[aa, r12], writes=[aa])
                        S.V(lambda e: e.tensor_tensor(bb.ap, bb.ap, r12.ap[:, 512:1024], ALU.mult), reads=[bb, r12], writes=[bb])
                        S.V(lambda e: e.scalar_tensor_tensor(oo.ap, bb.ap, NEGLAM, aa.ap, ALU.mult, ALU.add),
                            reads=[bb, aa, small], writes=[oo])
                        S.V(lambda e: e.tensor_tensor(osq.ap, oo.ap, oo.ap, ALU.mult), reads=[oo], writes=[osq])

                    def post_b(qb, cs_=cs_):
                        bs2 = sb_next()
                        S.T(lambda e: e.matmul(ps[:, bs2, :], cm(C_ONESHD), osq.ap, start=True, stop=True),
                            reads=[osq, cbf], writes=[PB[bs2]])
                        rsqrt_act(rs.ap, ps[:, bs2, :], [PB[bs2]], rs)
                        S.V(lambda e: e.scalar_tensor_tensor(cs_.ap[:, qb * 512:(qb + 1) * 512], oo.ap, SUBG, rs.ap, ALU.mult, ALU.mult),
                            reads=[oo, rs, small], writes=[cs_])

                    def issue_PV(info, v_k=v_k):
                        qb, kb, m, c0, nkb, Et = info
                        S.T(lambda e: e.matmul(ps[:, BO[m], c0:512], v_k.ap[:, kb, :], Et.ap[:, c0:512],
                                               start=(kb == 0), stop=(kb == nkb - 1)),
                            reads=[v_k, Et], writes=[PB[BO[m]]])
                        S.T(lambda e: e.matmul(ps[:, BL[m], c0:512], cm(C_ONES1), Et.ap[:, c0:512],
                                               start=(kb == 0), stop=(kb == nkb - 1)),
                            reads=[cbf, Et], writes=[PB[BL[m]]])
                        for t in tails:
                            t[0] -= 1
                        while tails and tails[0][0] <= 0:
                            tails.pop(0)[1]()
                        if kb == nkb - 1 and m == 1:
                            post_a(qb)
                            tails.append([5, lambda qb=qb: post_b(qb)])

                    steps = [(qb, kb, m) for qb in range(4) for kb in range(4 * qb + 4) for m in range(2)]
                    for st_ in steps:
                        queue.append(issue_S(*st_))
                        if len(queue) > DEPTH:
                            issue_PV(queue.pop(0))
                    while queue:
                        issue_PV(queue.pop(0))
                    while tails:
                        tails.pop(0)[1]()
                    S.dma("sp", cat_d[hd], cs_.ap, reads=[cs_], writes=[CAT[hd]])

        S.dma("sp", params.ap, params_d, writes=[params])
        S.dma("sp", cbf.ap, cbf_d.rearrange("p (a b) -> p a b", a=NCB), writes=[cbf])
        S.V(lambda e: e.memset(EPSC, EPS), writes=[small])
        AR.reset()
        lt = AR.alloc([64], F32, "lam_tmp")
        l4 = AR.alloc([4], F32, "lam4")
        dl = lambda i: params.ap[:, P_DL + 64 * i:P_DL + 64 * (i + 1)]
        S.V(lambda e: e.tensor_tensor(lt.ap, dl(0), dl(1), ALU.mult), reads=[params], writes=[lt])
        S.V(lambda e: e.reduce_sum(l4.ap[:, 0:1], lt.ap, axis=AX.X), reads=[lt], writes=[l4])
        S.V(lambda e: e.tensor_tensor(lt.ap, dl(2), dl(3), ALU.mult), reads=[params, l4], writes=[lt])
        S.V(lambda e: e.reduce_sum(l4.ap[:, 1:2], lt.ap, axis=AX.X), reads=[lt], writes=[l4])
        S.A(lambda e: e.activation(l4.ap[:, 2:4], l4.ap[:, 0:2], AF.Exp), reads=[l4], writes=[l4])
        S.V(lambda e: e.tensor_tensor(NEGLAM, l4.ap[:, 3:4], l4.ap[:, 2:3], ALU.subtract), reads=[l4], writes=[small])
        S.V(lambda e: e.tensor_scalar_add(NEGLAM, NEGLAM, -LAM_INIT), reads=[small], writes=[small])
        S.V(lambda e: e.tensor_scalar_mul(SUBG, pcol(P_SUB), 1.0 - LAM_INIT), reads=[params], writes=[small])

        phases = [
            lambda: transpose_in(x_in, xs, S_, lambda dc, tb: [XS[dc][tb]]),
            lambda: transpose_in(mem_in, memT, MEM, lambda dc, tb: [MEMT]),
            mixer0,
            lambda: out_proj(0),
            lambda: ffn(0),
            kv_phase,
            q_phase,
            attn_phase,
            lambda: out_proj(1),
            lambda: ffn(1, fuse_out=True),
        ]
        for i, ph in enumerate(phases):
            if i > stop_after:
                break
            ph()
        if stop_after < len(phases) - 1:
            transpose_out()

        with nc.Block() as block:
            S.emit(block, engsem, rings)
    return nc


_CACHE = {}


def kernel(**inputs):
    stop_after = int(os.environ.get("MK_STOP", "99"))
    dbg = os.environ.get("MK_DBG", "0") == "1"
    inp = {k: np.asarray(v) for k, v in inputs.items()}
    key = (stop_after, dbg)
    if key not in _CACHE:
        _CACHE[key] = build_nc(stop_after, dbg)
    nc = _CACHE[key]
    params = pack_params(inp)
    cbf = const_bf16()
    cs = rope_cs()
    shared = {
        "pool_w": np.ascontiguousarray(inp["pool_w"][0], np.float32),
        "w_kv": np.ascontiguousarray(inp["w_kv"], np.float32),
        "params": params, "cbf": cbf, "ropecs": cs,
    }
    for l in range(2):
        shared[f"w_in{l}"] = np.ascontiguousarray(inp["w_in"][l], np.float32)
        shared[f"w_out{l}"] = np.ascontiguousarray(inp["w_out"][l], np.float32)
        shared[f"w_mkv{l}"] = np.ascontiguousarray(inp["w_mem_kv"][l], np.float32)
        shared[f"w_ff1{l}"] = np.ascontiguousarray(inp["w_ff1"][l], np.float32)
        shared[f"w_ff2{l}"] = np.ascontiguousarray(inp["w_ff2"][l], np.float32)
    in_maps = []
    for b in range(8):
        m = dict(shared)
        m["x"] = np.ascontiguousarray(inp["x"][b], np.float32)
        m["mem"] = np.ascontiguousarray(inp["mem"][b], np.float32)
        in_maps.append(m)
    ncores = int(os.environ.get("MK_CORES", "8"))
    res = run_bass_kernel_spmd(nc, in_maps[:ncores], core_ids=list(range(ncores)))
    if dbg:
        kernel.last_results = res.results
    out = np.stack([np.asarray(r["out"], np.float32) for r in res.results], axis=0)
    return out
```

```python
import math
import os
from contextlib import ExitStack

import ml_dtypes
import numpy as np

import concourse.bass as bass
import concourse.mybir as mybir
from concourse.bass_utils import run_bass_kernel_spmd

F32 = mybir.dt.float32
BF16 = mybir.dt.bfloat16
U8 = mybir.dt.uint8
AF = mybir.ActivationFunctionType
ALU = mybir.AluOpType
AX = mybir.AxisListType

S_ = 2048
D = 2048
KC = 16
MEM = 256
EPS = 1e-6
LAM_INIT = 0.8 - 0.6 * math.exp(-0.3 * 1)
ROPE_THETA = 500000.0


class Tile:
    __slots__ = ("ap", "w", "r", "rd", "name", "off", "size", "excl")

    def __init__(self, ap, name="", off=-1, size=0):
        self.excl = False
        self.ap = ap
        self.w = None
        self.r = {}
        self.rd = []
        self.name = name
        self.off = off
        self.size = size


class Op:
    __slots__ = ("eng", "fn", "deps", "marked", "semval", "is_dma", "slot", "dval", "idx")

    def __init__(self, eng, fn, is_dma=False):
        self.eng = eng
        self.fn = fn
        self.deps = set()
        self.marked = False
        self.semval = 0
        self.is_dma = is_dma
        self.slot = -1
        self.dval = 0
        self.idx = 0


ENGS = ("pe", "act", "dve", "pool", "sp")
RING = {"sp": 16, "pool": 8}


class Sched:
    def __init__(self):
        self.ops = {e: [] for e in ENGS}
        self.dma_count = {q: 0 for q in RING}
        self.dma_last = {q: [None] * RING[q] for q in RING}

    def add(self, eng, fn, reads=(), writes=(), dma=False):
        o = Op(eng, fn, dma)
        deps = o.deps
        for t in reads:
            if t.w is not None:
                deps.add(t.w)
            if t.excl:
                for en, ro in t.r.items():
                    if en != eng:
                        deps.add(ro)
        for t in writes:
            if t.w is not None:
                deps.add(t.w)
            deps.update(t.r.values())
            deps.update(t.rd)
        for t in reads:
            if dma:
                t.rd.append(o)
            else:
                t.r[eng] = o
        for t in writes:
            t.w = o
            t.r = {}
            t.rd = []
        if dma:
            k = self.dma_count[eng]
            R = RING[eng]
            o.slot = k % R
            o.dval = 16 * (k // R + 1)
            prev = self.dma_last[eng][o.slot]
            if prev is not None:
                deps.add(prev)
            self.dma_last[eng][o.slot] = o
            self.dma_count[eng] = k + 1
        if eng == "pe" and not dma:
            o.deps = {d for d in deps if d.is_dma or d.eng != "pe"}
        o.deps.discard(o)
        o.idx = len(self.ops[eng])
        self.ops[eng].append(o)
        return o

    def T(self, fn, reads=(), writes=()):
        return self.add("pe", fn, reads, writes)

    def A(self, fn, reads=(), writes=()):
        return self.add("act", fn, reads, writes)

    def V(self, fn, reads=(), writes=()):
        return self.add("dve", fn, reads, writes)

    def G(self, fn, reads=(), writes=()):
        return self.add("pool", fn, reads, writes)

    def dma(self, q, out_ap, in_ap, reads=(), writes=()):
        return self.add(q, lambda e: e.dma_start(out=out_ap, in_=in_ap), reads, writes, dma=True)

    def finalize(self):
        for e in ENGS:
            for o in self.ops[e]:
                for d in o.deps:
                    if not d.is_dma:
                        d.marked = True
        for e in ENGS:
            c = 0
            for o in self.ops[e]:
                if (not o.is_dma) and o.marked:
                    c += 1
                    o.semval = c

    def emit(self, block, engsem, rings):
        self.finalize()
        sched = self

        def run(ename, e):
            seen = {}
            for o in sched.ops[ename]:
                for d in sorted(o.deps, key=lambda d: (d.eng, d.idx)):
                    if d.is_dma:
                        key = (d.eng, d.slot)
                        sem = rings[d.eng][d.slot]
                        val = d.dval
                    else:
                        key = d.eng
                        sem = engsem[d.eng]
                        val = d.semval
                    if seen.get(key, 0) >= val:
                        continue
                    seen[key] = val
                    e.wait_ge(sem, val)
                ins = o.fn(e)
                if o.is_dma:
                    ins.then_inc(rings[ename][o.slot], 16)
                elif o.marked:
                    ins.then_inc(engsem[ename], 1)
            if ename in RING:
                for s, last in enumerate(sched.dma_last[ename]):
                    if last is not None and seen.get((ename, s), 0) < last.dval:
                        e.wait_ge(rings[ename][s], last.dval)

        @block.tensor
        def _(e):
            run("pe", e)

        @block.scalar
        def _(e):
            run("act", e)

        @block.vector
        def _(e):
            run("dve", e)

        @block.gpsimd
        def _(e):
            run("pool", e)

        @block.sync
        def _(e):
            run("sp", e)


class Arena:
    def __init__(self, ap, nbytes, sched):
        self.ap = ap
        self.nbytes = nbytes
        self.live = []
        self.S = sched
        self.top = 0

    def reset(self, top=0):
        self.top = top

    def alloc(self, shape, dt, name=""):
        isz = 4 if dt == F32 else 2
        n = int(np.prod(shape)) * isz
        off = (self.top + 31) // 32 * 32
        assert off + n <= self.nbytes, (name, off, n, self.nbytes)
        self.top = off + n
        return self.at(off, shape, dt, name)

    def at(self, off, shape, dt, name=""):
        isz = 4 if dt == F32 else 2
        n = int(np.prod(shape)) * isz
        assert off + n <= self.nbytes, (name, off, n, self.nbytes)
        ap = self.ap[:, off:off + n].bitcast(dt)
        if len(shape) == 2:
            ap = ap.rearrange("p (a b) -> p a b", a=shape[0])
        elif len(shape) == 3:
            ap = ap.rearrange("p (a b c) -> p a b c", a=shape[0], b=shape[1])
        t = Tile(ap, name, off, n)
        keep = []
        ops = self.S.ops
        for o in self.live:
            if o.off < off + n and off < o.off + o.size:
                cands = list(o.r.values())
                if o.w is not None:
                    if o.w.is_dma:
                        t.rd.append(o.w)
                    else:
                        cands.append(o.w)
                for c in cands:
                    cur = t.r.get(c.eng)
                    if cur is None or cur.idx < c.idx:
                        t.r[c.eng] = c
                t.rd.extend(o.rd)
            else:
                keep.append(o)
        keep.append(t)
        self.live = keep
        return t


P_MIX = 0
P_FFN = 32
P_MEMN = 64
P_KVN = 96
P_PSC = 112
P_MQ = 124
P_MK = 126
P_KN = 128
P_QN = 129
P_SUB = 130
P_DL = 131
P_INVC = 387
P_IDENT = 451
PC = 579

C_IDENT, C_ONESD, C_ONESHD, C_ONES1, C_BLK, C_RT, C_TRI, C_TRIB = range(8)
NCB = 8


def _vec16(v):
    return np.asarray(v, np.float32).reshape(16, 128).T


def pack_params(inp):
    p = np.zeros((128, PC), np.float32)
    for l in range(2):
        p[:, P_MIX + 16 * l:P_MIX + 16 * l + 16] = _vec16(inp["mix_norm"][l])
        p[:, P_FFN + 16 * l:P_FFN + 16 * l + 16] = _vec16(inp["ffn_norm"][l])
        p[:, P_MEMN + 16 * l:P_MEMN + 16 * l + 16] = _vec16(inp["mem_norm"][l])
        p[:, P_MQ + l] = inp["mem_q_norm"][l]
        p[:, P_MK + l] = inp["mem_k_norm"][l]
    p[:, P_KVN:P_KVN + 16] = _vec16(inp["kv_norm"])
    p[:, P_PSC:P_PSC + 12] = np.asarray(inp["pool_scale"][0], np.float32).reshape(12, 128).T
    p[:, P_KN] = np.tile(np.asarray(inp["k_norm"], np.float32), 2)
    p[:, P_QN] = np.tile(np.asarray(inp["q_norm"][0], np.float32), 2)
    p[:, P_SUB] = inp["subln_norm"][0]
    p[:, P_DL:P_DL + 256] = np.asarray(inp["diff_lambda"][0], np.float32).reshape(1, 256)
    for g, w in enumerate((2, 4, 8, 16)):
        t = np.arange(16)
        p[:, P_INVC + 16 * g:P_INVC + 16 * g + 16] = (1.0 / np.minimum(t + 1, w)).astype(np.float32)[None, :]
    p[:, P_IDENT:P_IDENT + 128] = np.eye(128, dtype=np.float32)
    return p


def const_bf16():
    c = np.zeros((128, NCB, 128), np.float32)
    c[:, C_IDENT] = np.eye(128)
    c[:, C_ONESD] = 1.0 / 2048.0
    c[:, C_ONESHD] = 1.0 / 128.0
    c[:, C_ONES1] = 1.0
    blk = np.zeros((128, 128))
    blk[:64, :64] = 1.0 / 64.0
    blk[64:, 64:] = 1.0 / 64.0
    c[:, C_BLK] = blk
    Rm = np.zeros((128, 128))
    for hb in (0, 64):
        for j in range(8):
            Rm[hb + j, hb + j + 8] = -1.0
            Rm[hb + j + 8, hb + j] = 1.0
    c[:, C_RT] = Rm.T
    k = np.arange(128)[:, None]
    q = np.arange(128)[None, :]
    c[:, C_TRI] = (k <= q).astype(np.float32)
    c[:, C_TRIB] = np.where(k <= q, 0.0, -30000.0)
    return c.reshape(128, NCB * 128).astype(ml_dtypes.bfloat16)


def rope_cs():
    pos = np.arange(S_, dtype=np.float32)
    inv = (np.float32(ROPE_THETA) ** (-(np.arange(8, dtype=np.float32) * np.float32(2.0)) / np.float32(16))).astype(np.float32)
    ang = (pos[:, None] * inv[None, :]).astype(np.float32)
    cs = np.zeros((128, 2, S_), np.float32)
    cs[:, 0, :] = 1.0
    for hb in (0, 64):
        for j in range(16):
            cs[hb + j, 0, :] = np.cos(ang[:, j % 8])
            cs[hb + j, 1, :] = np.sin(ang[:, j % 8])
    return cs


ARENA_BYTES = 187 * 1024
SUB = int(os.environ.get('MK_SUB', '99'))
SUB2 = int(os.environ.get('MK_SUB2', '99'))
NBW = 4


def build_nc(stop_after=99, dbg=False):
    nc = bass.Bass("TRN2", target_bir_lowering=False)
    okind = "ExternalOutput" if dbg else "Internal"

    def din(name, shape, dt=F32):
        return nc.dram_tensor(name, list(shape), dt, kind="ExternalInput").ap()

    x_in = din("x", [S_, D])
    mem_in = din("mem", [MEM, D])
    w_in = [din(f"w_in{l}", [D, D]) for l in range(2)]
    w_out = [din(f"w_out{l}", [D, D]) for l in range(2)]
    w_mkv = [din(f"w_mkv{l}", [D, 1024]) for l in range(2)]
    w_ff1 = [din(f"w_ff1{l}", [D, 4 * D]) for l in range(2)]
    w_ff2 = [din(f"w_ff2{l}", [4 * D, D]) for l in range(2)]
    pool_w = din("pool_w", [4, 384, 384])
    w_kv = din("w_kv", [D, 3072])
    params_d = din("params", [128, PC])
    cbf_d = din("cbf", [128, NCB * 128], BF16)
    cs_d = din("ropecs", [128, 2, S_])
    out_d = nc.dram_tensor("out", [S_, D], F32, kind="ExternalOutput").ap()

    xs = nc.dram_tensor("xs", [D, S_], F32, kind=okind).ap()
    memT = nc.dram_tensor("memT", [D, MEM], F32, kind=okind).ap()
    dbg_b = nc.dram_tensor("dbg_b", [128, 2048], BF16, kind=okind).ap()
    dbg_h = nc.dram_tensor("dbg_h", [128, 4096], BF16, kind=okind).ap()
    cat_d = nc.dram_tensor("cat_d", [16, 128, S_], BF16, kind=okind).ap()
    kT_d = nc.dram_tensor("kT_d", [12, 128, S_], BF16, kind=okind).ap()
    qT_d = nc.dram_tensor("qT_d", [12, 128, S_], BF16, kind=okind).ap()
    vT_d = nc.dram_tensor("vT_d", [12, 128, S_], BF16, kind=okind).ap()

    S = Sched()
    with ExitStack() as ctx:
        engsem = {e: ctx.enter_context(nc.semaphore("sem_" + e)) for e in ENGS}
        rings = {q: [ctx.enter_context(nc.semaphore(f"r_{q}_{i}")) for i in range(RING[q])] for q in RING}
        params_t = ctx.enter_context(nc.sbuf_tensor("sb_params", [128, PC], F32))
        cbf_t = ctx.enter_context(nc.sbuf_tensor("sb_cbf", [128, NCB, 128], BF16))
        small_t = ctx.enter_context(nc.sbuf_tensor("sb_small", [128, 16], F32))
        wring_t = ctx.enter_context(nc.sbuf_tensor("sb_wring", [128, NBW, 16, 128], BF16))
        arena_t = ctx.enter_context(nc.sbuf_tensor("sb_arena", [128, ARENA_BYTES], U8))
        ps_t = ctx.enter_context(nc.psum_tensor("ps_all", [128, 8, 512], F32))
        ps = ps_t[:, :, :]
        PB = [Tile(ps[:, b, :], f"ps{b}") for b in range(8)]
        for t_ in PB:
            t_.excl = True
        params = Tile(params_t[:, :], "params")
        cbf = Tile(cbf_t[:, :, :], "cbf")
        small = Tile(small_t[:, :], "small")
        wring = [Tile(wring_t[:, i, :, :], f"w{i}") for i in range(NBW)]
        AR = Arena(arena_t[:, :], ARENA_BYTES, S)
        XS = [[Tile(None, f"xs{dc}_{tb}") for tb in range(4)] for dc in range(KC)]
        MEMT = Tile(None, "memT")
        CAT = [Tile(None, f"cat{i}") for i in range(16)]
        KT = [Tile(None, f"kT{i}") for i in range(12)]
        QT = [Tile(None, f"qT{i}") for i in range(12)]
        VT = [Tile(None, f"vT{i}") for i in range(12)]

        def xs_cols(c0, n):
            return [XS[dc][tb] for dc in range(KC) for tb in range(c0 // 512, (c0 + n + 511) // 512)]

        def xs_rows(dc, c0, n):
            return [XS[dc][tb] for tb in range(c0 // 512, (c0 + n + 511) // 512)]
        st = {"w": 0, "bank": 0}

        def pcol(c, n=1):
            return params.ap[:, c:c + n]

        def cm(i):
            return cbf.ap[:, i, :]

        ident_f = params.ap[:, P_IDENT:P_IDENT + 128]
        EPSC = small.ap[:, 0:1]
        NEGLAM = small.ap[:, 1:2]
        SUBG = small.ap[:, 2:3]

        def next_group(n):
            res = st.get("reserved", ())
            for _ in range(9):
                b = (st["bank"] + n - 1) // n * n
                if b + n > 8:
                    b = 0
                st["bank"] = b + n
                if not any((x in res) for x in range(b, b + n)):
                    return list(range(b, b + n))
            raise AssertionError("no free PSUM group")

        def next_bank():
            return next_group(1)[0]

        def psg(g):
            return ps[:, g[0]:g[0] + len(g), :]

        def pbs(g):
            return [PB[b] for b in g]

        def load_w(src, nk):
            wt = wring[st["w"] % NBW]
            st["w"] += 1
            S.dma("pool", wt.ap[:, 0:nk, :], src.rearrange("(kc p) c -> p kc c", p=128), writes=[wt])
            return wt

        def rsqrt_act(out_ap, in_ap, rd, wr):
            S.A(lambda e: e.activation(out_ap, in_ap, AF.Ln, bias=EPSC, scale=1.0), reads=rd + [small], writes=[wr])
            S.A(lambda e: e.activation(out_ap, out_ap, AF.Exp, scale=-0.5), reads=[wr], writes=[wr])

        def recip_act(out_ap, in_ap, rd, wr):
            S.A(lambda e: e.activation(out_ap, in_ap, AF.Ln), reads=rd, writes=[wr])
            S.A(lambda e: e.activation(out_ap, out_ap, AF.Exp, scale=-1.0), reads=[wr], writes=[wr])

        def gemm_fm(wsrc, n_oc, nk, rhs_tiles_fn, rhs_fn, ntb, evac, ncols=512, use_hooks=False):
            pending = []
            hooks = {max(0, (3 * nk) // 8 - 1): 0, max(0, (3 * nk) // 4 - 1): 1} if (nk >= 8 and use_hooks) else {}

            def run_stage(h, oc):
                for p in list(pending):
                    if p[1] == h and p[2] < oc:
                        p[0][p[1]]()
                        p[1] += 1
                        p[2] = oc
                        if p[1] >= len(p[0]):
                            pending.remove(p)

            for oc in range(n_oc):
                gk = "g%d" % ntb
                gi = st.get(gk, 0)
                st[gk] = gi + 1
                ng = 8 // ntb
                g = list(range((gi % ng) * ntb, (gi % ng) * ntb + ntb))
                st["bank"] = g[-1] + 1
                st["reserved"] = set(g)
                nq = (nk + 15) // 16
                for kq in range(nq):
                    nkk = min(16, nk - kq * 16)
                    wt = load_w(wsrc(oc, kq), nkk)
                    for k16 in range(nkk):
                        kc = kq * 16 + k16
                        for tb in range(ntb):
                            S.T(lambda e, b=g[tb], wt=wt, k16=k16, kc=kc, tb=tb: e.matmul(
                                ps[:, b, 0:ncols], wt.ap[:, k16, :], rhs_fn(kc, tb),
                                start=(kc == 0), stop=(kc == nk - 1)),
                                reads=[wt] + rhs_tiles_fn(kc), writes=[PB[g[tb]]])
                        if kc in hooks:
                            run_stage(hooks[kc], oc)
                st["reserved"] = ()
                cur = evac(oc, g)
                if not hooks:
                    run_stage(0, oc + 1)
                    run_stage(1, oc + 1)
                if cur is not None:
                    pending.append([list(cur) if isinstance(cur, (list, tuple)) else [cur], 0, oc])
            k = n_oc
            while pending:
                k += 1
                run_stage(0, k)
                run_stage(1, k)

        def transpose_in(src, dst, ntok, dtiles):
            TW = min(512, ntok)
            ntt = TW // 128
            AR.reset()
            xin = [AR.alloc([ntt, D], F32, f"xin{i}") for i in range(2)]
            stg = [AR.alloc([TW], F32, f"stg{i}") for i in range(4)]
            k = 0
            for tb in range(ntok // TW):
                xt = xin[tb % 2]
                S.dma("sp", xt.ap, src[tb * TW:(tb + 1) * TW, :].rearrange("(tt p) d -> p tt d", p=128), writes=[xt])
                for dc in range(KC):
                    b = next_bank()
                    for tt in range(ntt):
                        S.T(lambda e, b=b, tt=tt, dc=dc, xt=xt: e.transpose(
                            ps[:, b, tt * 128:(tt + 1) * 128], xt.ap[:, tt, dc * 128:(dc + 1) * 128], ident_f),
                            reads=[xt, params], writes=[PB[b]])
                    sg = stg[k % 4]
                    eng = S.A if k % 2 == 0 else S.V
                    if k % 2 == 0:
                        S.A(lambda e, sg=sg, b=b: e.copy(sg.ap, ps[:, b, 0:TW]), reads=[PB[b]], writes=[sg])
                    else:
                        S.V(lambda e, sg=sg, b=b: e.tensor_copy(sg.ap, ps[:, b, 0:TW]), reads=[PB[b]], writes=[sg])
                    S.dma("sp", dst[dc * 128:(dc + 1) * 128, tb * TW:(tb + 1) * TW], sg.ap, reads=[sg], writes=dtiles(dc, tb))
                    k += 1

        def transpose_out():
            AR.reset()
            xin = [AR.alloc([KC, 512], F32, f"xo_in{i}") for i in range(2)]
            ost = [AR.alloc([D], F32, f"ost{i}") for i in range(2)]
            k = 0
            j = 0
            xsv = xs.rearrange("(dc p) t -> p dc t", p=128)
            for tb in range(4):
                xt = xin[tb % 2]
                S.dma("sp", xt.ap, xsv[:, :, tb * 512:(tb + 1) * 512], reads=xs_cols(tb * 512, 512), writes=[xt])
                for tt in range(4):
                    o = ost[j % 2]
                    j += 1
                    for dc4 in range(4):
                        b = next_bank()
                        for i in range(4):
                            S.T(lambda e, b=b, i=i, dc4=dc4, tt=tt, xt=xt: e.transpose(
                                ps[:, b, i * 128:(i + 1) * 128], xt.ap[:, dc4 * 4 + i, tt * 128:(tt + 1) * 128], ident_f),
                                reads=[xt, params], writes=[PB[b]])
                        if k % 2 == 0:
                            S.A(lambda e, o=o, b=b, dc4=dc4: e.copy(o.ap[:, dc4 * 512:(dc4 + 1) * 512], ps[:, b, :]),
                                reads=[PB[b]], writes=[o])
                        else:
                            S.V(lambda e, o=o, b=b, dc4=dc4: e.tensor_copy(o.ap[:, dc4 * 512:(dc4 + 1) * 512], ps[:, b, :]),
                                reads=[PB[b]], writes=[o])
                        k += 1
                    r0 = tb * 512 + tt * 128
                    S.dma("sp", out_d[r0:r0 + 128, :], o.ap, reads=[o])

        def norm_fm(src, gcol, h, tok0, T, TW, tmp_base, stiles=None):
            AR.reset(tmp_base)
            xts = [AR.alloc([KC, TW], F32, f"nx{i}") for i in range(2)]
            sq = AR.alloc([KC, TW], BF16, "nsq")
            rstd = AR.alloc([TW], F32, "nrstd")
            srcv = src.rearrange("(dc p) t -> p dc t", p=128)
            for tb in range(T // TW):
                xt = xts[tb % 2]
                c0 = tok0 + tb * TW
                S.dma("sp", xt.ap, srcv[:, :, c0:c0 + TW], reads=(stiles if stiles is not None else xs_cols(c0, TW)), writes=[xt])
                hk = KC // 2
                S.A(lambda e, xt=xt: e.activation(sq.ap[:, 0:hk, :], xt.ap[:, 0:hk, :], AF.Square), reads=[xt], writes=[sq])
                S.A(lambda e, xt=xt: e.activation(sq.ap[:, hk:KC, :], xt.ap[:, hk:KC, :], AF.Square), reads=[xt], writes=[sq])
                b = next_bank()
                for dc in range(KC):
                    S.T(lambda e, b=b, dc=dc: e.matmul(ps[:, b, 0:TW], cm(C_ONESD), sq.ap[:, dc, :],
                                                      start=(dc == 0), stop=(dc == KC - 1)),
                        reads=[sq, cbf], writes=[PB[b]])
                rsqrt_act(rstd.ap, ps[:, b, 0:TW], [PB[b]], rstd)
                for dc in range(KC):
                    S.V(lambda e, dc=dc, xt=xt, tb=tb: e.scalar_tensor_tensor(
                        h.ap[:, dc, tb * TW:(tb + 1) * TW], xt.ap[:, dc, :], pcol(gcol + dc), rstd.ap, ALU.mult, ALU.mult),
                        reads=[xt, rstd, params], writes=[h])

        def mem_prep(l, base):
            AR.reset(base)
            kT = AR.alloc([4, MEM], BF16, "mkT")
            vsb = AR.alloc([4, 2, 128], BF16, "mvsb")
            vsb.ap = AR.ap[:, vsb.off:vsb.off + vsb.size].bitcast(BF16).rearrange("p (h m d) -> p h m d", h=4, m=2)
            hm = AR.alloc([KC, MEM], BF16, "hm")
            kf = [AR.alloc([MEM], F32, f"mkf{i}") for i in range(2)]
            sqm = [AR.alloc([MEM], BF16, f"msq{i}") for i in range(2)]
            vf = [AR.alloc([MEM], BF16, f"mvf{i}") for i in range(2)]
            rs = AR.alloc([MEM], F32, "mrs")
            tmp_base = AR.top
            norm_fm(memT, P_MEMN + 16 * l, hm, 0, MEM, MEM, tmp_base, stiles=[MEMT])

            def evac(oc, g):
                b = g[0]
                if oc < 4:
                    k_f = kf[oc % 2]
                    s_q = sqm[oc % 2]
                    S.A(lambda e: e.copy(k_f.ap, ps[:, b, 0:MEM]), reads=[PB[b]], writes=[k_f])
                    S.A(lambda e: e.activation(s_q.ap, ps[:, b, 0:MEM], AF.Square), reads=[PB[b]], writes=[s_q])

                    def post():
                        b2 = next_bank()
                        S.T(lambda e: e.matmul(ps[:, b2, 0:MEM], cm(C_ONESHD), s_q.ap, start=True, stop=True),
                            reads=[s_q, cbf], writes=[PB[b2]])
                        rsqrt_act(rs.ap, ps[:, b2, 0:MEM], [PB[b2]], rs)
                        S.V(lambda e: e.scalar_tensor_tensor(kT.ap[:, oc, :], k_f.ap, pcol(P_MK + l), rs.ap, ALU.mult, ALU.mult),
                            reads=[k_f, rs, params], writes=[kT])
                    return post
                else:
                    hh = oc - 4
                    v_f = vf[oc % 2]
                    S.A(lambda e: e.copy(v_f.ap, ps[:, b, 0:MEM]), reads=[PB[b]], writes=[v_f])

                    def post():
                        b2 = next_bank()
                        pb16 = ps[:, b2, :].bitcast(BF16)
                        for mt in range(2):
                            S.T(lambda e, mt=mt: e.transpose(pb16[:, mt * 128:(mt + 1) * 128], v_f.ap[:, mt * 128:(mt + 1) * 128], cm(C_IDENT)),
                                reads=[v_f, cbf], writes=[PB[b2]])
                        S.V(lambda e: e.tensor_copy(vsb.ap[:, hh, :, :], pb16[:, 0:256].rearrange("p (m d) -> p m d", m=2)),
                            reads=[PB[b2]], writes=[vsb])
                    return post

            gemm_fm(lambda oc, kq: w_mkv[l][:, oc * 128:(oc + 1) * 128], 8, KC,
                    lambda kc: [hm], lambda kc, tb: hm.ap[:, kc, :], 1, evac, ncols=MEM)
            if dbg and l == 0:
                S.dma("sp", dbg_b[:, 0:1024], kT.ap.rearrange("p a b -> p (a b)"), reads=[kT])
                S.dma("sp", dbg_b[:, 1024:2048], vsb.ap.rearrange("p h m d -> p (h m d)"), reads=[vsb])
                S.dma("sp", dbg_h, hm.ap.rearrange("p a b -> p (a b)"), reads=[hm])
            return kT, vsb

        def mem_attn_bufs(rstd=None):
            d = {}
            d["rstd"] = rstd if rstd is not None else AR.alloc([S_], F32, "ma_rstd")
            d["qn"] = AR.alloc([S_], BF16, "ma_qn")
            d["E"] = [AR.alloc([512], BF16, f"ma_E{i}") for i in range(4)]
            d["rl"] = AR.alloc([512], F32, "ma_rl")
            d["osb"] = AR.alloc([512], F32, "ma_osb")
            d["ei"] = 0
            return d

        def mem_attn_head(l, hh, qf, sq, kT, vsb, mb, catst):
            g = next_group(4)
            for tb in range(4):
                S.T(lambda e, tb=tb: e.matmul(ps[:, g[tb], :], cm(C_ONESHD), sq.ap[:, tb * 512:(tb + 1) * 512], start=True, stop=True),
                    reads=[sq, cbf], writes=[PB[g[tb]]])
            rstd = mb["rstd"]
            qn = mb["qn"]
            rsqrt_act(rstd.ap.rearrange("p (b n) -> p b n", b=4), psg(g), pbs(g), rstd)
            S.V(lambda e: e.scalar_tensor_tensor(qn.ap, qf, pcol(P_MQ + l), rstd.ap, ALU.mult, ALU.mult),
                reads=[mb["qf_tile"], rstd, params], writes=[qn])
            sc = 128.0 ** -0.5
            if SUB2 <= 1:
                return
            for tb in range(4):
                Es = []
                for mt in range(2):
                    bS = next_bank()
                    S.T(lambda e, bS=bS, mt=mt, tb=tb: e.matmul(ps[:, bS, :], kT.ap[:, hh, mt * 128:(mt + 1) * 128],
                                                               qn.ap[:, tb * 512:(tb + 1) * 512], start=True, stop=True),
                        reads=[kT, qn], writes=[PB[bS]])
                    E = mb["E"][mb["ei"] % 4]
                    mb["ei"] += 1
                    S.A(lambda e, bS=bS, E=E: e.activation(E.ap, ps[:, bS, :], AF.Exp, scale=sc), reads=[PB[bS]], writes=[E])
                    Es.append(E)
                if SUB2 <= 2:
                    continue
                bo = next_bank()
                bl = next_bank()
                for mt in range(2):
                    S.T(lambda e, mt=mt, E=Es[mt], bo=bo: e.matmul(ps[:, bo, :], vsb.ap[:, hh, mt, :], E.ap, start=(mt == 0), stop=(mt == 1)),
                        reads=[vsb, Es[mt]], writes=[PB[bo]])
                for mt in range(2):
                    S.T(lambda e, mt=mt, E=Es[mt], bl=bl: e.matmul(ps[:, bl, :], cm(C_ONES1), E.ap, start=(mt == 0), stop=(mt == 1)),
                        reads=[cbf, Es[mt]], writes=[PB[bl]])
                if SUB2 <= 3:
                    continue
                rl = mb["rl"]
                osb = mb["osb"]
                recip_act(rl.ap, ps[:, bl, :], [PB[bl]], rl)
                S.V(lambda e, tb=tb, bo=bo: e.tensor_tensor(catst.ap[:, tb * 512:(tb + 1) * 512], ps[:, bo, :], rl.ap, ALU.mult),
                    reads=[PB[bo], rl], writes=[catst])
            if SUB2 <= 4:
                return
            S.dma("sp", cat_d[12 + hh], catst.ap, reads=[catst], writes=[CAT[12 + hh]])

        def out_proj(l):
            AR.reset()
            catq = [AR.alloc([4, S_], BF16, f"catq{i}") for i in range(4)]
            xo = [AR.alloc([S_], F32, f"xo{i}") for i in range(2)]
            cv = cat_d.rearrange("c p t -> p c t")
            for i in range(4):
                S.dma("sp", catq[i].ap, cv[:, 4 * i:4 * i + 4, :], reads=CAT[4 * i:4 * i + 4], writes=[catq[i]])

            def evac(oc, g):
                x_o = xo[oc % 2]
                S.dma("sp", x_o.ap, xs[oc * 128:(oc + 1) * 128, :], reads=xs_rows(oc, 0, S_), writes=[x_o])
                S.V(lambda e: e.tensor_tensor(x_o.ap.rearrange("p (b n) -> p b n", b=4), psg(g),
                                              x_o.ap.rearrange("p (b n) -> p b n", b=4), ALU.add),
                    reads=pbs(g) + [x_o], writes=[x_o])
                S.dma("sp", xs[oc * 128:(oc + 1) * 128, :], x_o.ap, reads=[x_o], writes=xs_rows(oc, 0, S_))
                return None

            gemm_fm(lambda oc, kq: w_out[l][:, oc * 128:(oc + 1) * 128], KC, KC,
                    lambda kc: [catq[kc // 4]], lambda kc, tb: catq[kc // 4].ap[:, kc % 4, tb * 512:(tb + 1) * 512], 4, evac)

        def ffn(l, fuse_out=False):
            HT = 1024
            ostc = {"i": 0}
            for half in range(2):
                AR.reset()
                z = [AR.alloc([HT], BF16, f"z{i}") for i in range(64)]
                hh = AR.alloc([KC, HT], BF16, "ffn_h")
                rr = [AR.alloc([HT], F32, f"ffn_r{i}") for i in range(2)]
                xo = [AR.alloc([HT], F32, f"ffn_xo{i}") for i in range(2)]
                ost = [AR.alloc([8, 128], F32, f"ffn_ost{i}") for i in range(2)] if fuse_out else None
                norm_fm(xs, P_FFN + 16 * l, hh, half * HT, HT, 512, 0)

                def evac1(fc, g):
                    r = rr[fc % 2]
                    S.A(lambda e: e.activation(r.ap.rearrange("p (b n) -> p b n", b=2), psg(g), AF.Relu),
                        reads=pbs(g), writes=[r])
                    S.V(lambda e: e.tensor_tensor(z[fc].ap, r.ap, r.ap, ALU.mult), reads=[r], writes=[z[fc]])
                    return None

                for i in range(64):
                    z[i] = AR.at(z[i].off, [HT], BF16, f"z{i}")
                gemm_fm(lambda fc, kq: w_ff1[l][:, fc * 128:(fc + 1) * 128], 64, KC,
                        lambda kc: [hh], lambda kc, tb: hh.ap[:, kc, tb * 512:(tb + 1) * 512], 2, evac1)

                def evac2(dc, g):
                    x_o = xo[dc % 2]
                    S.dma("sp", x_o.ap, xs[dc * 128:(dc + 1) * 128, half * HT:(half + 1) * HT], reads=xs_rows(dc, half * HT, HT), writes=[x_o])
                    S.V(lambda e: e.tensor_tensor(x_o.ap.rearrange("p (b n) -> p b n", b=2), psg(g),
                                                  x_o.ap.rearrange("p (b n) -> p b n", b=2), ALU.add),
                        reads=pbs(g) + [x_o], writes=[x_o])
                    if not fuse_out:
                        S.dma("sp", xs[dc * 128:(dc + 1) * 128, half * HT:(half + 1) * HT], x_o.ap, reads=[x_o], writes=xs_rows(dc, half * HT, HT))
                        return None

                    def tail(half=half):
                        o_t = ost[ostc["i"] % 2]
                        ostc["i"] += 1
                        g2 = next_group(2)
                        for t in range(8):
                            S.T(lambda e, t=t: e.transpose(ps[:, g2[t // 4], (t % 4) * 128:(t % 4 + 1) * 128],
                                                           x_o.ap[:, t * 128:(t + 1) * 128], ident_f),
                                reads=[x_o, params], writes=[PB[g2[t // 4]]])
                        S.A(lambda e: e.copy(o_t.ap[:, 0:4, :], ps[:, g2[0], :].rearrange("p (j d) -> p j d", j=4)),
                            reads=[PB[g2[0]]], writes=[o_t])
                        S.V(lambda e: e.tensor_copy(o_t.ap[:, 4:8, :], ps[:, g2[1], :].rearrange("p (j d) -> p j d", j=4)),
                            reads=[PB[g2[1]]], writes=[o_t])
                        S.dma("sp", out_d[half * HT:(half + 1) * HT, dc * 128:(dc + 1) * 128].rearrange("(t p) d -> p t d", p=128),
                              o_t.ap, reads=[o_t])
                    return tail

                gemm_fm(lambda dc, kq: w_ff2[l][kq * 2048:(kq + 1) * 2048, dc * 128:(dc + 1) * 128], KC, 64,
                        lambda kc: [z[kc]], lambda kc, tb: z[kc].ap[:, tb * 512:(tb + 1) * 512], 2, evac2, use_hooks=fuse_out)

        def mixer0():
            AR.reset()
            h = AR.alloc([KC, S_], BF16, "h0")
            base = AR.top
            kT, vsb = mem_prep(0, base)
            if SUB <= 1:
                return
            base2 = vsb.off + vsb.size
            AR.reset(base2)
            xin = [AR.alloc([4, D], F32, f"fx_in{i}") for i in range(2)]
            xblk = AR.alloc([KC, 512], F32, "fx_blk")
            fsq = AR.alloc([KC, 512], BF16, "fx_sq")
            frs = AR.alloc([512], F32, "fx_rstd")
            xsv = xs.rearrange("(dc p) t -> p dc t", p=128)
            kk_ = 0
            for tb in range(4):
                xt = xin[tb % 2]
                S.dma("sp", xt.ap, x_in[tb * 512:(tb + 1) * 512, :].rearrange("(tt p) d -> p tt d", p=128), writes=[xt])
                for dc in range(KC):
                    b = next_bank()
                    for tt in range(4):
                        S.T(lambda e, b=b, tt=tt, dc=dc, xt=xt: e.transpose(
                            ps[:, b, tt * 128:(tt + 1) * 128], xt.ap[:, tt, dc * 128:(dc + 1) * 128], ident_f),
                            reads=[xt, params], writes=[PB[b]])
                    if kk_ % 2 == 0:
                        S.A(lambda e, b=b, dc=dc: e.copy(xblk.ap[:, dc, :], ps[:, b, :]), reads=[PB[b]], writes=[xblk])
                    else:
                        S.V(lambda e, b=b, dc=dc: e.tensor_copy(xblk.ap[:, dc, :], ps[:, b, :]), reads=[PB[b]], writes=[xblk])
                    kk_ += 1
                S.dma("sp", xsv[:, :, tb * 512:(tb + 1) * 512], xblk.ap, reads=[xblk], writes=xs_cols(tb * 512, 512))
                hk = KC // 2
                S.A(lambda e: e.activation(fsq.ap[:, 0:hk, :], xblk.ap[:, 0:hk, :], AF.Square), reads=[xblk], writes=[fsq])
                S.A(lambda e: e.activation(fsq.ap[:, hk:KC, :], xblk.ap[:, hk:KC, :], AF.Square), reads=[xblk], writes=[fsq])
                b = next_bank()
                for dc in range(KC):
                    S.T(lambda e, b=b, dc=dc: e.matmul(ps[:, b, :], cm(C_ONESD), fsq.ap[:, dc, :], start=(dc == 0), stop=(dc == KC - 1)),
                        reads=[fsq, cbf], writes=[PB[b]])
                rsqrt_act(frs.ap, ps[:, b, :], [PB[b]], frs)
                for dc in range(KC):
                    S.V(lambda e, dc=dc, tb=tb: e.scalar_tensor_tensor(
                        h.ap[:, dc, tb * 512:(tb + 1) * 512], xblk.ap[:, dc, :], pcol(P_MIX + dc), frs.ap, ALU.mult, ALU.mult),
                        reads=[xblk, frs, params], writes=[h])
            if SUB <= 2:
                return
            AR.reset(base2)
            PADW = S_ + 16
            upad = [AR.alloc([PADW], F32, f"upad{i}") for i in range(2)]
            sb = [AR.alloc([PADW], F32, f"spp{i}") for i in range(2)]
            pooled = [AR.alloc([S_], BF16, f"pooled{i}") for i in range(6)]
            catst = [AR.alloc([S_], BF16, f"catst{i}") for i in range(2)]
            tmp16 = AR.alloc([16], F32, "tmp16")
            sq = [AR.alloc([S_], BF16, f"msq{i}") for i in range(2)]
            mb = mem_attn_bufs()
            pw_tiles = {}
            for t in upad + sb:
                S.V(lambda e, t=t: e.memset(t.ap[:, 0:16], 0.0), writes=[t])
            cst = {"i": 0}

            def token_out(gi):
                for oc2 in range(3):
                    wt = load_w(pool_w[gi, :, oc2 * 128:(oc2 + 1) * 128], 3)
                    g = next_group(4)
                    for ic in range(3):
                        pl = pooled[(gi * 3 + ic) % 6]
                        for tb in range(4):
                            S.T(lambda e, b=g[tb], ic=ic, tb=tb, wt=wt, pl=pl: e.matmul(
                                ps[:, b, :], wt.ap[:, ic, :], pl.ap[:, tb * 512:(tb + 1) * 512], start=(ic == 0), stop=(ic == 2)),
                                reads=[wt, pl], writes=[PB[g[tb]]])
                    cs_ = catst[cst["i"] % 2]
                    cst["i"] += 1
                    oc = gi * 3 + oc2
                    S.A(lambda e, cs_=cs_, g=g, oc=oc: e.activation(cs_.ap.rearrange("p (b n) -> p b n", b=4), psg(g),
                                                                    AF.Identity, scale=pcol(P_PSC + oc)),
                        reads=pbs(g) + [params], writes=[cs_])
                    S.dma("sp", cat_d[oc], cs_.ap, reads=[cs_], writes=[CAT[oc]])

            def evac(oc, g):
                up = upad[oc % 2]
                S.A(lambda e: e.copy(up.ap[:, 16:PADW].rearrange("p (b n) -> p b n", b=4), psg(g)), reads=pbs(g), writes=[up])
                if SUB <= 3:
                    return None
                if oc < 12:
                    gi = oc // 3
                    w = 2 << gi
                    src = up
                    for step in range(gi + 1):
                        sh = 1 << step
                        dst = sb[step % 2]
                        S.V(lambda e, src=src, dst=dst, sh=sh: e.tensor_tensor(
                            dst.ap[:, 16:PADW], src.ap[:, 16:PADW], src.ap[:, 16 - sh:PADW - sh], ALU.add),
                            reads=[src], writes=[dst])
                        src = dst
                    pl = pooled[oc % 6]
                    S.V(lambda e, src=src, pl=pl: e.scalar_tensor_tensor(
                        pl.ap, src.ap[:, 16:PADW], 1.0 / w, up.ap[:, 16:PADW], ALU.mult, ALU.subtract),
                        reads=[src, up], writes=[pl])
                    S.V(lambda e, src=src: e.tensor_tensor(tmp16.ap, src.ap[:, 16:32], pcol(P_INVC + 16 * gi, 16), ALU.mult),
                        reads=[src, params], writes=[tmp16])
                    S.V(lambda e, pl=pl: e.tensor_tensor(pl.ap[:, 0:16], tmp16.ap, up.ap[:, 16:32], ALU.subtract),
                        reads=[tmp16, up], writes=[pl])
                    if oc % 3 == 2 and SUB >= 5:
                        return lambda: token_out(gi)
                    return None
                else:
                    if SUB <= 5:
                        return None
                    hh = oc - 12
                    s_q = sq[oc % 2]
                    S.A(lambda e: e.activation(s_q.ap.rearrange("p (b n) -> p b n", b=4), psg(g), AF.Square),
                        reads=pbs(g), writes=[s_q])
                    cs_ = catst[cst["i"] % 2]
                    cst["i"] += 1

                    def post():
                        mb["qf_tile"] = up
                        mem_attn_head(0, hh, up.ap[:, 16:PADW], s_q, kT, vsb, mb, cs_)
                    return post

            gemm_fm(lambda oc, kq: w_in[0][:, oc * 128:(oc + 1) * 128], KC, KC,
                    lambda kc: [h], lambda kc, tb: h.ap[:, kc, tb * 512:(tb + 1) * 512], 4, evac)

        def hnr_bufs():
            d = {}
            d["cs"] = AR.alloc([2, S_], F32, "ropecs")
            S.dma("sp", d["cs"].ap, cs_d, writes=[d["cs"]])
            d["kf"] = [AR.alloc([S_], F32, f"kf{i}") for i in range(3)]
            d["sq"] = [AR.alloc([S_], BF16, f"ksq{i}") for i in range(2)]
            d["rstd"] = AR.alloc([S_], F32, "krstd")
            d["knb"] = [AR.alloc([S_], BF16, f"knb{i}") for i in range(2)]
            d["t1"] = AR.alloc([S_], F32, "kt1")
            d["t2"] = AR.alloc([S_], F32, "kt2")
            d["ob"] = [AR.alloc([S_], BF16, f"kob{i}") for i in range(2)]
            d["i"] = 0
            return d

        def hnr_evac(hb, g, gcol, dst, dtile):
            i = hb["i"]
            hb["i"] += 1
            kf = hb["kf"][i % 3]
            sq = hb["sq"][i % 2]
            ob = hb["ob"][i % 2]
            rstd, knb, t1, t2, cs = hb["rstd"], hb["knb"][i % 2], hb["t1"], hb["t2"], hb["cs"]
            v4 = lambda t: t.ap.rearrange("p (b n) -> p b n", b=4)
            v2 = lambda t, h: t.ap[:, h * 1024:(h + 1) * 1024].rearrange("p (b n) -> p b n", b=2)
            for h_ in range(2):
                ga = g[2 * h_:2 * h_ + 2]
                gd = g[2 * (1 - h_):2 * (1 - h_) + 2]
                S.A(lambda e, h_=h_, ga=ga: e.activation(v2(sq, h_), psg(ga), AF.Square), reads=pbs(ga), writes=[sq])
                S.V(lambda e, h_=h_, gd=gd: e.tensor_copy(v2(kf, 1 - h_), psg(gd)), reads=pbs(gd), writes=[kf])

            def postA():
                g2 = next_group(4)
                for tb in range(4):
                    S.T(lambda e, tb=tb: e.matmul(ps[:, g2[tb], :], cm(C_BLK), sq.ap[:, tb * 512:(tb + 1) * 512], start=True, stop=True),
                        reads=[sq, cbf], writes=[PB[g2[tb]]])
                rsqrt_act(v4(rstd), psg(g2), pbs(g2), rstd)
                S.V(lambda e: e.scalar_tensor_tensor(kf.ap, kf.ap, pcol(gcol), rstd.ap, ALU.mult, ALU.mult),
                    reads=[kf, rstd, params], writes=[kf])
                S.A(lambda e: e.copy(knb.ap, kf.ap), reads=[kf], writes=[knb])

            def postB():
                g3 = next_group(4)
                for tb in range(4):
                    S.T(lambda e, tb=tb: e.matmul(ps[:, g3[tb], :], cm(C_RT), knb.ap[:, tb * 512:(tb + 1) * 512], start=True, stop=True),
                        reads=[knb, cbf], writes=[PB[g3[tb]]])
                S.G(lambda e: e.tensor_tensor(t1.ap, kf.ap, cs.ap[:, 0, :], ALU.mult), reads=[kf, cs], writes=[t1])
                S.V(lambda e: e.tensor_tensor(v4(t2), psg(g3), cs.ap[:, 1, :].rearrange("p (b n) -> p b n", b=4), ALU.mult),
                    reads=pbs(g3) + [cs], writes=[t2])
                S.V(lambda e: e.tensor_tensor(ob.ap, t1.ap, t2.ap, ALU.add), reads=[t1, t2], writes=[ob])
                S.dma("sp", dst, ob.ap, reads=[ob], writes=[dtile])
            return [postA, postB]

        def kv_phase():
            AR.reset()
            h = AR.alloc([KC, S_], BF16, "hkv")
            base = AR.top
            norm_fm(xs, P_KVN, h, 0, S_, 512, base)
            AR.reset(base)
            hb = hnr_bufs()
            vst = [AR.alloc([S_], BF16, f"vst{i}") for i in range(2)]

            def evac(oc, g):
                if oc < 12:
                    return hnr_evac(hb, g, P_KN, kT_d[oc], KT[oc])
                v_s = vst[oc % 2]
                S.A(lambda e: e.copy(v_s.ap.rearrange("p (b n) -> p b n", b=4), psg(g)), reads=pbs(g), writes=[v_s])
                S.dma("sp", vT_d[oc - 12], v_s.ap, reads=[v_s], writes=[VT[oc - 12]])
                return None

            gemm_fm(lambda oc, kq: w_kv[:, oc * 128:(oc + 1) * 128], 24, KC,
                    lambda kc: [h], lambda kc, tb: h.ap[:, kc, tb * 512:(tb + 1) * 512], 4, evac, use_hooks=True)

        def q_phase():
            AR.reset()
            h = AR.alloc([KC, S_], BF16, "hq")
            base = AR.top
            kT, vsb = mem_prep(1, base)
            base2 = vsb.off + vsb.size
            norm_fm(xs, P_MIX + 16, h, 0, S_, 512, base2)
            AR.reset(base2)
            hb = hnr_bufs()
            qf = hb["kf"]
            sq = hb["sq"]
            catst = [AR.alloc([S_], BF16, f"qcatst{i}") for i in range(2)]
            mb = mem_attn_bufs(hb["rstd"])

            def evac(oc, g):
                if oc < 12:
                    return hnr_evac(hb, g, P_QN, qT_d[oc], QT[oc])
                hh = oc - 12
                q_f = qf[hb["i"] % 3]
                s_q = sq[hb["i"] % 2]
                hb["i"] += 1
                cs_ = catst[oc % 2]
                S.A(lambda e: e.copy(q_f.ap.rearrange("p (b n) -> p b n", b=4), psg(g)), reads=pbs(g), writes=[q_f])
                S.A(lambda e: e.activation(s_q.ap.rearrange("p (b n) -> p b n", b=4), psg(g), AF.Square), reads=pbs(g), writes=[s_q])

                def post():
                    mb["qf_tile"] = q_f
                    mem_attn_head(1, hh, q_f.ap, s_q, kT, vsb, mb, cs_)
                return post

            gemm_fm(lambda oc, kq: w_in[1][:, oc * 128:(oc + 1) * 128], KC, KC,
                    lambda kc: [h], lambda kc, tb: h.ap[:, kc, tb * 512:(tb + 1) * 512], 4, evac, use_hooks=True)

        def attn_phase():
            AR.reset()
            kk = [[AR.alloc([S_], BF16, f"kk{s}_{i}") for i in range(2)] for s in range(2)]
            zq = [[AR.alloc([S_], BF16, f"zq{m}_{hf}") for hf in range(2)] for m in range(2)]
            for m in range(2):
                for hf in range(2):
                    z0 = 64 * (1 - hf)
                    S.V(lambda e, m=m, hf=hf, z0=z0: e.memset(zq[m][hf].ap[z0:z0 + 64, :], 0.0), writes=[zq[m][hf]])
            vT = [AR.alloc([S_], BF16, f"vT{i}") for i in range(2)]
            vtok = [AR.alloc([16, 128], BF16, f"vtok{i}") for i in range(2)]
            E = [AR.alloc([512], BF16, f"E{i}") for i in range(8)]
            r12 = AR.alloc([1024], F32, "r12")
            aa = AR.alloc([512], F32, "aa")
            bb = AR.alloc([512], F32, "bb")
            oo = AR.alloc([512], F32, "oo")
            osq = AR.alloc([512], BF16, "osq")
            rs = AR.alloc([512], F32, "ars")
            catst = [AR.alloc([S_], BF16, f"acat{i}") for i in range(2)]
            sbank = {"i": 0}
            spair = {"i": 0}
            ei = {"i": 0}
            sc = 64.0 ** -0.5
            BO = (4, 5)
            BL = (6, 7)

            def sb_next():
                b = sbank["i"] % 4
                sbank["i"] += 1
                return b

            for hp in range(6):
                k2t = kk[hp % 2]
                S.dma("sp", k2t[0].ap, kT_d[hp], reads=[KT[hp]], writes=[k2t[0]])
                S.dma("sp", k2t[1].ap, kT_d[6 + hp], reads=[KT[6 + hp]], writes=[k2t[1]])
                for m in range(2):
                    for hf in range(2):
                        q0 = 64 * hf
                        S.dma("sp", zq[m][hf].ap[q0:q0 + 64, :], qT_d[6 * m + hp][q0:q0 + 64, :],
                              reads=[QT[6 * m + hp]], writes=[zq[m][hf]])
                for half in range(2):
                    hd = 2 * hp + half
                    p0 = 64 * half
                    v_T = vT[hd % 2]
                    v_k = vtok[hd % 2]
                    S.dma("sp", v_T.ap, vT_d[hd], reads=[VT[hd]], writes=[v_T])
                    for t4i in range(4):
                        b = sb_next()
                        pb16 = ps[:, b, :].bitcast(BF16)
                        for j in range(4):
                            tt = t4i * 4 + j
                            S.T(lambda e, pb16=pb16, j=j, tt=tt, v_T=v_T: e.transpose(
                                pb16[:, j * 128:(j + 1) * 128], v_T.ap[:, tt * 128:(tt + 1) * 128], cm(C_IDENT)),
                                reads=[v_T, cbf], writes=[PB[b]])
                        S.V(lambda e, pb16=pb16, t4i=t4i, v_k=v_k: e.tensor_copy(
                            v_k.ap[:, t4i * 4:(t4i + 1) * 4, :], pb16[:, 0:512].rearrange("p (j d) -> p j d", j=4)),
                            reads=[PB[b]], writes=[v_k])
                    cs_ = catst[hd % 2]
                    DEPTH = 2
                    queue = []
                    tails = []

                    def issue_S(qb, kb, m, half=half, k2t=k2t):
                        nkb = 4 * qb + 4
                        i_d = kb - 4 * qb
                        c0 = 128 * i_d if i_d > 0 else 0
                        qt = zq[m][half]
                        kt = k2t[m]
                        bS = sb_next()
                        diag = i_d >= 0
                        S.T(lambda e: e.matmul(ps[:, bS, c0:512], kt.ap[:, kb * 128:(kb + 1) * 128],
                                               qt.ap[:, qb * 512 + c0:(qb + 1) * 512], start=True, stop=(not diag)),
                            reads=[kt, qt], writes=[PB[bS]])
                        if diag:
                            S.T(lambda e: e.matmul(ps[:, bS, c0:c0 + 128], cm(C_IDENT), cm(C_TRIB), start=False, stop=True),
                                reads=[cbf], writes=[PB[bS]])
                        Et = E[ei["i"] % len(E)]
                        ei["i"] += 1
                        S.A(lambda e: e.activation(Et.ap[:, c0:512], ps[:, bS, c0:512], AF.Exp, scale=sc),
                            reads=[PB[bS]], writes=[Et])
                        return (qb, kb, m, c0, nkb, Et)

                    def post_a(qb):
                        S.V(lambda e: e.tensor_copy(aa.ap, ps[:, BO[0], :]), reads=[PB[BO[0]]], writes=[aa])
                        S.V(lambda e: e.tensor_copy(bb.ap, ps[:, BO[1], :]), reads=[PB[BO[1]]], writes=[bb])
                        recip_act(r12.ap.rearrange("p (b n) -> p b n", b=2), ps[:, BL[0]:BL[1] + 1, :], [PB[BL[0]], PB[BL[1]]], r12)
                        S.V(lambda e: e.tensor_tensor(aa.ap, aa.ap, r12.ap[:, 0:512], ALU.mult), reads=[aa, r12], writes=[aa])
                        S.V(lambda e: e.tensor_tensor(bb.ap, bb.ap, r12.ap[:, 512:1024], ALU.mult), reads=[bb, r12], writes=[bb])
                        S.V(lambda e: e.scalar_tensor_tensor(oo.ap, bb.ap, NEGLAM, aa.ap, ALU.mult, ALU.add),
                            reads=[bb, aa, small], writes=[oo])
                        S.V(lambda e: e.tensor_tensor(osq.ap, oo.ap, oo.ap, ALU.mult), reads=[oo], writes=[osq])

                    def post_b(qb, cs_=cs_):
                        bs2 = sb_next()
                        S.T(lambda e: e.matmul(ps[:, bs2, :], cm(C_ONESHD), osq.ap, start=True, stop=True),
                            reads=[osq, cbf], writes=[PB[bs2]])
                        rsqrt_act(rs.ap, ps[:, bs2, :], [PB[bs2]], rs)
                        S.V(lambda e: e.scalar_tensor_tensor(cs_.ap[:, qb * 512:(qb + 1) * 512], oo.ap, SUBG, rs.ap, ALU.mult, ALU.mult),
                            reads=[oo, rs, small], writes=[cs_])

                    def issue_PV(info, v_k=v_k):
                        qb, kb, m, c0, nkb, Et = info
                        S.T(lambda e: e.matmul(ps[:, BO[m], c0:512], v_k.ap[:, kb, :], Et.ap[:, c0:512],
                                               start=(kb == 0), stop=(kb == nkb - 1)),
                            reads=[v_k, Et], writes=[PB[BO[m]]])
                        S.T(lambda e: e.matmul(ps[:, BL[m], c0:512], cm(C_ONES1), Et.ap[:, c0:512],
                                               start=(kb == 0), stop=(kb == nkb - 1)),
                            reads=[cbf, Et], writes=[PB[BL[m]]])
                        for t in tails:
                            t[0] -= 1
                        while tails and tails[0][0] <= 0:
                            tails.pop(0)[1]()
                        if kb == nkb - 1 and m == 1:
                            post_a(qb)
                            tails.append([5, lambda qb=qb: post_b(qb)])

                    steps = [(qb, kb, m) for qb in range(4) for kb in range(4 * qb + 4) for m in range(2)]
                    for st_ in steps:
                        queue.append(issue_S(*st_))
                        if len(queue) > DEPTH:
                            issue_PV(queue.pop(0))
                    while queue:
                        issue_PV(queue.pop(0))
                    while tails:
                        tails.pop(0)[1]()
                    S.dma("sp", cat_d[hd], cs_.ap, reads=[cs_], writes=[CAT[hd]])

        S.dma("sp", params.ap, params_d, writes=[params])
        S.dma("sp", cbf.ap, cbf_d.rearrange("p (a b) -> p a b", a=NCB), writes=[cbf])
        S.V(lambda e: e.memset(EPSC, EPS), writes=[small])
        AR.reset()
        lt = AR.alloc([64], F32, "lam_tmp")
        l4 = AR.alloc([4], F32, "lam4")
        dl = lambda i: params.ap[:, P_DL + 64 * i:P_DL + 64 * (i + 1)]
        S.V(lambda e: e.tensor_tensor(lt.ap, dl(0), dl(1), ALU.mult), reads=[params], writes=[lt])
        S.V(lambda e: e.reduce_sum(l4.ap[:, 0:1], lt.ap, axis=AX.X), reads=[lt], writes=[l4])
        S.V(lambda e: e.tensor_tensor(lt.ap, dl(2), dl(3), ALU.mult), reads=[params, l4], writes=[lt])
        S.V(lambda e: e.reduce_sum(l4.ap[:, 1:2], lt.ap, axis=AX.X), reads=[lt], writes=[l4])
        S.A(lambda e: e.activation(l4.ap[:, 2:4], l4.ap[:, 0:2], AF.Exp), reads=[l4], writes=[l4])
        S.V(lambda e: e.tensor_tensor(NEGLAM, l4.ap[:, 3:4], l4.ap[:, 2:3], ALU.subtract), reads=[l4], writes=[small])
        S.V(lambda e: e.tensor_scalar_add(NEGLAM, NEGLAM, -LAM_INIT), reads=[small], writes=[small])
        S.V(lambda e: e.tensor_scalar_mul(SUBG, pcol(P_SUB), 1.0 - LAM_INIT), reads=[params], writes=[small])

        phases = [
            lambda: None,
            lambda: transpose_in(mem_in, memT, MEM, lambda dc, tb: [MEMT]),
            mixer0,
            lambda: out_proj(0),
            lambda: ffn(0),
            kv_phase,
            q_phase,
            attn_phase,
            lambda: out_proj(1),
            lambda: ffn(1, fuse_out=True),
        ]
        for i, ph in enumerate(phases):
            if i > stop_after:
                break
            ph()
        if stop_after < len(phases) - 1:
            transpose_out()

        with nc.Block() as block:
            S.emit(block, engsem, rings)
    return nc


_CACHE = {}


def kernel(**inputs):
    stop_after = int(os.environ.get("MK_STOP", "99"))
    dbg = os.environ.get("MK_DBG", "0") == "1"
    inp = {k: np.asarray(v) for k, v in inputs.items()}
    key = (stop_after, dbg)
    if key not in _CACHE:
        _CACHE[key] = build_nc(stop_after, dbg)
    nc = _CACHE[key]
    params = pack_params(inp)
    cbf = const_bf16()
    cs = rope_cs()
    shared = {
        "pool_w": np.ascontiguousarray(inp["pool_w"][0], np.float32),
        "w_kv": np.ascontiguousarray(inp["w_kv"], np.float32),
        "params": params, "cbf": cbf, "ropecs": cs,
    }
    for l in range(2):
        shared[f"w_in{l}"] = np.ascontiguousarray(inp["w_in"][l], np.float32)
        shared[f"w_out{l}"] = np.ascontiguousarray(inp["w_out"][l], np.float32)
        shared[f"w_mkv{l}"] = np.ascontiguousarray(inp["w_mem_kv"][l], np.float32)
        shared[f"w_ff1{l}"] = np.ascontiguousarray(inp["w_ff1"][l], np.float32)
        shared[f"w_ff2{l}"] = np.ascontiguousarray(inp["w_ff2"][l], np.float32)
    in_maps = []
    for b in range(8):
        m = dict(shared)
        m["x"] = np.ascontiguousarray(inp["x"][b], np.float32)
        m["mem"] = np.ascontiguousarray(inp["mem"][b], np.float32)
        in_maps.append(m)
    ncores = int(os.environ.get("MK_CORES", "8"))
    res = run_bass_kernel_spmd(nc, in_maps[:ncores], core_ids=list(range(ncores)))
    if dbg:
        kernel.last_results = res.results
    out = np.stack([np.asarray(r["out"], np.float32) for r in res.results], axis=0)
    return out
```

```python
import math
import os
from contextlib import ExitStack

import ml_dtypes
import numpy as np

import concourse.bass as bass
import concourse.mybir as mybir
from concourse.bass_utils import run_bass_kernel_spmd

F32 = mybir.dt.float32
BF16 = mybir.dt.bfloat16
U8 = mybir.dt.uint8
AF = mybir.ActivationFunctionType
ALU = mybir.AluOpType
AX = mybir.AxisListType

S_ = 2048
D = 2048
KC = 16
MEM = 256
EPS = 1e-6
LAM_INIT = 0.8 - 0.6 * math.exp(-0.3 * 1)
ROPE_THETA = 500000.0


class Tile:
    __slots__ = ("ap", "w", "r", "rd", "name", "off", "size", "excl")

    def __init__(self, ap, name="", off=-1, size=0):
        self.excl = False
        self.ap = ap
        self.w = None
        self.r = {}
        self.rd = []
        self.name = name
        self.off = off
        self.size = size


class Op:
    __slots__ = ("eng", "fn", "deps", "marked", "semval", "is_dma", "slot", "dval", "idx")

    def __init__(self, eng, fn, is_dma=False):
        self.eng = eng
        self.fn = fn
        self.deps = set()
        self.marked = False
        self.semval = 0
        self.is_dma = is_dma
        self.slot = -1
        self.dval = 0
        self.idx = 0


ENGS = ("pe", "act", "dve", "pool", "sp")
RING = {"sp": 16, "pool": 8}


class Sched:
    def __init__(self):
        self.ops = {e: [] for e in ENGS}
        self.dma_count = {q: 0 for q in RING}
        self.dma_last = {q: [None] * RING[q] for q in RING}

    def add(self, eng, fn, reads=(), writes=(), dma=False):
        o = Op(eng, fn, dma)
        deps = o.deps
        for t in reads:
            if t.w is not None:
                deps.add(t.w)
            if t.excl:
                for en, ro in t.r.items():
                    if en != eng:
                        deps.add(ro)
        for t in writes:
            if t.w is not None:
                deps.add(t.w)
            deps.update(t.r.values())
            deps.update(t.rd)
        for t in reads:
            if dma:
                t.rd.append(o)
            else:
                t.r[eng] = o
        for t in writes:
            t.w = o
            t.r = {}
            t.rd = []
        if dma:
            k = self.dma_count[eng]
            R = RING[eng]
            o.slot = k % R
            o.dval = 16 * (k // R + 1)
            prev = self.dma_last[eng][o.slot]
            if prev is not None:
                deps.add(prev)
            self.dma_last[eng][o.slot] = o
            self.dma_count[eng] = k + 1
        if eng == "pe" and not dma:
            o.deps = {d for d in deps if d.is_dma or d.eng != "pe"}
        o.deps.discard(o)
        o.idx = len(self.ops[eng])
        self.ops[eng].append(o)
        return o

    def T(self, fn, reads=(), writes=()):
        return self.add("pe", fn, reads, writes)

    def A(self, fn, reads=(), writes=()):
        return self.add("act", fn, reads, writes)

    def V(self, fn, reads=(), writes=()):
        return self.add("dve", fn, reads, writes)

    def G(self, fn, reads=(), writes=()):
        return self.add("pool", fn, reads, writes)

    def dma(self, q, out_ap, in_ap, reads=(), writes=()):
        return self.add(q, lambda e: e.dma_start(out=out_ap, in_=in_ap), reads, writes, dma=True)

    def finalize(self):
        for e in ENGS:
            for o in self.ops[e]:
                for d in o.deps:
                    if not d.is_dma:
                        d.marked = True
        for e in ENGS:
            c = 0
            for o in self.ops[e]:
                if (not o.is_dma) and o.marked:
                    c += 1
                    o.semval = c

    def emit(self, block, engsem, rings):
        self.finalize()
        sched = self

        def run(ename, e):
            seen = {}
            for o in sched.ops[ename]:
                for d in sorted(o.deps, key=lambda d: (d.eng, d.idx)):
                    if d.is_dma:
                        key = (d.eng, d.slot)
                        sem = rings[d.eng][d.slot]
                        val = d.dval
                    else:
                        key = d.eng
                        sem = engsem[d.eng]
                        val = d.semval
                    if seen.get(key, 0) >= val:
                        continue
                    seen[key] = val
                    e.wait_ge(sem, val)
                ins = o.fn(e)
                if o.is_dma:
                    ins.then_inc(rings[ename][o.slot], 16)
                elif o.marked:
                    ins.then_inc(engsem[ename], 1)
            if ename in RING:
                for s, last in enumerate(sched.dma_last[ename]):
                    if last is not None and seen.get((ename, s), 0) < last.dval:
                        e.wait_ge(rings[ename][s], last.dval)

        @block.tensor
        def _(e):
            run("pe", e)

        @block.scalar
        def _(e):
            run("act", e)

        @block.vector
        def _(e):
            run("dve", e)

        @block.gpsimd
        def _(e):
            run("pool", e)

        @block.sync
        def _(e):
            run("sp", e)


class Arena:
    def __init__(self, ap, nbytes, sched):
        self.ap = ap
        self.nbytes = nbytes
        self.live = []
        self.S = sched
        self.top = 0

    def reset(self, top=0):
        self.top = top

    def alloc(self, shape, dt, name=""):
        isz = 4 if dt == F32 else 2
        n = int(np.prod(shape)) * isz
        off = (self.top + 31) // 32 * 32
        assert off + n <= self.nbytes, (name, off, n, self.nbytes)
        self.top = off + n
        return self.at(off, shape, dt, name)

    def at(self, off, shape, dt, name=""):
        isz = 4 if dt == F32 else 2
        n = int(np.prod(shape)) * isz
        assert off + n <= self.nbytes, (name, off, n, self.nbytes)
        ap = self.ap[:, off:off + n].bitcast(dt)
        if len(shape) == 2:
            ap = ap.rearrange("p (a b) -> p a b", a=shape[0])
        elif len(shape) == 3:
            ap = ap.rearrange("p (a b c) -> p a b c", a=shape[0], b=shape[1])
        t = Tile(ap, name, off, n)
        keep = []
        ops = self.S.ops
        for o in self.live:
            if o.off < off + n and off < o.off + o.size:
                cands = list(o.r.values())
                if o.w is not None:
                    if o.w.is_dma:
                        t.rd.append(o.w)
                    else:
                        cands.append(o.w)
                for c in cands:
                    cur = t.r.get(c.eng)
                    if cur is None or cur.idx < c.idx:
                        t.r[c.eng] = c
                t.rd.extend(o.rd)
            else:
                keep.append(o)
        keep.append(t)
        self.live = keep
        return t


P_MIX = 0
P_FFN = 32
P_MEMN = 64
P_KVN = 96
P_PSC = 112
P_MQ = 124
P_MK = 126
P_KN = 128
P_QN = 129
P_SUB = 130
P_DL = 131
P_INVC = 387
P_IDENT = 451
PC = 579

C_IDENT, C_ONESD, C_ONESHD, C_ONES1, C_BLK, C_RT, C_TRI, C_TRIB = range(8)
NCB = 8


def _vec16(v):
    return np.asarray(v, np.float32).reshape(16, 128).T


def pack_params(inp):
    p = np.zeros((128, PC), np.float32)
    for l in range(2):
        p[:, P_MIX + 16 * l:P_MIX + 16 * l + 16] = _vec16(inp["mix_norm"][l])
        p[:, P_FFN + 16 * l:P_FFN + 16 * l + 16] = _vec16(inp["ffn_norm"][l])
        p[:, P_MEMN + 16 * l:P_MEMN + 16 * l + 16] = _vec16(inp["mem_norm"][l])
        p[:, P_MQ + l] = inp["mem_q_norm"][l]
        p[:, P_MK + l] = inp["mem_k_norm"][l]
    p[:, P_KVN:P_KVN + 16] = _vec16(inp["kv_norm"])
    p[:, P_PSC:P_PSC + 12] = np.asarray(inp["pool_scale"][0], np.float32).reshape(12, 128).T
    p[:, P_KN] = np.tile(np.asarray(inp["k_norm"], np.float32), 2)
    p[:, P_QN] = np.tile(np.asarray(inp["q_norm"][0], np.float32), 2)
    p[:, P_SUB] = inp["subln_norm"][0]
    p[:, P_DL:P_DL + 256] = np.asarray(inp["diff_lambda"][0], np.float32).reshape(1, 256)
    for g, w in enumerate((2, 4, 8, 16)):
        t = np.arange(16)
        p[:, P_INVC + 16 * g:P_INVC + 16 * g + 16] = (1.0 / np.minimum(t + 1, w)).astype(np.float32)[None, :]
    p[:, P_IDENT:P_IDENT + 128] = np.eye(128, dtype=np.float32)
    return p


def const_bf16():
    c = np.zeros((128, NCB, 128), np.float32)
    c[:, C_IDENT] = np.eye(128)
    c[:, C_ONESD] = 1.0 / 2048.0
    c[:, C_ONESHD] = 1.0 / 128.0
    c[:, C_ONES1] = 1.0
    blk = np.zeros((128, 128))
    blk[:64, :64] = 1.0 / 64.0
    blk[64:, 64:] = 1.0 / 64.0
    c[:, C_BLK] = blk
    Rm = np.zeros((128, 128))
    for hb in (0, 64):
        for j in range(8):
            Rm[hb + j, hb + j + 8] = -1.0
            Rm[hb + j + 8, hb + j] = 1.0
    c[:, C_RT] = Rm.T
    k = np.arange(128)[:, None]
    q = np.arange(128)[None, :]
    c[:, C_TRI] = (k <= q).astype(np.float32)
    c[:, C_TRIB] = np.where(k <= q, 0.0, -30000.0)
    return c.reshape(128, NCB * 128).astype(ml_dtypes.bfloat16)


def rope_cs():
    pos = np.arange(S_, dtype=np.float32)
    inv = (np.float32(ROPE_THETA) ** (-(np.arange(8, dtype=np.float32) * np.float32(2.0)) / np.float32(16))).astype(np.float32)
    ang = (pos[:, None] * inv[None, :]).astype(np.float32)
    cs = np.zeros((128, 2, S_), np.float32)
    cs[:, 0, :] = 1.0
    for hb in (0, 64):
        for j in range(16):
            cs[hb + j, 0, :] = np.cos(ang[:, j % 8])
            cs[hb + j, 1, :] = np.sin(ang[:, j % 8])
    return cs


ARENA_BYTES = 187 * 1024
SUB = int(os.environ.get('MK_SUB', '99'))
SUB2 = int(os.environ.get('MK_SUB2', '99'))
NBW = 4


def build_nc(stop_after=99, dbg=False):
    nc = bass.Bass("TRN2", target_bir_lowering=False)
    okind = "ExternalOutput" if dbg else "Internal"

    def din(name, shape, dt=F32):
        return nc.dram_tensor(name, list(shape), dt, kind="ExternalInput").ap()

    x_in = din("x", [S_, D])
    mem_in = din("mem", [MEM, D])
    w_in = [din(f"w_in{l}", [D, D]) for l in range(2)]
    w_out = [din(f"w_out{l}", [D, D]) for l in range(2)]
    w_mkv = [din(f"w_mkv{l}", [D, 1024]) for l in range(2)]
    w_ff1 = [din(f"w_ff1{l}", [D, 4 * D]) for l in range(2)]
    w_ff2 = [din(f"w_ff2{l}", [4 * D, D]) for l in range(2)]
    pool_w = din("pool_w", [4, 384, 384])
    w_kv = din("w_kv", [D, 3072])
    params_d = din("params", [128, PC])
    cbf_d = din("cbf", [128, NCB * 128], BF16)
    cs_d = din("ropecs", [128, 2, S_])
    out_d = nc.dram_tensor("out", [S_, D], F32, kind="ExternalOutput").ap()

    xs = nc.dram_tensor("xs", [D, S_], F32, kind=okind).ap()
    memT = nc.dram_tensor("memT", [D, MEM], F32, kind=okind).ap()
    dbg_b = nc.dram_tensor("dbg_b", [128, 2048], BF16, kind=okind).ap()
    dbg_h = nc.dram_tensor("dbg_h", [128, 4096], BF16, kind=okind).ap()
    cat_d = nc.dram_tensor("cat_d", [16, 128, S_], BF16, kind=okind).ap()
    kT_d = nc.dram_tensor("kT_d", [12, 128, S_], BF16, kind=okind).ap()
    qT_d = nc.dram_tensor("qT_d", [12, 128, S_], BF16, kind=okind).ap()
    vT_d = nc.dram_tensor("vT_d", [12, 128, S_], BF16, kind=okind).ap()

    S = Sched()
    with ExitStack() as ctx:
        engsem = {e: ctx.enter_context(nc.semaphore("sem_" + e)) for e in ENGS}
        rings = {q: [ctx.enter_context(nc.semaphore(f"r_{q}_{i}")) for i in range(RING[q])] for q in RING}
        params_t = ctx.enter_context(nc.sbuf_tensor("sb_params", [128, PC], F32))
        cbf_t = ctx.enter_context(nc.sbuf_tensor("sb_cbf", [128, NCB, 128], BF16))
        small_t = ctx.enter_context(nc.sbuf_tensor("sb_small", [128, 16], F32))
        wring_t = ctx.enter_context(nc.sbuf_tensor("sb_wring", [128, NBW, 16, 128], BF16))
        arena_t = ctx.enter_context(nc.sbuf_tensor("sb_arena", [128, ARENA_BYTES], U8))
        ps_t = ctx.enter_context(nc.psum_tensor("ps_all", [128, 8, 512], F32))
        ps = ps_t[:, :, :]
        PB = [Tile(ps[:, b, :], f"ps{b}") for b in range(8)]
        for t_ in PB:
            t_.excl = True
        params = Tile(params_t[:, :], "params")
        cbf = Tile(cbf_t[:, :, :], "cbf")
        small = Tile(small_t[:, :], "small")
        wring = [Tile(wring_t[:, i, :, :], f"w{i}") for i in range(NBW)]
        AR = Arena(arena_t[:, :], ARENA_BYTES, S)
        XS = [[Tile(None, f"xs{dc}_{tb}") for tb in range(4)] for dc in range(KC)]
        MEMT = Tile(None, "memT")
        CAT = [Tile(None, f"cat{i}") for i in range(16)]
        KT = [Tile(None, f"kT{i}") for i in range(12)]
        QT = [Tile(None, f"qT{i}") for i in range(12)]
        VT = [Tile(None, f"vT{i}") for i in range(12)]

        def xs_cols(c0, n):
            return [XS[dc][tb] for dc in range(KC) for tb in range(c0 // 512, (c0 + n + 511) // 512)]

        def xs_rows(dc, c0, n):
            return [XS[dc][tb] for tb in range(c0 // 512, (c0 + n + 511) // 512)]
        st = {"w": 0, "bank": 0}

        def pcol(c, n=1):
            return params.ap[:, c:c + n]

        def cm(i):
            return cbf.ap[:, i, :]

        ident_f = params.ap[:, P_IDENT:P_IDENT + 128]
        EPSC = small.ap[:, 0:1]
        NEGLAM = small.ap[:, 1:2]
        SUBG = small.ap[:, 2:3]

        def next_group(n):
            res = st.get("reserved", ())
            for _ in range(9):
                b = (st["bank"] + n - 1) // n * n
                if b + n > 8:
                    b = 0
                st["bank"] = b + n
                if not any((x in res) for x in range(b, b + n)):
                    return list(range(b, b + n))
            raise AssertionError("no free PSUM group")

        def next_bank():
            return next_group(1)[0]

        def psg(g):
            return ps[:, g[0]:g[0] + len(g), :]

        def pbs(g):
            return [PB[b] for b in g]

        def load_w(src, nk):
            wt = wring[st["w"] % NBW]
            st["w"] += 1
            S.dma("pool", wt.ap[:, 0:nk, :], src.rearrange("(kc p) c -> p kc c", p=128), writes=[wt])
            return wt

        def rsqrt_act(out_ap, in_ap, rd, wr):
            S.A(lambda e: e.activation(out_ap, in_ap, AF.Ln, bias=EPSC, scale=1.0), reads=rd + [small], writes=[wr])
            S.A(lambda e: e.activation(out_ap, out_ap, AF.Exp, scale=-0.5), reads=[wr], writes=[wr])

        def recip_act(out_ap, in_ap, rd, wr):
            S.A(lambda e: e.activation(out_ap, in_ap, AF.Ln), reads=rd, writes=[wr])
            S.A(lambda e: e.activation(out_ap, out_ap, AF.Exp, scale=-1.0), reads=[wr], writes=[wr])

        def gemm_fm(wsrc, n_oc, nk, rhs_tiles_fn, rhs_fn, ntb, evac, ncols=512, use_hooks=False):
            pending = []
            hooks = {max(0, (3 * nk) // 8 - 1): 0, max(0, (3 * nk) // 4 - 1): 1} if (nk >= 8 and use_hooks) else {}

            def run_stage(h, oc):
                for p in list(pending):
                    if p[1] == h and p[2] < oc:
                        p[0][p[1]]()
                        p[1] += 1
                        p[2] = oc
                        if p[1] >= len(p[0]):
                            pending.remove(p)

            for oc in range(n_oc):
                gk = "g%d" % ntb
                gi = st.get(gk, 0)
                st[gk] = gi + 1
                ng = 8 // ntb
                g = list(range((gi % ng) * ntb, (gi % ng) * ntb + ntb))
                st["bank"] = g[-1] + 1
                st["reserved"] = set(g)
                nq = (nk + 15) // 16
                for kq in range(nq):
                    nkk = min(16, nk - kq * 16)
                    wt = load_w(wsrc(oc, kq), nkk)
                    for k16 in range(nkk):
                        kc = kq * 16 + k16
                        for tb in range(ntb):
                            S.T(lambda e, b=g[tb], wt=wt, k16=k16, kc=kc, tb=tb: e.matmul(
                                ps[:, b, 0:ncols], wt.ap[:, k16, :], rhs_fn(kc, tb),
                                start=(kc == 0), stop=(kc == nk - 1)),
                                reads=[wt] + rhs_tiles_fn(kc), writes=[PB[g[tb]]])
                        if kc in hooks:
                            run_stage(hooks[kc], oc)
                st["reserved"] = ()
                cur = evac(oc, g)
                if not hooks:
                    run_stage(0, oc + 1)
                    run_stage(1, oc + 1)
                if cur is not None:
                    pending.append([list(cur) if isinstance(cur, (list, tuple)) else [cur], 0, oc])
            k = n_oc
            while pending:
                k += 1
                run_stage(0, k)
                run_stage(1, k)

        def transpose_in(src, dst, ntok, dtiles):
            TW = min(512, ntok)
            ntt = TW // 128
            AR.reset()
            xin = [AR.alloc([ntt, D], F32, f"xin{i}") for i in range(2)]
            stg = [AR.alloc([TW], F32, f"stg{i}") for i in range(4)]
            k = 0
            for tb in range(ntok // TW):
                xt = xin[tb % 2]
                S.dma("sp", xt.ap, src[tb * TW:(tb + 1) * TW, :].rearrange("(tt p) d -> p tt d", p=128), writes=[xt])
                for dc in range(KC):
                    b = next_bank()
                    for tt in range(ntt):
                        S.T(lambda e, b=b, tt=tt, dc=dc, xt=xt: e.transpose(
                            ps[:, b, tt * 128:(tt + 1) * 128], xt.ap[:, tt, dc * 128:(dc + 1) * 128], ident_f),
                            reads=[xt, params], writes=[PB[b]])
                    sg = stg[k % 4]
                    eng = S.A if k % 2 == 0 else S.V
                    if k % 2 == 0:
                        S.A(lambda e, sg=sg, b=b: e.copy(sg.ap, ps[:, b, 0:TW]), reads=[PB[b]], writes=[sg])
                    else:
                        S.V(lambda e, sg=sg, b=b: e.tensor_copy(sg.ap, ps[:, b, 0:TW]), reads=[PB[b]], writes=[sg])
                    S.dma("sp", dst[dc * 128:(dc + 1) * 128, tb * TW:(tb + 1) * TW], sg.ap, reads=[sg], writes=dtiles(dc, tb))
                    k += 1

        def transpose_out():
            AR.reset()
            xin = [AR.alloc([KC, 512], F32, f"xo_in{i}") for i in range(2)]
            ost = [AR.alloc([D], F32, f"ost{i}") for i in range(2)]
            k = 0
            j = 0
            xsv = xs.rearrange("(dc p) t -> p dc t", p=128)
            for tb in range(4):
                xt = xin[tb % 2]
                S.dma("sp", xt.ap, xsv[:, :, tb * 512:(tb + 1) * 512], reads=xs_cols(tb * 512, 512), writes=[xt])
                for tt in range(4):
                    o = ost[j % 2]
                    j += 1
                    for dc4 in range(4):
                        b = next_bank()
                        for i in range(4):
                            S.T(lambda e, b=b, i=i, dc4=dc4, tt=tt, xt=xt: e.transpose(
                                ps[:, b, i * 128:(i + 1) * 128], xt.ap[:, dc4 * 4 + i, tt * 128:(tt + 1) * 128], ident_f),
                                reads=[xt, params], writes=[PB[b]])
                        if k % 2 == 0:
                            S.A(lambda e, o=o, b=b, dc4=dc4: e.copy(o.ap[:, dc4 * 512:(dc4 + 1) * 512], ps[:, b, :]),
                                reads=[PB[b]], writes=[o])
                        else:
                            S.V(lambda e, o=o, b=b, dc4=dc4: e.tensor_copy(o.ap[:, dc4 * 512:(dc4 + 1) * 512], ps[:, b, :]),
                                reads=[PB[b]], writes=[o])
                        k += 1
                    r0 = tb * 512 + tt * 128
                    S.dma("sp", out_d[r0:r0 + 128, :], o.ap, reads=[o])

        def norm_fm(src, gcol, h, tok0, T, TW, tmp_base, stiles=None):
            AR.reset(tmp_base)
            xts = [AR.alloc([KC, TW], F32, f"nx{i}") for i in range(2)]
            sq = AR.alloc([KC, TW], BF16, "nsq")
            rstd = AR.alloc([TW], F32, "nrstd")
            srcv = src.rearrange("(dc p) t -> p dc t", p=128)
            for tb in range(T // TW):
                xt = xts[tb % 2]
                c0 = tok0 + tb * TW
                S.dma("sp", xt.ap, srcv[:, :, c0:c0 + TW], reads=(stiles if stiles is not None else xs_cols(c0, TW)), writes=[xt])
                hk = KC // 2
                S.A(lambda e, xt=xt: e.activation(sq.ap[:, 0:hk, :], xt.ap[:, 0:hk, :], AF.Square), reads=[xt], writes=[sq])
                S.A(lambda e, xt=xt: e.activation(sq.ap[:, hk:KC, :], xt.ap[:, hk:KC, :], AF.Square), reads=[xt], writes=[sq])
                b = next_bank()
                for dc in range(KC):
                    S.T(lambda e, b=b, dc=dc: e.matmul(ps[:, b, 0:TW], cm(C_ONESD), sq.ap[:, dc, :],
                                                      start=(dc == 0), stop=(dc == KC - 1)),
                        reads=[sq, cbf], writes=[PB[b]])
                rsqrt_act(rstd.ap, ps[:, b, 0:TW], [PB[b]], rstd)
                for dc in range(KC):
                    S.V(lambda e, dc=dc, xt=xt, tb=tb: e.scalar_tensor_tensor(
                        h.ap[:, dc, tb * TW:(tb + 1) * TW], xt.ap[:, dc, :], pcol(gcol + dc), rstd.ap, ALU.mult, ALU.mult),
                        reads=[xt, rstd, params], writes=[h])

        def mem_prep(l, base):
            AR.reset(base)
            kT = AR.alloc([4, MEM], BF16, "mkT")
            vsb = AR.alloc([4, 2, 128], BF16, "mvsb")
            vsb.ap = AR.ap[:, vsb.off:vsb.off + vsb.size].bitcast(BF16).rearrange("p (h m d) -> p h m d", h=4, m=2)
            hm = AR.alloc([KC, MEM], BF16, "hm")
            kf = [AR.alloc([MEM], F32, f"mkf{i}") for i in range(2)]
            sqm = [AR.alloc([MEM], BF16, f"msq{i}") for i in range(2)]
            vf = [AR.alloc([MEM], BF16, f"mvf{i}") for i in range(2)]
            rs = AR.alloc([MEM], F32, "mrs")
            tmp_base = AR.top
            norm_fm(memT, P_MEMN + 16 * l, hm, 0, MEM, MEM, tmp_base, stiles=[MEMT])

            def evac(oc, g):
                b = g[0]
                if oc < 4:
                    k_f = kf[oc % 2]
                    s_q = sqm[oc % 2]
                    S.A(lambda e: e.copy(k_f.ap, ps[:, b, 0:MEM]), reads=[PB[b]], writes=[k_f])
                    S.A(lambda e: e.activation(s_q.ap, ps[:, b, 0:MEM], AF.Square), reads=[PB[b]], writes=[s_q])

                    def post():
                        b2 = next_bank()
                        S.T(lambda e: e.matmul(ps[:, b2, 0:MEM], cm(C_ONESHD), s_q.ap, start=True, stop=True),
                            reads=[s_q, cbf], writes=[PB[b2]])
                        rsqrt_act(rs.ap, ps[:, b2, 0:MEM], [PB[b2]], rs)
                        S.V(lambda e: e.scalar_tensor_tensor(kT.ap[:, oc, :], k_f.ap, pcol(P_MK + l), rs.ap, ALU.mult, ALU.mult),
                            reads=[k_f, rs, params], writes=[kT])
                    return post
                else:
                    hh = oc - 4
                    v_f = vf[oc % 2]
                    S.A(lambda e: e.copy(v_f.ap, ps[:, b, 0:MEM]), reads=[PB[b]], writes=[v_f])

                    def post():
                        b2 = next_bank()
                        pb16 = ps[:, b2, :].bitcast(BF16)
                        for mt in range(2):
                            S.T(lambda e, mt=mt: e.transpose(pb16[:, mt * 128:(mt + 1) * 128], v_f.ap[:, mt * 128:(mt + 1) * 128], cm(C_IDENT)),
                                reads=[v_f, cbf], writes=[PB[b2]])
                        S.V(lambda e: e.tensor_copy(vsb.ap[:, hh, :, :], pb16[:, 0:256].rearrange("p (m d) -> p m d", m=2)),
                            reads=[PB[b2]], writes=[vsb])
                    return post

            gemm_fm(lambda oc, kq: w_mkv[l][:, oc * 128:(oc + 1) * 128], 8, KC,
                    lambda kc: [hm], lambda kc, tb: hm.ap[:, kc, :], 1, evac, ncols=MEM)
            if dbg and l == 0:
                S.dma("sp", dbg_b[:, 0:1024], kT.ap.rearrange("p a b -> p (a b)"), reads=[kT])
                S.dma("sp", dbg_b[:, 1024:2048], vsb.ap.rearrange("p h m d -> p (h m d)"), reads=[vsb])
                S.dma("sp", dbg_h, hm.ap.rearrange("p a b -> p (a b)"), reads=[hm])
            return kT, vsb

        def mem_attn_bufs(rstd=None):
            d = {}
            d["rstd"] = rstd if rstd is not None else AR.alloc([S_], F32, "ma_rstd")
            d["qn"] = AR.alloc([S_], BF16, "ma_qn")
            d["E"] = [AR.alloc([512], BF16, f"ma_E{i}") for i in range(4)]
            d["rl"] = AR.alloc([512], F32, "ma_rl")
            d["osb"] = AR.alloc([512], F32, "ma_osb")
            d["ei"] = 0
            return d

        def mem_attn_head(l, hh, qf, sq, kT, vsb, mb, catst):
            g = next_group(4)
            for tb in range(4):
                S.T(lambda e, tb=tb: e.matmul(ps[:, g[tb], :], cm(C_ONESHD), sq.ap[:, tb * 512:(tb + 1) * 512], start=True, stop=True),
                    reads=[sq, cbf], writes=[PB[g[tb]]])
            rstd = mb["rstd"]
            qn = mb["qn"]
            rsqrt_act(rstd.ap.rearrange("p (b n) -> p b n", b=4), psg(g), pbs(g), rstd)
            S.V(lambda e: e.scalar_tensor_tensor(qn.ap, qf, pcol(P_MQ + l), rstd.ap, ALU.mult, ALU.mult),
                reads=[mb["qf_tile"], rstd, params], writes=[qn])
            sc = 128.0 ** -0.5
            if SUB2 <= 1:
                return
            for tb in range(4):
                Es = []
                for mt in range(2):
                    bS = next_bank()
                    S.T(lambda e, bS=bS, mt=mt, tb=tb: e.matmul(ps[:, bS, :], kT.ap[:, hh, mt * 128:(mt + 1) * 128],
                                                               qn.ap[:, tb * 512:(tb + 1) * 512], start=True, stop=True),
                        reads=[kT, qn], writes=[PB[bS]])
                    E = mb["E"][mb["ei"] % 4]
                    mb["ei"] += 1
                    S.A(lambda e, bS=bS, E=E: e.activation(E.ap, ps[:, bS, :], AF.Exp, scale=sc), reads=[PB[bS]], writes=[E])
                    Es.append(E)
                if SUB2 <= 2:
                    continue
                bo = next_bank()
                bl = next_bank()
                for mt in range(2):
                    S.T(lambda e, mt=mt, E=Es[mt], bo=bo: e.matmul(ps[:, bo, :], vsb.ap[:, hh, mt, :], E.ap, start=(mt == 0), stop=(mt == 1)),
                        reads=[vsb, Es[mt]], writes=[PB[bo]])
                for mt in range(2):
                    S.T(lambda e, mt=mt, E=Es[mt], bl=bl: e.matmul(ps[:, bl, :], cm(C_ONES1), E.ap, start=(mt == 0), stop=(mt == 1)),
                        reads=[cbf, Es[mt]], writes=[PB[bl]])
                if SUB2 <= 3:
                    continue
                rl = mb["rl"]
                osb = mb["osb"]
                recip_act(rl.ap, ps[:, bl, :], [PB[bl]], rl)
                S.V(lambda e, tb=tb, bo=bo: e.tensor_tensor(catst.ap[:, tb * 512:(tb + 1) * 512], ps[:, bo, :], rl.ap, ALU.mult),
                    reads=[PB[bo], rl], writes=[catst])
            if SUB2 <= 4:
                return
            S.dma("sp", cat_d[12 + hh], catst.ap, reads=[catst], writes=[CAT[12 + hh]])

        def out_proj(l):
            AR.reset()
            catq = [AR.alloc([4, S_], BF16, f"catq{i}") for i in range(4)]
            xo = [AR.alloc([S_], F32, f"xo{i}") for i in range(2)]
            cv = cat_d.rearrange("c p t -> p c t")
            for i in range(4):
                S.dma("sp", catq[i].ap, cv[:, 4 * i:4 * i + 4, :], reads=CAT[4 * i:4 * i + 4], writes=[catq[i]])

            def evac(oc, g):
                x_o = xo[oc % 2]
                S.dma("sp", x_o.ap, xs[oc * 128:(oc + 1) * 128, :], reads=xs_rows(oc, 0, S_), writes=[x_o])
                S.V(lambda e: e.tensor_tensor(x_o.ap.rearrange("p (b n) -> p b n", b=4), psg(g),
                                              x_o.ap.rearrange("p (b n) -> p b n", b=4), ALU.add),
                    reads=pbs(g) + [x_o], writes=[x_o])
                S.dma("sp", xs[oc * 128:(oc + 1) * 128, :], x_o.ap, reads=[x_o], writes=xs_rows(oc, 0, S_))
                return None

            gemm_fm(lambda oc, kq: w_out[l][:, oc * 128:(oc + 1) * 128], KC, KC,
                    lambda kc: [catq[kc // 4]], lambda kc, tb: catq[kc // 4].ap[:, kc % 4, tb * 512:(tb + 1) * 512], 4, evac)

        def ffn(l, fuse_out=False):
            HT = 1024
            ostc = {"i": 0}
            for half in range(2):
                AR.reset()
                z = [AR.alloc([HT], BF16, f"z{i}") for i in range(64)]
                hh = AR.alloc([KC, HT], BF16, "ffn_h")
                rr = [AR.alloc([HT], F32, f"ffn_r{i}") for i in range(2)]
                xo = [AR.alloc([HT], F32, f"ffn_xo{i}") for i in range(2)]
                ost = [AR.alloc([8, 128], F32, f"ffn_ost{i}") for i in range(2)] if fuse_out else None
                norm_fm(xs, P_FFN + 16 * l, hh, half * HT, HT, 512, 0)

                def evac1(fc, g):
                    r = rr[fc % 2]
                    S.A(lambda e: e.activation(r.ap.rearrange("p (b n) -> p b n", b=2), psg(g), AF.Relu),
                        reads=pbs(g), writes=[r])
                    S.V(lambda e: e.tensor_tensor(z[fc].ap, r.ap, r.ap, ALU.mult), reads=[r], writes=[z[fc]])
                    return None

                for i in range(64):
                    z[i] = AR.at(z[i].off, [HT], BF16, f"z{i}")
                gemm_fm(lambda fc, kq: w_ff1[l][:, fc * 128:(fc + 1) * 128], 64, KC,
                        lambda kc: [hh], lambda kc, tb: hh.ap[:, kc, tb * 512:(tb + 1) * 512], 2, evac1)

                def evac2(dc, g):
                    x_o = xo[dc % 2]
                    S.dma("sp", x_o.ap, xs[dc * 128:(dc + 1) * 128, half * HT:(half + 1) * HT], reads=xs_rows(dc, half * HT, HT), writes=[x_o])
                    S.V(lambda e: e.tensor_tensor(x_o.ap.rearrange("p (b n) -> p b n", b=2), psg(g),
                                                  x_o.ap.rearrange("p (b n) -> p b n", b=2), ALU.add),
                        reads=pbs(g) + [x_o], writes=[x_o])
                    if not fuse_out:
                        S.dma("sp", xs[dc * 128:(dc + 1) * 128, half * HT:(half + 1) * HT], x_o.ap, reads=[x_o], writes=xs_rows(dc, half * HT, HT))
                        return None

                    def tail(half=half):
                        o_t = ost[ostc["i"] % 2]
                        ostc["i"] += 1
                        g2 = next_group(2)
                        for t in range(8):
                            S.T(lambda e, t=t: e.transpose(ps[:, g2[t // 4], (t % 4) * 128:(t % 4 + 1) * 128],
                                                           x_o.ap[:, t * 128:(t + 1) * 128], ident_f),
                                reads=[x_o, params], writes=[PB[g2[t // 4]]])
                        S.A(lambda e: e.copy(o_t.ap[:, 0:4, :], ps[:, g2[0], :].rearrange("p (j d) -> p j d", j=4)),
                            reads=[PB[g2[0]]], writes=[o_t])
                        S.V(lambda e: e.tensor_copy(o_t.ap[:, 4:8, :], ps[:, g2[1], :].rearrange("p (j d) -> p j d", j=4)),
                            reads=[PB[g2[1]]], writes=[o_t])
                        S.dma("sp", out_d[half * HT:(half + 1) * HT, dc * 128:(dc + 1) * 128].rearrange("(t p) d -> p t d", p=128),
                              o_t.ap, reads=[o_t])
                    return tail

                gemm_fm(lambda dc, kq: w_ff2[l][kq * 2048:(kq + 1) * 2048, dc * 128:(dc + 1) * 128], KC, 64,
                        lambda kc: [z[kc]], lambda kc, tb: z[kc].ap[:, tb * 512:(tb + 1) * 512], 2, evac2, use_hooks=fuse_out)

        def mixer0():
            AR.reset()
            h = AR.alloc([KC, S_], BF16, "h0")
            base = AR.top
            kT, vsb = mem_prep(0, base)
            if SUB <= 1:
                return
            base2 = vsb.off + vsb.size
            AR.reset(base2)
            xin = [AR.alloc([4, D], F32, f"fx_in{i}") for i in range(2)]
            xblk = AR.alloc([KC, 512], F32, "fx_blk")
            fsq = AR.alloc([KC, 512], BF16, "fx_sq")
            frs = AR.alloc([512], F32, "fx_rstd")
            xsv = xs.rearrange("(dc p) t -> p dc t", p=128)
            kk_ = 0
            for tb in range(4):
                xt = xin[tb % 2]
                S.dma("sp", xt.ap, x_in[tb * 512:(tb + 1) * 512, :].rearrange("(tt p) d -> p tt d", p=128), writes=[xt])
                for dc in range(KC):
                    b = next_bank()
                    for tt in range(4):
                        S.T(lambda e, b=b, tt=tt, dc=dc, xt=xt: e.transpose(
                            ps[:, b, tt * 128:(tt + 1) * 128], xt.ap[:, tt, dc * 128:(dc + 1) * 128], ident_f),
                            reads=[xt, params], writes=[PB[b]])
                    if kk_ % 2 == 0:
                        S.A(lambda e, b=b, dc=dc: e.copy(xblk.ap[:, dc, :], ps[:, b, :]), reads=[PB[b]], writes=[xblk])
                    else:
                        S.V(lambda e, b=b, dc=dc: e.tensor_copy(xblk.ap[:, dc, :], ps[:, b, :]), reads=[PB[b]], writes=[xblk])
                    kk_ += 1
                S.dma("sp", xsv[:, :, tb * 512:(tb + 1) * 512], xblk.ap, reads=[xblk], writes=xs_cols(tb * 512, 512))
                hk = KC // 2
                S.A(lambda e: e.activation(fsq.ap[:, 0:hk, :], xblk.ap[:, 0:hk, :], AF.Square), reads=[xblk], writes=[fsq])
                S.A(lambda e: e.activation(fsq.ap[:, hk:KC, :], xblk.ap[:, hk:KC, :], AF.Square), reads=[xblk], writes=[fsq])
                b = next_bank()
                for dc in range(KC):
                    S.T(lambda e, b=b, dc=dc: e.matmul(ps[:, b, :], cm(C_ONESD), fsq.ap[:, dc, :], start=(dc == 0), stop=(dc == KC - 1)),
                        reads=[fsq, cbf], writes=[PB[b]])
                rsqrt_act(frs.ap, ps[:, b, :], [PB[b]], frs)
                for dc in range(KC):
                    S.V(lambda e, dc=dc, tb=tb: e.scalar_tensor_tensor(
                        h.ap[:, dc, tb * 512:(tb + 1) * 512], xblk.ap[:, dc, :], pcol(P_MIX + dc), frs.ap, ALU.mult, ALU.mult),
                        reads=[xblk, frs, params], writes=[h])
            if SUB <= 2:
                return
            AR.reset(base2)
            PADW = S_ + 16
            upad = [AR.alloc([PADW], F32, f"upad{i}") for i in range(2)]
            sb = [AR.alloc([PADW], F32, f"spp{i}") for i in range(2)]
            pooled = [AR.alloc([S_], BF16, f"pooled{i}") for i in range(6)]
            catst = [AR.alloc([S_], BF16, f"catst{i}") for i in range(2)]
            tmp16 = AR.alloc([16], F32, "tmp16")
            sq = [AR.alloc([S_], BF16, f"msq{i}") for i in range(2)]
            mb = mem_attn_bufs()
            pw_tiles = {}
            for t in upad + sb:
                S.V(lambda e, t=t: e.memset(t.ap[:, 0:16], 0.0), writes=[t])
            cst = {"i": 0}

            def token_out(gi):
                for oc2 in range(3):
                    wt = load_w(pool_w[gi, :, oc2 * 128:(oc2 + 1) * 128], 3)
                    g = next_group(4)
                    for ic in range(3):
                        pl = pooled[(gi * 3 + ic) % 6]
                        for tb in range(4):
                            S.T(lambda e, b=g[tb], ic=ic, tb=tb, wt=wt, pl=pl: e.matmul(
                                ps[:, b, :], wt.ap[:, ic, :], pl.ap[:, tb * 512:(tb + 1) * 512], start=(ic == 0), stop=(ic == 2)),
                                reads=[wt, pl], writes=[PB[g[tb]]])
                    cs_ = catst[cst["i"] % 2]
                    cst["i"] += 1
                    oc = gi * 3 + oc2
                    S.A(lambda e, cs_=cs_, g=g, oc=oc: e.activation(cs_.ap.rearrange("p (b n) -> p b n", b=4), psg(g),
                                                                    AF.Identity, scale=pcol(P_PSC + oc)),
                        reads=pbs(g) + [params], writes=[cs_])
                    S.dma("sp", cat_d[oc], cs_.ap, reads=[cs_], writes=[CAT[oc]])

            def evac(oc, g):
                up = upad[oc % 2]
                S.A(lambda e: e.copy(up.ap[:, 16:PADW].rearrange("p (b n) -> p b n", b=4), psg(g)), reads=pbs(g), writes=[up])
                if SUB <= 3:
                    return None
                if oc < 12:
                    gi = oc // 3
                    w = 2 << gi
                    src = up
                    for step in range(gi + 1):
                        sh = 1 << step
                        dst = sb[step % 2]
                        S.V(lambda e, src=src, dst=dst, sh=sh: e.tensor_tensor(
                            dst.ap[:, 16:PADW], src.ap[:, 16:PADW], src.ap[:, 16 - sh:PADW - sh], ALU.add),
                            reads=[src], writes=[dst])
                        src = dst
                    pl = pooled[oc % 6]
                    S.V(lambda e, src=src, pl=pl: e.scalar_tensor_tensor(
                        pl.ap, src.ap[:, 16:PADW], 1.0 / w, up.ap[:, 16:PADW], ALU.mult, ALU.subtract),
                        reads=[src, up], writes=[pl])
                    S.V(lambda e, src=src: e.tensor_tensor(tmp16.ap, src.ap[:, 16:32], pcol(P_INVC + 16 * gi, 16), ALU.mult),
                        reads=[src, params], writes=[tmp16])
                    S.V(lambda e, pl=pl: e.tensor_tensor(pl.ap[:, 0:16], tmp16.ap, up.ap[:, 16:32], ALU.subtract),
                        reads=[tmp16, up], writes=[pl])
                    if oc % 3 == 2 and SUB >= 5:
                        return lambda: token_out(gi)
                    return None
                else:
                    if SUB <= 5:
                        return None
                    hh = oc - 12
                    s_q = sq[oc % 2]
                    S.A(lambda e: e.activation(s_q.ap.rearrange("p (b n) -> p b n", b=4), psg(g), AF.Square),
                        reads=pbs(g), writes=[s_q])
                    cs_ = catst[cst["i"] % 2]
                    cst["i"] += 1

                    def post():
                        mb["qf_tile"] = up
                        mem_attn_head(0, hh, up.ap[:, 16:PADW], s_q, kT, vsb, mb, cs_)
                    return post

            gemm_fm(lambda oc, kq: w_in[0][:, oc * 128:(oc + 1) * 128], KC, KC,
                    lambda kc: [h], lambda kc, tb: h.ap[:, kc, tb * 512:(tb + 1) * 512], 4, evac)

        def hnr_bufs():
            d = {}
            d["cs"] = AR.alloc([2, S_], F32, "ropecs")
            S.dma("sp", d["cs"].ap, cs_d, writes=[d["cs"]])
            d["kf"] = [AR.alloc([S_], F32, f"kf{i}") for i in range(3)]
            d["sq"] = [AR.alloc([S_], BF16, f"ksq{i}") for i in range(2)]
            d["rstd"] = AR.alloc([S_], F32, "krstd")
            d["knb"] = [AR.alloc([S_], BF16, f"knb{i}") for i in range(2)]
            d["t1"] = AR.alloc([S_], F32, "kt1")
            d["t2"] = AR.alloc([S_], F32, "kt2")
            d["ob"] = [AR.alloc([S_], BF16, f"kob{i}") for i in range(2)]
            d["i"] = 0
            return d

        def hnr_evac(hb, g, gcol, dst, dtile):
            i = hb["i"]
            hb["i"] += 1
            kf = hb["kf"][i % 3]
            sq = hb["sq"][i % 2]
            ob = hb["ob"][i % 2]
            rstd, knb, t1, t2, cs = hb["rstd"], hb["knb"][i % 2], hb["t1"], hb["t2"], hb["cs"]
            v4 = lambda t: t.ap.rearrange("p (b n) -> p b n", b=4)
            v2 = lambda t, h: t.ap[:, h * 1024:(h + 1) * 1024].rearrange("p (b n) -> p b n", b=2)
            for h_ in range(2):
                ga = g[2 * h_:2 * h_ + 2]
                gd = g[2 * (1 - h_):2 * (1 - h_) + 2]
                S.A(lambda e, h_=h_, ga=ga: e.activation(v2(sq, h_), psg(ga), AF.Square), reads=pbs(ga), writes=[sq])
                S.V(lambda e, h_=h_, gd=gd: e.tensor_copy(v2(kf, 1 - h_), psg(gd)), reads=pbs(gd), writes=[kf])

            def postA():
                g2 = next_group(4)
                for tb in range(4):
                    S.T(lambda e, tb=tb: e.matmul(ps[:, g2[tb], :], cm(C_BLK), sq.ap[:, tb * 512:(tb + 1) * 512], start=True, stop=True),
                        reads=[sq, cbf], writes=[PB[g2[tb]]])
                rsqrt_act(v4(rstd), psg(g2), pbs(g2), rstd)
                S.V(lambda e: e.scalar_tensor_tensor(kf.ap, kf.ap, pcol(gcol), rstd.ap, ALU.mult, ALU.mult),
                    reads=[kf, rstd, params], writes=[kf])
                S.A(lambda e: e.copy(knb.ap, kf.ap), reads=[kf], writes=[knb])

            def postB():
                g3 = next_group(4)
                for tb in range(4):
                    S.T(lambda e, tb=tb: e.matmul(ps[:, g3[tb], :], cm(C_RT), knb.ap[:, tb * 512:(tb + 1) * 512], start=True, stop=True),
                        reads=[knb, cbf], writes=[PB[g3[tb]]])
                S.G(lambda e: e.tensor_tensor(t1.ap, kf.ap, cs.ap[:, 0, :], ALU.mult), reads=[kf, cs], writes=[t1])
                S.V(lambda e: e.tensor_tensor(v4(t2), psg(g3), cs.ap[:, 1, :].rearrange("p (b n) -> p b n", b=4), ALU.mult),
                    reads=pbs(g3) + [cs], writes=[t2])
                S.V(lambda e: e.tensor_tensor(ob.ap, t1.ap, t2.ap, ALU.add), reads=[t1, t2], writes=[ob])
                S.dma("sp", dst, ob.ap, reads=[ob], writes=[dtile])
            return [postA, postB]

        def kv_phase():
            AR.reset()
            h = AR.alloc([KC, S_], BF16, "hkv")
            base = AR.top
            norm_fm(xs, P_KVN, h, 0, S_, 512, base)
            AR.reset(base)
            hb = hnr_bufs()
            vst = [AR.alloc([S_], BF16, f"vst{i}") for i in range(2)]

            def evac(oc, g):
                if oc < 12:
                    return hnr_evac(hb, g, P_KN, kT_d[oc], KT[oc])
                v_s = vst[oc % 2]
                S.A(lambda e: e.copy(v_s.ap.rearrange("p (b n) -> p b n", b=4), psg(g)), reads=pbs(g), writes=[v_s])
                S.dma("sp", vT_d[oc - 12], v_s.ap, reads=[v_s], writes=[VT[oc - 12]])
                return None

            gemm_fm(lambda oc, kq: w_kv[:, oc * 128:(oc + 1) * 128], 24, KC,
                    lambda kc: [h], lambda kc, tb: h.ap[:, kc, tb * 512:(tb + 1) * 512], 4, evac, use_hooks=True)

        def q_phase():
            AR.reset()
            h = AR.alloc([KC, S_], BF16, "hq")
            base = AR.top
            kT, vsb = mem_prep(1, base)
            base2 = vsb.off + vsb.size
            norm_fm(xs, P_MIX + 16, h, 0, S_, 512, base2)
            AR.reset(base2)
            hb = hnr_bufs()
            qf = hb["kf"]
            sq = hb["sq"]
            catst = [AR.alloc([S_], BF16, f"qcatst{i}") for i in range(2)]
            mb = mem_attn_bufs(hb["rstd"])

            def evac(oc, g):
                if oc < 12:
                    return hnr_evac(hb, g, P_QN, qT_d[oc], QT[oc])
                hh = oc - 12
                q_f = qf[hb["i"] % 3]
                s_q = sq[hb["i"] % 2]
                hb["i"] += 1
                cs_ = catst[oc % 2]
                S.A(lambda e: e.copy(q_f.ap.rearrange("p (b n) -> p b n", b=4), psg(g)), reads=pbs(g), writes=[q_f])
                S.A(lambda e: e.activation(s_q.ap.rearrange("p (b n) -> p b n", b=4), psg(g), AF.Square), reads=pbs(g), writes=[s_q])

                def post():
                    mb["qf_tile"] = q_f
                    mem_attn_head(1, hh, q_f.ap, s_q, kT, vsb, mb, cs_)
                return post

            gemm_fm(lambda oc, kq: w_in[1][:, oc * 128:(oc + 1) * 128], KC, KC,
                    lambda kc: [h], lambda kc, tb: h.ap[:, kc, tb * 512:(tb + 1) * 512], 4, evac, use_hooks=True)

        def attn_phase():
            AR.reset()
            kk = [[AR.alloc([S_], BF16, f"kk{s}_{i}") for i in range(2)] for s in range(2)]
            zq = [[AR.alloc([S_], BF16, f"zq{m}_{hf}") for hf in range(2)] for m in range(2)]
            for m in range(2):
                for hf in range(2):
                    z0 = 64 * (1 - hf)
                    S.V(lambda e, m=m, hf=hf, z0=z0: e.memset(zq[m][hf].ap[z0:z0 + 64, :], 0.0), writes=[zq[m][hf]])
            vT = [AR.alloc([S_], BF16, f"vT{i}") for i in range(2)]
            vtok = [AR.alloc([16, 128], BF16, f"vtok{i}") for i in range(2)]
            E = [AR.alloc([512], BF16, f"E{i}") for i in range(8)]
            r12 = AR.alloc([1024], F32, "r12")
            aa = AR.alloc([512], F32, "aa")
            bb = AR.alloc([512], F32, "bb")
            oo = AR.alloc([512], F32, "oo")
            osq = AR.alloc([512], BF16, "osq")
            rs = AR.alloc([512], F32, "ars")
            catst = [AR.alloc([S_], BF16, f"acat{i}") for i in range(2)]
            sbank = {"i": 0}
            spair = {"i": 0}
            ei = {"i": 0}
            sc = 64.0 ** -0.5
            BO = (4, 5)
            BL = (6, 7)

            def sb_next():
                b = sbank["i"] % 4
                sbank["i"] += 1
                return b

            for hp in range(6):
                k2t = kk[hp % 2]
                S.dma("sp", k2t[0].ap, kT_d[hp], reads=[KT[hp]], writes=[k2t[0]])
                S.dma("sp", k2t[1].ap, kT_d[6 + hp], reads=[KT[6 + hp]], writes=[k2t[1]])
                for m in range(2):
                    for hf in range(2):
                        q0 = 64 * hf
                        S.dma("sp", zq[m][hf].ap[q0:q0 + 64, :], qT_d[6 * m + hp][q0:q0 + 64, :],
                              reads=[QT[6 * m + hp]], writes=[zq[m][hf]])
                for half in range(2):
                    hd = 2 * hp + half
                    p0 = 64 * half
                    v_T = vT[hd % 2]
                    v_k = vtok[hd % 2]
                    S.dma("sp", v_T.ap, vT_d[hd], reads=[VT[hd]], writes=[v_T])
                    for t4i in range(4):
                        b = sb_next()
                        pb16 = ps[:, b, :].bitcast(BF16)
                        for j in range(4):
                            tt = t4i * 4 + j
                            S.T(lambda e, pb16=pb16, j=j, tt=tt, v_T=v_T: e.transpose(
                                pb16[:, j * 128:(j + 1) * 128], v_T.ap[:, tt * 128:(tt + 1) * 128], cm(C_IDENT)),
                                reads=[v_T, cbf], writes=[PB[b]])
                        S.V(lambda e, pb16=pb16, t4i=t4i, v_k=v_k: e.tensor_copy(
                            v_k.ap[:, t4i * 4:(t4i + 1) * 4, :], pb16[:, 0:512].rearrange("p (j d) -> p j d", j=4)),
                            reads=[PB[b]], writes=[v_k])
                    cs_ = catst[hd % 2]
                    DEPTH = 3
                    queue = []
                    tails = []

                    def issue_S(qb, kb, m, half=half, k2t=k2t):
                        nkb = 4 * qb + 4
                        i_d = kb - 4 * qb
                        c0 = 128 * i_d if i_d > 0 else 0
                        qt = zq[m][half]
                        kt = k2t[m]
                        bS = sb_next()
                        diag = i_d >= 0
                        S.T(lambda e: e.matmul(ps[:, bS, c0:512], kt.ap[:, kb * 128:(kb + 1) * 128],
                                               qt.ap[:, qb * 512 + c0:(qb + 1) * 512], start=True, stop=(not diag)),
                            reads=[kt, qt], writes=[PB[bS]])
                        if diag:
                            S.T(lambda e: e.matmul(ps[:, bS, c0:c0 + 128], cm(C_IDENT), cm(C_TRIB), start=False, stop=True),
                                reads=[cbf], writes=[PB[bS]])
                        Et = E[ei["i"] % len(E)]
                        ei["i"] += 1
                        S.A(lambda e: e.activation(Et.ap[:, c0:512], ps[:, bS, c0:512], AF.Exp, scale=sc),
                            reads=[PB[bS]], writes=[Et])
                        return (qb, kb, m, c0, nkb, Et)

                    def post_a(qb):
                        S.V(lambda e: e.tensor_copy(aa.ap, ps[:, BO[0], :]), reads=[PB[BO[0]]], writes=[aa])
                        S.V(lambda e: e.tensor_copy(bb.ap, ps[:, BO[1], :]), reads=[PB[BO[1]]], writes=[bb])
                        recip_act(r12.ap.rearrange("p (b n) -> p b n", b=2), ps[:, BL[0]:BL[1] + 1, :], [PB[BL[0]], PB[BL[1]]], r12)
                        S.V(lambda e: e.tensor_tensor(aa.ap, aa.ap, r12.ap[:, 0:512], ALU.mult), reads=[aa, r12], writes=[aa])
                        S.V(lambda e: e.tensor_tensor(bb.ap, bb.ap, r12.ap[:, 512:1024], ALU.mult), reads=[bb, r12], writes=[bb])
                        S.V(lambda e: e.scalar_tensor_tensor(oo.ap, bb.ap, NEGLAM, aa.ap, ALU.mult, ALU.add),
                            reads=[bb, aa, small], writes=[oo])
                        S.V(lambda e: e.tensor_tensor(osq.ap, oo.ap, oo.ap, ALU.mult), reads=[oo], writes=[osq])

                    def post_b(qb, cs_=cs_):
                        bs2 = sb_next()
                        S.T(lambda e: e.matmul(ps[:, bs2, :], cm(C_ONESHD), osq.ap, start=True, stop=True),
                            reads=[osq, cbf], writes=[PB[bs2]])
                        rsqrt_act(rs.ap, ps[:, bs2, :], [PB[bs2]], rs)
                        S.V(lambda e: e.scalar_tensor_tensor(cs_.ap[:, qb * 512:(qb + 1) * 512], oo.ap, SUBG, rs.ap, ALU.mult, ALU.mult),
                            reads=[oo, rs, small], writes=[cs_])

                    def issue_PV(info, v_k=v_k):
                        qb, kb, m, c0, nkb, Et = info
                        S.T(lambda e: e.matmul(ps[:, BO[m], c0:512], v_k.ap[:, kb, :], Et.ap[:, c0:512],
                                               start=(kb == 0), stop=(kb == nkb - 1)),
                            reads=[v_k, Et], writes=[PB[BO[m]]])
                        S.T(lambda e: e.matmul(ps[:, BL[m], c0:512], cm(C_ONES1), Et.ap[:, c0:512],
                                               start=(kb == 0), stop=(kb == nkb - 1)),
                            reads=[cbf, Et], writes=[PB[BL[m]]])
                        for t in tails:
                            t[0] -= 1
                        while tails and tails[0][0] <= 0:
                            tails.pop(0)[1]()
                        if kb == nkb - 1 and m == 1:
                            post_a(qb)
                            tails.append([5, lambda qb=qb: post_b(qb)])

                    steps = [(qb, kb, m) for qb in range(4) for kb in range(4 * qb + 4) for m in range(2)]
                    for st_ in steps:
                        queue.append(issue_S(*st_))
                        if len(queue) > DEPTH:
                            issue_PV(queue.pop(0))
                    while queue:
                        issue_PV(queue.pop(0))
                    while tails:
                        tails.pop(0)[1]()
                    S.dma("sp", cat_d[hd], cs_.ap, reads=[cs_], writes=[CAT[hd]])

        S.dma("sp", params.ap, params_d, writes=[params])
        S.dma("sp", cbf.ap, cbf_d.rearrange("p (a b) -> p a b", a=NCB), writes=[cbf])
        S.V(lambda e: e.memset(EPSC, EPS), writes=[small])
        AR.reset()
        lt = AR.alloc([64], F32, "lam_tmp")
        l4 = AR.alloc([4], F32, "lam4")
        dl = lambda i: params.ap[:, P_DL + 64 * i:P_DL + 64 * (i + 1)]
        S.V(lambda e: e.tensor_tensor(lt.ap, dl(0), dl(1), ALU.mult), reads=[params], writes=[lt])
        S.V(lambda e: e.reduce_sum(l4.ap[:, 0:1], lt.ap, axis=AX.X), reads=[lt], writes=[l4])
        S.V(lambda e: e.tensor_tensor(lt.ap, dl(2), dl(3), ALU.mult), reads=[params, l4], writes=[lt])
        S.V(lambda e: e.reduce_sum(l4.ap[:, 1:2], lt.ap, axis=AX.X), reads=[lt], writes=[l4])
        S.A(lambda e: e.activation(l4.ap[:, 2:4], l4.ap[:, 0:2], AF.Exp), reads=[l4], writes=[l4])
        S.V(lambda e: e.tensor_tensor(NEGLAM, l4.ap[:, 3:4], l4.ap[:, 2:3], ALU.subtract), reads=[l4], writes=[small])
        S.V(lambda e: e.tensor_scalar_add(NEGLAM, NEGLAM, -LAM_INIT), reads=[small], writes=[small])
        S.V(lambda e: e.tensor_scalar_mul(SUBG, pcol(P_SUB), 1.0 - LAM_INIT), reads=[params], writes=[small])

        phases = [
            lambda: None,
            lambda: transpose_in(mem_in, memT, MEM, lambda dc, tb: [MEMT]),
            mixer0,
            lambda: out_proj(0),
            lambda: ffn(0),
            kv_phase,
            q_phase,
            attn_phase,
            lambda: out_proj(1),
            lambda: ffn(1, fuse_out=True),
        ]
        for i, ph in enumerate(phases):
            if i > stop_after:
                break
            ph()
        if stop_after < len(phases) - 1:
            transpose_out()

        with nc.Block() as block:
            S.emit(block, engsem, rings)
    return nc


_CACHE = {}


def kernel(**inputs):
    stop_after = int(os.environ.get("MK_STOP", "99"))
    dbg = os.environ.get("MK_DBG", "0") == "1"
    inp = {k: np.asarray(v) for k, v in inputs.items()}
    key = (stop_after, dbg)
    if key not in _CACHE:
        _CACHE[key] = build_nc(stop_after, dbg)
    nc = _CACHE[key]
    params = pack_params(inp)
    cbf = const_bf16()
    cs = rope_cs()
    shared = {
        "pool_w": np.ascontiguousarray(inp["pool_w"][0], np.float32),
        "w_kv": np.ascontiguousarray(inp["w_kv"], np.float32),
        "params": params, "cbf": cbf, "ropecs": cs,
    }
    for l in range(2):
        shared[f"w_in{l}"] = np.ascontiguousarray(inp["w_in"][l], np.float32)
        shared[f"w_out{l}"] = np.ascontiguousarray(inp["w_out"][l], np.float32)
        shared[f"w_mkv{l}"] = np.ascontiguousarray(inp["w_mem_kv"][l], np.float32)
        shared[f"w_ff1{l}"] = np.ascontiguousarray(inp["w_ff1"][l], np.float32)
        shared[f"w_ff2{l}"] = np.ascontiguousarray(inp["w_ff2"][l], np.float32)
    in_maps = []
    for b in range(8):
        m = dict(shared)
        m["x"] = np.ascontiguousarray(inp["x"][b], np.float32)
        m["mem"] = np.ascontiguousarray(inp["mem"][b], np.float32)
        in_maps.append(m)
    ncores = int(os.environ.get("MK_CORES", "8"))
    res = run_bass_kernel_spmd(nc, in_maps[:ncores], core_ids=list(range(ncores)))
    if dbg:
        kernel.last_results = res.results
    out = np.stack([np.asarray(r["out"], np.float32) for r in res.results], axis=0)
    return out
```

```python
import math
import os
from contextlib import ExitStack

import ml_dtypes
import numpy as np

import concourse.bass as bass
import concourse.mybir as mybir
from concourse.bass_utils import run_bass_kernel_spmd

F32 = mybir.dt.float32
BF16 = mybir.dt.bfloat16
U8 = mybir.dt.uint8
AF = mybir.ActivationFunctionType
ALU = mybir.AluOpType
AX = mybir.AxisListType

S_ = 2048
D = 2048
KC = 16
MEM = 256
EPS = 1e-6
LAM_INIT = 0.8 - 0.6 * math.exp(-0.3 * 1)
ROPE_THETA = 500000.0


class Tile:
    __slots__ = ("ap", "w", "r", "rd", "name", "off", "size", "excl")

    def __init__(self, ap, name="", off=-1, size=0):
        self.excl = False
        self.ap = ap
        self.w = None
        self.r = {}
        self.rd = []
        self.name = name
        self.off = off
        self.size = size


class Op:
    __slots__ = ("eng", "fn", "deps", "marked", "semval", "is_dma", "slot", "dval", "idx")

    def __init__(self, eng, fn, is_dma=False):
        self.eng = eng
        self.fn = fn
        self.deps = set()
        self.marked = False
        self.semval = 0
        self.is_dma = is_dma
        self.slot = -1
        self.dval = 0
        self.idx = 0


ENGS = ("pe", "act", "dve", "pool", "sp")
RING = {"sp": 16, "pool": 8}


class Sched:
    def __init__(self):
        self.ops = {e: [] for e in ENGS}
        self.dma_count = {q: 0 for q in RING}
        self.dma_last = {q: [None] * RING[q] for q in RING}

    def add(self, eng, fn, reads=(), writes=(), dma=False):
        o = Op(eng, fn, dma)
        deps = o.deps
        for t in reads:
            if t.w is not None:
                deps.add(t.w)
            if t.excl:
                for en, ro in t.r.items():
                    if en != eng:
                        deps.add(ro)
        for t in writes:
            if t.w is not None:
                deps.add(t.w)
            deps.update(t.r.values())
            deps.update(t.rd)
        for t in reads:
            if dma:
                t.rd.append(o)
            else:
                t.r[eng] = o
        for t in writes:
            t.w = o
            t.r = {}
            t.rd = []
        if dma:
            k = self.dma_count[eng]
            R = RING[eng]
            o.slot = k % R
            o.dval = 16 * (k // R + 1)
            prev = self.dma_last[eng][o.slot]
            if prev is not None:
                deps.add(prev)
            self.dma_last[eng][o.slot] = o
            self.dma_count[eng] = k + 1
        if eng == "pe" and not dma:
            o.deps = {d for d in deps if d.is_dma or d.eng != "pe"}
        o.deps.discard(o)
        o.idx = len(self.ops[eng])
        self.ops[eng].append(o)
        return o

    def T(self, fn, reads=(), writes=()):
        return self.add("pe", fn, reads, writes)

    def A(self, fn, reads=(), writes=()):
        return self.add("act", fn, reads, writes)

    def V(self, fn, reads=(), writes=()):
        return self.add("dve", fn, reads, writes)

    def G(self, fn, reads=(), writes=()):
        return self.add("pool", fn, reads, writes)

    def dma(self, q, out_ap, in_ap, reads=(), writes=()):
        return self.add(q, lambda e: e.dma_start(out=out_ap, in_=in_ap), reads, writes, dma=True)

    def finalize(self):
        for e in ENGS:
            for o in self.ops[e]:
                for d in o.deps:
                    if not d.is_dma:
                        d.marked = True
        for e in ENGS:
            c = 0
            for o in self.ops[e]:
                if (not o.is_dma) and o.marked:
                    c += 1
                    o.semval = c

    def emit(self, block, engsem, rings):
        self.finalize()
        sched = self

        def run(ename, e):
            seen = {}
            for o in sched.ops[ename]:
                for d in sorted(o.deps, key=lambda d: (d.eng, d.idx)):
                    if d.is_dma:
                        key = (d.eng, d.slot)
                        sem = rings[d.eng][d.slot]
                        val = d.dval
                    else:
                        key = d.eng
                        sem = engsem[d.eng]
                        val = d.semval
                    if seen.get(key, 0) >= val:
                        continue
                    seen[key] = val
                    e.wait_ge(sem, val)
                ins = o.fn(e)
                if o.is_dma:
                    ins.then_inc(rings[ename][o.slot], 16)
                elif o.marked:
                    ins.then_inc(engsem[ename], 1)
            if ename in RING:
                for s, last in enumerate(sched.dma_last[ename]):
                    if last is not None and seen.get((ename, s), 0) < last.dval:
                        e.wait_ge(rings[ename][s], last.dval)

        @block.tensor
        def _(e):
            run("pe", e)

        @block.scalar
        def _(e):
            run("act", e)

        @block.vector
        def _(e):
            run("dve", e)

        @block.gpsimd
        def _(e):
            run("pool", e)

        @block.sync
        def _(e):
            run("sp", e)


class Arena:
    def __init__(self, ap, nbytes, sched):
        self.ap = ap
        self.nbytes = nbytes
        self.live = []
        self.S = sched
        self.top = 0

    def reset(self, top=0):
        self.top = top

    def alloc(self, shape, dt, name=""):
        isz = 4 if dt == F32 else 2
        n = int(np.prod(shape)) * isz
        off = (self.top + 31) // 32 * 32
        assert off + n <= self.nbytes, (name, off, n, self.nbytes)
        self.top = off + n
        return self.at(off, shape, dt, name)

    def at(self, off, shape, dt, name=""):
        isz = 4 if dt == F32 else 2
        n = int(np.prod(shape)) * isz
        assert off + n <= self.nbytes, (name, off, n, self.nbytes)
        ap = self.ap[:, off:off + n].bitcast(dt)
        if len(shape) == 2:
            ap = ap.rearrange("p (a b) -> p a b", a=shape[0])
        elif len(shape) == 3:
            ap = ap.rearrange("p (a b c) -> p a b c", a=shape[0], b=shape[1])
        t = Tile(ap, name, off, n)
        keep = []
        ops = self.S.ops
        for o in self.live:
            if o.off < off + n and off < o.off + o.size:
                cands = list(o.r.values())
                if o.w is not None:
                    if o.w.is_dma:
                        t.rd.append(o.w)
                    else:
                        cands.append(o.w)
                for c in cands:
                    cur = t.r.get(c.eng)
                    if cur is None or cur.idx < c.idx:
                        t.r[c.eng] = c
                t.rd.extend(o.rd)
                if not (off <= o.off and o.off + o.size <= off + n):
                    keep.append(o)
            else:
                keep.append(o)
        keep.append(t)
        self.live = keep
        return t


P_MIX = 0
P_FFN = 32
P_MEMN = 64
P_KVN = 96
P_PSC = 112
P_MQ = 124
P_MK = 126
P_KN = 128
P_QN = 129
P_SUB = 130
P_DL = 131
P_INVC = 387
P_IDENT = 451
PC = 579

C_IDENT, C_ONESD, C_ONESHD, C_ONES1, C_BLK, C_RT, C_TRI, C_TRIB = range(8)
NCB = 8


def _vec16(v):
    return np.asarray(v, np.float32).reshape(16, 128).T


def pack_params(inp):
    p = np.zeros((128, PC), np.float32)
    for l in range(2):
        p[:, P_MIX + 16 * l:P_MIX + 16 * l + 16] = _vec16(inp["mix_norm"][l])
        p[:, P_FFN + 16 * l:P_FFN + 16 * l + 16] = _vec16(inp["ffn_norm"][l])
        p[:, P_MEMN + 16 * l:P_MEMN + 16 * l + 16] = _vec16(inp["mem_norm"][l])
        p[:, P_MQ + l] = inp["mem_q_norm"][l]
        p[:, P_MK + l] = inp["mem_k_norm"][l]
    p[:, P_KVN:P_KVN + 16] = _vec16(inp["kv_norm"])
    p[:, P_PSC:P_PSC + 12] = np.asarray(inp["pool_scale"][0], np.float32).reshape(12, 128).T
    p[:, P_KN] = np.tile(np.asarray(inp["k_norm"], np.float32), 2)
    p[:, P_QN] = np.tile(np.asarray(inp["q_norm"][0], np.float32), 2)
    p[:, P_SUB] = inp["subln_norm"][0]
    p[:, P_DL:P_DL + 256] = np.asarray(inp["diff_lambda"][0], np.float32).reshape(1, 256)
    for g, w in enumerate((2, 4, 8, 16)):
        t = np.arange(16)
        p[:, P_INVC + 16 * g:P_INVC + 16 * g + 16] = (1.0 / np.minimum(t + 1, w)).astype(np.float32)[None, :]
    p[:, P_IDENT:P_IDENT + 128] = np.eye(128, dtype=np.float32)
    return p


def const_bf16():
    c = np.zeros((128, NCB, 128), np.float32)
    c[:, C_IDENT] = np.eye(128)
    c[:, C_ONESD] = 1.0 / 2048.0
    c[:, C_ONESHD] = 1.0 / 128.0
    c[:, C_ONES1] = 1.0
    blk = np.zeros((128, 128))
    blk[:64, :64] = 1.0 / 64.0
    blk[64:, 64:] = 1.0 / 64.0
    c[:, C_BLK] = blk
    Rm = np.zeros((128, 128))
    for hb in (0, 64):
        for j in range(8):
            Rm[hb + j, hb + j + 8] = -1.0
            Rm[hb + j + 8, hb + j] = 1.0
    c[:, C_RT] = Rm.T
    k = np.arange(128)[:, None]
    q = np.arange(128)[None, :]
    c[:, C_TRI] = (k <= q).astype(np.float32)
    c[:, C_TRIB] = np.where(k <= q, 0.0, -30000.0)
    return c.reshape(128, NCB * 128).astype(ml_dtypes.bfloat16)


def rope_cs():
    pos = np.arange(S_, dtype=np.float32)
    inv = (np.float32(ROPE_THETA) ** (-(np.arange(8, dtype=np.float32) * np.float32(2.0)) / np.float32(16))).astype(np.float32)
    ang = (pos[:, None] * inv[None, :]).astype(np.float32)
    cs = np.zeros((128, 2, S_), np.float32)
    cs[:, 0, :] = 1.0
    for hb in (0, 64):
        for j in range(16):
            cs[hb + j, 0, :] = np.cos(ang[:, j % 8])
            cs[hb + j, 1, :] = np.sin(ang[:, j % 8])
    return cs


ARENA_BYTES = 187 * 1024
SUB = int(os.environ.get('MK_SUB', '99'))
SUB2 = int(os.environ.get('MK_SUB2', '99'))
NBW = 4


def build_nc(stop_after=99, dbg=False):
    nc = bass.Bass("TRN2", target_bir_lowering=False)
    okind = "ExternalOutput" if dbg else "Internal"

    def din(name, shape, dt=F32):
        return nc.dram_tensor(name, list(shape), dt, kind="ExternalInput").ap()

    x_in = din("x", [S_, D])
    mem_in = din("mem", [MEM, D])
    w_in = [din(f"w_in{l}", [D, D]) for l in range(2)]
    w_out = [din(f"w_out{l}", [D, D]) for l in range(2)]
    w_mkv = [din(f"w_mkv{l}", [D, 1024]) for l in range(2)]
    w_ff1 = [din(f"w_ff1{l}", [D, 4 * D]) for l in range(2)]
    w_ff2 = [din(f"w_ff2{l}", [4 * D, D]) for l in range(2)]
    pool_w = din("pool_w", [4, 384, 384])
    w_kv = din("w_kv", [D, 3072])
    params_d = din("params", [128, PC])
    cbf_d = din("cbf", [128, NCB * 128], BF16)
    cs_d = din("ropecs", [128, 2, S_])
    out_d = nc.dram_tensor("out", [S_, D], F32, kind="ExternalOutput").ap()

    xs = nc.dram_tensor("xs", [D, S_], F32, kind=okind).ap()
    memT = nc.dram_tensor("memT", [D, MEM], F32, kind=okind).ap()
    dbg_b = nc.dram_tensor("dbg_b", [128, 2048], BF16, kind=okind).ap()
    dbg_h = nc.dram_tensor("dbg_h", [128, 4096], BF16, kind=okind).ap()
    cat_d = nc.dram_tensor("cat_d", [16, 128, S_], BF16, kind=okind).ap()
    kT_d = nc.dram_tensor("kT_d", [12, 128, S_], BF16, kind=okind).ap()
    qT_d = nc.dram_tensor("qT_d", [12, 128, S_], BF16, kind=okind).ap()
    vT_d = nc.dram_tensor("vT_d", [12, 128, S_], BF16, kind=okind).ap()

    S = Sched()
    with ExitStack() as ctx:
        engsem = {e: ctx.enter_context(nc.semaphore("sem_" + e)) for e in ENGS}
        rings = {q: [ctx.enter_context(nc.semaphore(f"r_{q}_{i}")) for i in range(RING[q])] for q in RING}
        params_t = ctx.enter_context(nc.sbuf_tensor("sb_params", [128, PC], F32))
        cbf_t = ctx.enter_context(nc.sbuf_tensor("sb_cbf", [128, NCB, 128], BF16))
        small_t = ctx.enter_context(nc.sbuf_tensor("sb_small", [128, 16], F32))
        wring_t = ctx.enter_context(nc.sbuf_tensor("sb_wring", [128, NBW, 16, 128], BF16))
        arena_t = ctx.enter_context(nc.sbuf_tensor("sb_arena", [128, ARENA_BYTES], U8))
        ps_t = ctx.enter_context(nc.psum_tensor("ps_all", [128, 8, 512], F32))
        ps = ps_t[:, :, :]
        PB = [Tile(ps[:, b, :], f"ps{b}") for b in range(8)]
        for t_ in PB:
            t_.excl = True
        params = Tile(params_t[:, :], "params")
        cbf = Tile(cbf_t[:, :, :], "cbf")
        small = Tile(small_t[:, :], "small")
        wring = [Tile(wring_t[:, i, :, :], f"w{i}") for i in range(NBW)]
        AR = Arena(arena_t[:, :], ARENA_BYTES, S)
        XS = [[Tile(None, f"xs{dc}_{tb}") for tb in range(4)] for dc in range(KC)]
        MEMT = Tile(None, "memT")
        CAT = [Tile(None, f"cat{i}") for i in range(16)]
        KT = [Tile(None, f"kT{i}") for i in range(12)]
        QT = [Tile(None, f"qT{i}") for i in range(12)]
        VT = [Tile(None, f"vT{i}") for i in range(12)]

        def xs_cols(c0, n):
            return [XS[dc][tb] for dc in range(KC) for tb in range(c0 // 512, (c0 + n + 511) // 512)]

        def xs_rows(dc, c0, n):
            return [XS[dc][tb] for tb in range(c0 // 512, (c0 + n + 511) // 512)]
        st = {"w": 0, "bank": 0}

        def pcol(c, n=1):
            return params.ap[:, c:c + n]

        def cm(i):
            return cbf.ap[:, i, :]

        ident_f = params.ap[:, P_IDENT:P_IDENT + 128]
        EPSC = small.ap[:, 0:1]
        NEGLAM = small.ap[:, 1:2]
        SUBG = small.ap[:, 2:3]

        def next_group(n):
            res = st.get("reserved", ())
            for _ in range(9):
                b = (st["bank"] + n - 1) // n * n
                if b + n > 8:
                    b = 0
                st["bank"] = b + n
                if not any((x in res) for x in range(b, b + n)):
                    return list(range(b, b + n))
            raise AssertionError("no free PSUM group")

        def next_bank():
            return next_group(1)[0]

        def psg(g):
            return ps[:, g[0]:g[0] + len(g), :]

        def pbs(g):
            return [PB[b] for b in g]

        def load_w(src, nk):
            wt = wring[st["w"] % NBW]
            st["w"] += 1
            S.dma("pool", wt.ap[:, 0:nk, :], src.rearrange("(kc p) c -> p kc c", p=128), writes=[wt])
            return wt

        def rsqrt_act(out_ap, in_ap, rd, wr):
            S.A(lambda e: e.activation(out_ap, in_ap, AF.Ln, bias=EPSC, scale=1.0), reads=rd + [small], writes=[wr])
            S.A(lambda e: e.activation(out_ap, out_ap, AF.Exp, scale=-0.5), reads=[wr], writes=[wr])

        def recip_act(out_ap, in_ap, rd, wr):
            S.A(lambda e: e.activation(out_ap, in_ap, AF.Ln), reads=rd, writes=[wr])
            S.A(lambda e: e.activation(out_ap, out_ap, AF.Exp, scale=-1.0), reads=[wr], writes=[wr])

        def gemm_fm(wsrc, n_oc, nk, rhs_tiles_fn, rhs_fn, ntb, evac, ncols=512, use_hooks=False):
            pending = []
            hooks = {max(0, (3 * nk) // 8 - 1): 0, max(0, (3 * nk) // 4 - 1): 1} if (nk >= 8 and use_hooks) else {}

            def run_stage(h, oc):
                for p in list(pending):
                    if p[1] == h and p[2] < oc:
                        p[0][p[1]]()
                        p[1] += 1
                        p[2] = oc
                        if p[1] >= len(p[0]):
                            pending.remove(p)

            for oc in range(n_oc):
                gk = "g%d" % ntb
                gi = st.get(gk, 0)
                st[gk] = gi + 1
                ng = 8 // ntb
                g = list(range((gi % ng) * ntb, (gi % ng) * ntb + ntb))
                st["bank"] = g[-1] + 1
                st["reserved"] = set(g)
                nq = (nk + 15) // 16
                for kq in range(nq):
                    nkk = min(16, nk - kq * 16)
                    wt = load_w(wsrc(oc, kq), nkk)
                    for k16 in range(nkk):
                        kc = kq * 16 + k16
                        for tb in range(ntb):
                            S.T(lambda e, b=g[tb], wt=wt, k16=k16, kc=kc, tb=tb: e.matmul(
                                ps[:, b, 0:ncols], wt.ap[:, k16, :], rhs_fn(kc, tb),
                                start=(kc == 0), stop=(kc == nk - 1)),
                                reads=[wt] + rhs_tiles_fn(kc), writes=[PB[g[tb]]])
                        if kc in hooks:
                            run_stage(hooks[kc], oc)
                st["reserved"] = ()
                cur = evac(oc, g)
                if not hooks:
                    run_stage(0, oc + 1)
                    run_stage(1, oc + 1)
                if cur is not None:
                    pending.append([list(cur) if isinstance(cur, (list, tuple)) else [cur], 0, oc])
            k = n_oc
            while pending:
                k += 1
                run_stage(0, k)
                run_stage(1, k)

        def transpose_in(src, dst, ntok, dtiles):
            TW = min(512, ntok)
            ntt = TW // 128
            AR.reset()
            xin = [AR.alloc([ntt, D], F32, f"xin{i}") for i in range(2)]
            stg = [AR.alloc([TW], F32, f"stg{i}") for i in range(4)]
            k = 0
            for tb in range(ntok // TW):
                xt = xin[tb % 2]
                S.dma("sp", xt.ap, src[tb * TW:(tb + 1) * TW, :].rearrange("(tt p) d -> p tt d", p=128), writes=[xt])
                for dc in range(KC):
                    b = next_bank()
                    for tt in range(ntt):
                        S.T(lambda e, b=b, tt=tt, dc=dc, xt=xt: e.transpose(
                            ps[:, b, tt * 128:(tt + 1) * 128], xt.ap[:, tt, dc * 128:(dc + 1) * 128], ident_f),
                            reads=[xt, params], writes=[PB[b]])
                    sg = stg[k % 4]
                    eng = S.A if k % 2 == 0 else S.V
                    if k % 2 == 0:
                        S.A(lambda e, sg=sg, b=b: e.copy(sg.ap, ps[:, b, 0:TW]), reads=[PB[b]], writes=[sg])
                    else:
                        S.V(lambda e, sg=sg, b=b: e.tensor_copy(sg.ap, ps[:, b, 0:TW]), reads=[PB[b]], writes=[sg])
                    S.dma("sp", dst[dc * 128:(dc + 1) * 128, tb * TW:(tb + 1) * TW], sg.ap, reads=[sg], writes=dtiles(dc, tb))
                    k += 1

        def transpose_out():
            AR.reset()
            xin = [AR.alloc([KC, 512], F32, f"xo_in{i}") for i in range(2)]
            ost = [AR.alloc([D], F32, f"ost{i}") for i in range(2)]
            k = 0
            j = 0
            xsv = xs.rearrange("(dc p) t -> p dc t", p=128)
            for tb in range(4):
                xt = xin[tb % 2]
                S.dma("sp", xt.ap, xsv[:, :, tb * 512:(tb + 1) * 512], reads=xs_cols(tb * 512, 512), writes=[xt])
                for tt in range(4):
                    o = ost[j % 2]
                    j += 1
                    for dc4 in range(4):
                        b = next_bank()
                        for i in range(4):
                            S.T(lambda e, b=b, i=i, dc4=dc4, tt=tt, xt=xt: e.transpose(
                                ps[:, b, i * 128:(i + 1) * 128], xt.ap[:, dc4 * 4 + i, tt * 128:(tt + 1) * 128], ident_f),
                                reads=[xt, params], writes=[PB[b]])
                        if k % 2 == 0:
                            S.A(lambda e, o=o, b=b, dc4=dc4: e.copy(o.ap[:, dc4 * 512:(dc4 + 1) * 512], ps[:, b, :]),
                                reads=[PB[b]], writes=[o])
                        else:
                            S.V(lambda e, o=o, b=b, dc4=dc4: e.tensor_copy(o.ap[:, dc4 * 512:(dc4 + 1) * 512], ps[:, b, :]),
                                reads=[PB[b]], writes=[o])
                        k += 1
                    r0 = tb * 512 + tt * 128
                    S.dma("sp", out_d[r0:r0 + 128, :], o.ap, reads=[o])

        def norm_fm(src, gcol, h, tok0, T, TW, tmp_base, stiles=None):
            AR.reset(tmp_base)
            xts = [AR.alloc([KC, TW], F32, f"nx{i}") for i in range(2)]
            sq = AR.alloc([KC, TW], BF16, "nsq")
            rstd = AR.alloc([TW], F32, "nrstd")
            srcv = src.rearrange("(dc p) t -> p dc t", p=128)
            for tb in range(T // TW):
                xt = xts[tb % 2]
                c0 = tok0 + tb * TW
                S.dma("sp", xt.ap, srcv[:, :, c0:c0 + TW], reads=(stiles if stiles is not None else xs_cols(c0, TW)), writes=[xt])
                hk = KC // 2
                S.A(lambda e, xt=xt: e.activation(sq.ap[:, 0:hk, :], xt.ap[:, 0:hk, :], AF.Square), reads=[xt], writes=[sq])
                S.A(lambda e, xt=xt: e.activation(sq.ap[:, hk:KC, :], xt.ap[:, hk:KC, :], AF.Square), reads=[xt], writes=[sq])
                b = next_bank()
                for dc in range(KC):
                    S.T(lambda e, b=b, dc=dc: e.matmul(ps[:, b, 0:TW], cm(C_ONESD), sq.ap[:, dc, :],
                                                      start=(dc == 0), stop=(dc == KC - 1)),
                        reads=[sq, cbf], writes=[PB[b]])
                rsqrt_act(rstd.ap, ps[:, b, 0:TW], [PB[b]], rstd)
                for dc in range(KC):
                    S.V(lambda e, dc=dc, xt=xt, tb=tb: e.scalar_tensor_tensor(
                        h.ap[:, dc, tb * TW:(tb + 1) * TW], xt.ap[:, dc, :], pcol(gcol + dc), rstd.ap, ALU.mult, ALU.mult),
                        reads=[xt, rstd, params], writes=[h])

        def mem_prep(l, base):
            AR.reset(base)
            kT = AR.alloc([4, MEM], BF16, "mkT")
            vsb = AR.alloc([4, 2, 128], BF16, "mvsb")
            vsb.ap = AR.ap[:, vsb.off:vsb.off + vsb.size].bitcast(BF16).rearrange("p (h m d) -> p h m d", h=4, m=2)
            hm = AR.alloc([KC, MEM], BF16, "hm")
            kf = [AR.alloc([MEM], F32, f"mkf{i}") for i in range(2)]
            sqm = [AR.alloc([MEM], BF16, f"msq{i}") for i in range(2)]
            vf = [AR.alloc([MEM], BF16, f"mvf{i}") for i in range(2)]
            rs = AR.alloc([MEM], F32, "mrs")
            tmp_base = AR.top
            norm_fm(memT, P_MEMN + 16 * l, hm, 0, MEM, MEM, tmp_base, stiles=[MEMT])

            def evac(oc, g):
                b = g[0]
                if oc < 4:
                    k_f = kf[oc % 2]
                    s_q = sqm[oc % 2]
                    S.A(lambda e: e.copy(k_f.ap, ps[:, b, 0:MEM]), reads=[PB[b]], writes=[k_f])
                    S.A(lambda e: e.activation(s_q.ap, ps[:, b, 0:MEM], AF.Square), reads=[PB[b]], writes=[s_q])

                    def post():
                        b2 = next_bank()
                        S.T(lambda e: e.matmul(ps[:, b2, 0:MEM], cm(C_ONESHD), s_q.ap, start=True, stop=True),
                            reads=[s_q, cbf], writes=[PB[b2]])
                        rsqrt_act(rs.ap, ps[:, b2, 0:MEM], [PB[b2]], rs)
                        S.V(lambda e: e.scalar_tensor_tensor(kT.ap[:, oc, :], k_f.ap, pcol(P_MK + l), rs.ap, ALU.mult, ALU.mult),
                            reads=[k_f, rs, params], writes=[kT])
                    return post
                else:
                    hh = oc - 4
                    v_f = vf[oc % 2]
                    S.A(lambda e: e.copy(v_f.ap, ps[:, b, 0:MEM]), reads=[PB[b]], writes=[v_f])

                    def post():
                        b2 = next_bank()
                        pb16 = ps[:, b2, :].bitcast(BF16)
                        for mt in range(2):
                            S.T(lambda e, mt=mt: e.transpose(pb16[:, mt * 128:(mt + 1) * 128], v_f.ap[:, mt * 128:(mt + 1) * 128], cm(C_IDENT)),
                                reads=[v_f, cbf], writes=[PB[b2]])
                        S.V(lambda e: e.tensor_copy(vsb.ap[:, hh, :, :], pb16[:, 0:256].rearrange("p (m d) -> p m d", m=2)),
                            reads=[PB[b2]], writes=[vsb])
                    return post

            gemm_fm(lambda oc, kq: w_mkv[l][:, oc * 128:(oc + 1) * 128], 8, KC,
                    lambda kc: [hm], lambda kc, tb: hm.ap[:, kc, :], 1, evac, ncols=MEM)
            if dbg and l == 0:
                S.dma("sp", dbg_b[:, 0:1024], kT.ap.rearrange("p a b -> p (a b)"), reads=[kT])
                S.dma("sp", dbg_b[:, 1024:2048], vsb.ap.rearrange("p h m d -> p (h m d)"), reads=[vsb])
                S.dma("sp", dbg_h, hm.ap.rearrange("p a b -> p (a b)"), reads=[hm])
            return kT, vsb

        def mem_attn_bufs(rstd=None):
            d = {}
            d["rstd"] = rstd if rstd is not None else AR.alloc([S_], F32, "ma_rstd")
            d["qn"] = AR.alloc([S_], BF16, "ma_qn")
            d["E"] = [AR.alloc([512], BF16, f"ma_E{i}") for i in range(4)]
            d["rl"] = AR.alloc([512], F32, "ma_rl")
            d["osb"] = AR.alloc([512], F32, "ma_osb")
            d["ei"] = 0
            return d

        def mem_attn_head(l, hh, qf, sq, kT, vsb, mb, catst):
            g = next_group(4)
            for tb in range(4):
                S.T(lambda e, tb=tb: e.matmul(ps[:, g[tb], :], cm(C_ONESHD), sq.ap[:, tb * 512:(tb + 1) * 512], start=True, stop=True),
                    reads=[sq, cbf], writes=[PB[g[tb]]])
            rstd = mb["rstd"]
            qn = mb["qn"]
            rsqrt_act(rstd.ap.rearrange("p (b n) -> p b n", b=4), psg(g), pbs(g), rstd)
            S.V(lambda e: e.scalar_tensor_tensor(qn.ap, qf, pcol(P_MQ + l), rstd.ap, ALU.mult, ALU.mult),
                reads=[mb["qf_tile"], rstd, params], writes=[qn])
            sc = 128.0 ** -0.5
            if SUB2 <= 1:
                return
            for tb in range(4):
                Es = []
                for mt in range(2):
                    bS = next_bank()
                    S.T(lambda e, bS=bS, mt=mt, tb=tb: e.matmul(ps[:, bS, :], kT.ap[:, hh, mt * 128:(mt + 1) * 128],
                                                               qn.ap[:, tb * 512:(tb + 1) * 512], start=True, stop=True),
                        reads=[kT, qn], writes=[PB[bS]])
                    E = mb["E"][mb["ei"] % 4]
                    mb["ei"] += 1
                    S.A(lambda e, bS=bS, E=E: e.activation(E.ap, ps[:, bS, :], AF.Exp, scale=sc), reads=[PB[bS]], writes=[E])
                    Es.append(E)
                if SUB2 <= 2:
                    continue
                bo = next_bank()
                bl = next_bank()
                for mt in range(2):
                    S.T(lambda e, mt=mt, E=Es[mt], bo=bo: e.matmul(ps[:, bo, :], vsb.ap[:, hh, mt, :], E.ap, start=(mt == 0), stop=(mt == 1)),
                        reads=[vsb, Es[mt]], writes=[PB[bo]])
                for mt in range(2):
                    S.T(lambda e, mt=mt, E=Es[mt], bl=bl: e.matmul(ps[:, bl, :], cm(C_ONES1), E.ap, start=(mt == 0), stop=(mt == 1)),
                        reads=[cbf, Es[mt]], writes=[PB[bl]])
                if SUB2 <= 3:
                    continue
                rl = mb["rl"]
                osb = mb["osb"]
                recip_act(rl.ap, ps[:, bl, :], [PB[bl]], rl)
                S.V(lambda e, tb=tb, bo=bo: e.tensor_tensor(catst.ap[:, tb * 512:(tb + 1) * 512], ps[:, bo, :], rl.ap, ALU.mult),
                    reads=[PB[bo], rl], writes=[catst])
            if SUB2 <= 4:
                return
            S.dma("sp", cat_d[12 + hh], catst.ap, reads=[catst], writes=[CAT[12 + hh]])

        def out_proj(l):
            AR.reset()
            catq = [AR.alloc([4, S_], BF16, f"catq{i}") for i in range(4)]
            xo = [AR.alloc([S_], F32, f"xo{i}") for i in range(2)]
            cv = cat_d.rearrange("c p t -> p c t")
            for i in range(4):
                S.dma("sp", catq[i].ap, cv[:, 4 * i:4 * i + 4, :], reads=CAT[4 * i:4 * i + 4], writes=[catq[i]])

            def evac(oc, g):
                x_o = xo[oc % 2]
                S.dma("sp", x_o.ap, xs[oc * 128:(oc + 1) * 128, :], reads=xs_rows(oc, 0, S_), writes=[x_o])
                S.V(lambda e: e.tensor_tensor(x_o.ap.rearrange("p (b n) -> p b n", b=4), psg(g),
                                              x_o.ap.rearrange("p (b n) -> p b n", b=4), ALU.add),
                    reads=pbs(g) + [x_o], writes=[x_o])
                S.dma("sp", xs[oc * 128:(oc + 1) * 128, :], x_o.ap, reads=[x_o], writes=xs_rows(oc, 0, S_))
                return None

            gemm_fm(lambda oc, kq: w_out[l][:, oc * 128:(oc + 1) * 128], KC, KC,
                    lambda kc: [catq[kc // 4]], lambda kc, tb: catq[kc // 4].ap[:, kc % 4, tb * 512:(tb + 1) * 512], 4, evac)

        def ffn(l, fuse_out=False):
            HT = 1024
            ostc = {"i": 0}
            for half in range(2):
                AR.reset()
                z = [AR.alloc([HT], BF16, f"z{i}") for i in range(64)]
                hh = AR.alloc([KC, HT], BF16, "ffn_h")
                rr = [AR.alloc([HT], F32, f"ffn_r{i}") for i in range(2)]
                xo = [AR.alloc([HT], F32, f"ffn_xo{i}") for i in range(2)]
                ost = [AR.alloc([8, 128], F32, f"ffn_ost{i}") for i in range(2)] if fuse_out else None
                norm_fm(xs, P_FFN + 16 * l, hh, half * HT, HT, 512, 0)

                def evac1(fc, g):
                    r = rr[fc % 2]
                    S.A(lambda e: e.activation(r.ap.rearrange("p (b n) -> p b n", b=2), psg(g), AF.Relu),
                        reads=pbs(g), writes=[r])
                    S.V(lambda e: e.tensor_tensor(z[fc].ap, r.ap, r.ap, ALU.mult), reads=[r], writes=[z[fc]])
                    return None

                for i in range(64):
                    z[i] = AR.at(z[i].off, [HT], BF16, f"z{i}")
                gemm_fm(lambda fc, kq: w_ff1[l][:, fc * 128:(fc + 1) * 128], 64, KC,
                        lambda kc: [hh], lambda kc, tb: hh.ap[:, kc, tb * 512:(tb + 1) * 512], 2, evac1)

                def evac2(dc, g):
                    x_o = xo[dc % 2]
                    S.dma("sp", x_o.ap, xs[dc * 128:(dc + 1) * 128, half * HT:(half + 1) * HT], reads=xs_rows(dc, half * HT, HT), writes=[x_o])
                    S.V(lambda e: e.tensor_tensor(x_o.ap.rearrange("p (b n) -> p b n", b=2), psg(g),
                                                  x_o.ap.rearrange("p (b n) -> p b n", b=2), ALU.add),
                        reads=pbs(g) + [x_o], writes=[x_o])
                    if not fuse_out:
                        S.dma("sp", xs[dc * 128:(dc + 1) * 128, half * HT:(half + 1) * HT], x_o.ap, reads=[x_o], writes=xs_rows(dc, half * HT, HT))
                        return None

                    def tail(half=half):
                        o_t = ost[ostc["i"] % 2]
                        ostc["i"] += 1
                        g2 = next_group(2)
                        for t in range(8):
                            S.T(lambda e, t=t: e.transpose(ps[:, g2[t // 4], (t % 4) * 128:(t % 4 + 1) * 128],
                                                           x_o.ap[:, t * 128:(t + 1) * 128], ident_f),
                                reads=[x_o, params], writes=[PB[g2[t // 4]]])
                        S.A(lambda e: e.copy(o_t.ap[:, 0:4, :], ps[:, g2[0], :].rearrange("p (j d) -> p j d", j=4)),
                            reads=[PB[g2[0]]], writes=[o_t])
                        S.V(lambda e: e.tensor_copy(o_t.ap[:, 4:8, :], ps[:, g2[1], :].rearrange("p (j d) -> p j d", j=4)),
                            reads=[PB[g2[1]]], writes=[o_t])
                        S.dma("sp", out_d[half * HT:(half + 1) * HT, dc * 128:(dc + 1) * 128].rearrange("(t p) d -> p t d", p=128),
                              o_t.ap, reads=[o_t])
                    return tail

                gemm_fm(lambda dc, kq: w_ff2[l][kq * 2048:(kq + 1) * 2048, dc * 128:(dc + 1) * 128], KC, 64,
                        lambda kc: [z[kc]], lambda kc, tb: z[kc].ap[:, tb * 512:(tb + 1) * 512], 2, evac2, use_hooks=fuse_out)

        def mixer0():
            AR.reset()
            h = AR.alloc([KC, S_], BF16, "h0")
            base = AR.top
            kT, vsb = mem_prep(0, base)
            if SUB <= 1:
                return
            base2 = vsb.off + vsb.size
            AR.reset(base2)
            xin = [AR.alloc([4, D], F32, f"fx_in{i}") for i in range(2)]
            xblk = AR.alloc([KC, 512], F32, "fx_blk")
            fsq = AR.alloc([KC, 512], BF16, "fx_sq")
            frs = AR.alloc([512], F32, "fx_rstd")
            xsv = xs.rearrange("(dc p) t -> p dc t", p=128)
            kk_ = 0
            for tb in range(4):
                xt = xin[tb % 2]
                S.dma("sp", xt.ap, x_in[tb * 512:(tb + 1) * 512, :].rearrange("(tt p) d -> p tt d", p=128), writes=[xt])
                for dc in range(KC):
                    b = next_bank()
                    for tt in range(4):
                        S.T(lambda e, b=b, tt=tt, dc=dc, xt=xt: e.transpose(
                            ps[:, b, tt * 128:(tt + 1) * 128], xt.ap[:, tt, dc * 128:(dc + 1) * 128], ident_f),
                            reads=[xt, params], writes=[PB[b]])
                    if kk_ % 2 == 0:
                        S.A(lambda e, b=b, dc=dc: e.copy(xblk.ap[:, dc, :], ps[:, b, :]), reads=[PB[b]], writes=[xblk])
                    else:
                        S.V(lambda e, b=b, dc=dc: e.tensor_copy(xblk.ap[:, dc, :], ps[:, b, :]), reads=[PB[b]], writes=[xblk])
                    kk_ += 1
                S.dma("sp", xsv[:, :, tb * 512:(tb + 1) * 512], xblk.ap, reads=[xblk], writes=xs_cols(tb * 512, 512))
                hk = KC // 2
                S.A(lambda e: e.activation(fsq.ap[:, 0:hk, :], xblk.ap[:, 0:hk, :], AF.Square), reads=[xblk], writes=[fsq])
                S.A(lambda e: e.activation(fsq.ap[:, hk:KC, :], xblk.ap[:, hk:KC, :], AF.Square), reads=[xblk], writes=[fsq])
                b = next_bank()
                for dc in range(KC):
                    S.T(lambda e, b=b, dc=dc: e.matmul(ps[:, b, :], cm(C_ONESD), fsq.ap[:, dc, :], start=(dc == 0), stop=(dc == KC - 1)),
                        reads=[fsq, cbf], writes=[PB[b]])
                rsqrt_act(frs.ap, ps[:, b, :], [PB[b]], frs)
                for dc in range(KC):
                    S.V(lambda e, dc=dc, tb=tb: e.scalar_tensor_tensor(
                        h.ap[:, dc, tb * 512:(tb + 1) * 512], xblk.ap[:, dc, :], pcol(P_MIX + dc), frs.ap, ALU.mult, ALU.mult),
                        reads=[xblk, frs, params], writes=[h])
            if SUB <= 2:
                return
            AR.reset(base2)
            PADW = S_ + 16
            upad = [AR.alloc([PADW], F32, f"upad{i}") for i in range(2)]
            sb = [AR.alloc([PADW], F32, f"spp{i}") for i in range(2)]
            pooled = [AR.alloc([S_], BF16, f"pooled{i}") for i in range(6)]
            catst = [AR.alloc([S_], BF16, f"catst{i}") for i in range(2)]
            tmp16 = AR.alloc([16], F32, "tmp16")
            sq = [AR.alloc([S_], BF16, f"msq{i}") for i in range(2)]
            mb = mem_attn_bufs()
            pw_tiles = {}
            for t in upad + sb:
                S.V(lambda e, t=t: e.memset(t.ap[:, 0:16], 0.0), writes=[t])
            cst = {"i": 0}

            def token_out(gi):
                for oc2 in range(3):
                    wt = load_w(pool_w[gi, :, oc2 * 128:(oc2 + 1) * 128], 3)
                    g = next_group(4)
                    for ic in range(3):
                        pl = pooled[(gi * 3 + ic) % 6]
                        for tb in range(4):
                            S.T(lambda e, b=g[tb], ic=ic, tb=tb, wt=wt, pl=pl: e.matmul(
                                ps[:, b, :], wt.ap[:, ic, :], pl.ap[:, tb * 512:(tb + 1) * 512], start=(ic == 0), stop=(ic == 2)),
                                reads=[wt, pl], writes=[PB[g[tb]]])
                    cs_ = catst[cst["i"] % 2]
                    cst["i"] += 1
                    oc = gi * 3 + oc2
                    S.A(lambda e, cs_=cs_, g=g, oc=oc: e.activation(cs_.ap.rearrange("p (b n) -> p b n", b=4), psg(g),
                                                                    AF.Identity, scale=pcol(P_PSC + oc)),
                        reads=pbs(g) + [params], writes=[cs_])
                    S.dma("sp", cat_d[oc], cs_.ap, reads=[cs_], writes=[CAT[oc]])

            def evac(oc, g):
                up = upad[oc % 2]
                S.A(lambda e: e.copy(up.ap[:, 16:PADW].rearrange("p (b n) -> p b n", b=4), psg(g)), reads=pbs(g), writes=[up])
                if SUB <= 3:
                    return None
                if oc < 12:
                    gi = oc // 3
                    w = 2 << gi
                    src = up
                    for step in range(gi + 1):
                        sh = 1 << step
                        dst = sb[step % 2]
                        S.V(lambda e, src=src, dst=dst, sh=sh: e.tensor_tensor(
                            dst.ap[:, 16:PADW], src.ap[:, 16:PADW], src.ap[:, 16 - sh:PADW - sh], ALU.add),
                            reads=[src], writes=[dst])
                        src = dst
                    pl = pooled[oc % 6]
                    S.V(lambda e, src=src, pl=pl: e.scalar_tensor_tensor(
                        pl.ap, src.ap[:, 16:PADW], 1.0 / w, up.ap[:, 16:PADW], ALU.mult, ALU.subtract),
                        reads=[src, up], writes=[pl])
                    S.V(lambda e, src=src: e.tensor_tensor(tmp16.ap, src.ap[:, 16:32], pcol(P_INVC + 16 * gi, 16), ALU.mult),
                        reads=[src, params], writes=[tmp16])
                    S.V(lambda e, pl=pl: e.tensor_tensor(pl.ap[:, 0:16], tmp16.ap, up.ap[:, 16:32], ALU.subtract),
                        reads=[tmp16, up], writes=[pl])
                    if oc % 3 == 2 and SUB >= 5:
                        return lambda: token_out(gi)
                    return None
                else:
                    if SUB <= 5:
                        return None
                    hh = oc - 12
                    s_q = sq[oc % 2]
                    S.A(lambda e: e.activation(s_q.ap.rearrange("p (b n) -> p b n", b=4), psg(g), AF.Square),
                        reads=pbs(g), writes=[s_q])
                    cs_ = catst[cst["i"] % 2]
                    cst["i"] += 1

                    def post():
                        mb["qf_tile"] = up
                        mem_attn_head(0, hh, up.ap[:, 16:PADW], s_q, kT, vsb, mb, cs_)
                    return post

            gemm_fm(lambda oc, kq: w_in[0][:, oc * 128:(oc + 1) * 128], KC, KC,
                    lambda kc: [h], lambda kc, tb: h.ap[:, kc, tb * 512:(tb + 1) * 512], 4, evac)

        def hnr_bufs():
            d = {}
            d["cs"] = AR.alloc([2, S_], F32, "ropecs")
            S.dma("sp", d["cs"].ap, cs_d, writes=[d["cs"]])
            d["kf"] = [AR.alloc([S_], F32, f"kf{i}") for i in range(3)]
            d["sq"] = [AR.alloc([S_], BF16, f"ksq{i}") for i in range(2)]
            d["rstd"] = AR.alloc([S_], F32, "krstd")
            d["knb"] = [AR.alloc([S_], BF16, f"knb{i}") for i in range(2)]
            d["t1"] = AR.alloc([S_], F32, "kt1")
            d["t2"] = AR.alloc([S_], F32, "kt2")
            d["ob"] = [AR.alloc([S_], BF16, f"kob{i}") for i in range(2)]
            d["i"] = 0
            return d

        def hnr_evac(hb, g, gcol, dst, dtile):
            i = hb["i"]
            hb["i"] += 1
            kf = hb["kf"][i % 3]
            sq = hb["sq"][i % 2]
            ob = hb["ob"][i % 2]
            rstd, knb, t1, t2, cs = hb["rstd"], hb["knb"][i % 2], hb["t1"], hb["t2"], hb["cs"]
            v4 = lambda t: t.ap.rearrange("p (b n) -> p b n", b=4)
            v2 = lambda t, h: t.ap[:, h * 1024:(h + 1) * 1024].rearrange("p (b n) -> p b n", b=2)
            for h_ in range(2):
                ga = g[2 * h_:2 * h_ + 2]
                gd = g[2 * (1 - h_):2 * (1 - h_) + 2]
                S.A(lambda e, h_=h_, ga=ga: e.activation(v2(sq, h_), psg(ga), AF.Square), reads=pbs(ga), writes=[sq])
                S.V(lambda e, h_=h_, gd=gd: e.tensor_copy(v2(kf, 1 - h_), psg(gd)), reads=pbs(gd), writes=[kf])

            def postA():
                g2 = next_group(4)
                for tb in range(4):
                    S.T(lambda e, tb=tb: e.matmul(ps[:, g2[tb], :], cm(C_BLK), sq.ap[:, tb * 512:(tb + 1) * 512], start=True, stop=True),
                        reads=[sq, cbf], writes=[PB[g2[tb]]])
                rsqrt_act(v4(rstd), psg(g2), pbs(g2), rstd)
                S.V(lambda e: e.scalar_tensor_tensor(kf.ap, kf.ap, pcol(gcol), rstd.ap, ALU.mult, ALU.mult),
                    reads=[kf, rstd, params], writes=[kf])
                S.A(lambda e: e.copy(knb.ap, kf.ap), reads=[kf], writes=[knb])

            def postB():
                g3 = next_group(4)
                for tb in range(4):
                    S.T(lambda e, tb=tb: e.matmul(ps[:, g3[tb], :], cm(C_RT), knb.ap[:, tb * 512:(tb + 1) * 512], start=True, stop=True),
                        reads=[knb, cbf], writes=[PB[g3[tb]]])
                S.G(lambda e: e.tensor_tensor(t1.ap, kf.ap, cs.ap[:, 0, :], ALU.mult), reads=[kf, cs], writes=[t1])
                S.V(lambda e: e.tensor_tensor(v4(t2), psg(g3), cs.ap[:, 1, :].rearrange("p (b n) -> p b n", b=4), ALU.mult),
                    reads=pbs(g3) + [cs], writes=[t2])
                S.V(lambda e: e.tensor_tensor(ob.ap, t1.ap, t2.ap, ALU.add), reads=[t1, t2], writes=[ob])
                S.dma("sp", dst, ob.ap, reads=[ob], writes=[dtile])
            return [postA, postB]

        def kv_phase():
            AR.reset()
            h = AR.alloc([KC, S_], BF16, "hkv")
            base = AR.top
            norm_fm(xs, P_KVN, h, 0, S_, 512, base)
            AR.reset(base)
            hb = hnr_bufs()
            vst = [AR.alloc([S_], BF16, f"vst{i}") for i in range(2)]

            def evac(oc, g):
                if oc < 12:
                    return hnr_evac(hb, g, P_KN, kT_d[oc], KT[oc])
                v_s = vst[oc % 2]
                S.A(lambda e: e.copy(v_s.ap.rearrange("p (b n) -> p b n", b=4), psg(g)), reads=pbs(g), writes=[v_s])
                S.dma("sp", vT_d[oc - 12], v_s.ap, reads=[v_s], writes=[VT[oc - 12]])
                return None

            gemm_fm(lambda oc, kq: w_kv[:, oc * 128:(oc + 1) * 128], 24, KC,
                    lambda kc: [h], lambda kc, tb: h.ap[:, kc, tb * 512:(tb + 1) * 512], 4, evac, use_hooks=True)

        def q_phase():
            AR.reset()
            h = AR.alloc([KC, S_], BF16, "hq")
            base = AR.top
            kT, vsb = mem_prep(1, base)
            base2 = vsb.off + vsb.size
            norm_fm(xs, P_MIX + 16, h, 0, S_, 512, base2)
            AR.reset(base2)
            hb = hnr_bufs()
            qf = hb["kf"]
            sq = hb["sq"]
            catst = [AR.alloc([S_], BF16, f"qcatst{i}") for i in range(2)]
            mb = mem_attn_bufs(hb["rstd"])

            def evac(oc, g):
                if oc < 12:
                    return hnr_evac(hb, g, P_QN, qT_d[oc], QT[oc])
                hh = oc - 12
                q_f = qf[hb["i"] % 3]
                s_q = sq[hb["i"] % 2]
                hb["i"] += 1
                cs_ = catst[oc % 2]
                S.A(lambda e: e.copy(q_f.ap.rearrange("p (b n) -> p b n", b=4), psg(g)), reads=pbs(g), writes=[q_f])
                S.A(lambda e: e.activation(s_q.ap.rearrange("p (b n) -> p b n", b=4), psg(g), AF.Square), reads=pbs(g), writes=[s_q])

                def post():
                    mb["qf_tile"] = q_f
                    mem_attn_head(1, hh, q_f.ap, s_q, kT, vsb, mb, cs_)
                return post

            gemm_fm(lambda oc, kq: w_in[1][:, oc * 128:(oc + 1) * 128], KC, KC,
                    lambda kc: [h], lambda kc, tb: h.ap[:, kc, tb * 512:(tb + 1) * 512], 4, evac, use_hooks=True)

        def attn_phase():
            AR.reset()
            kk = [[AR.alloc([S_], BF16, f"kk{s}_{i}") for i in range(2)] for s in range(2)]
            zq = [[AR.alloc([S_], BF16, f"zq{m}_{hf}") for hf in range(2)] for m in range(2)]
            for m in range(2):
                for hf in range(2):
                    z0 = 64 * (1 - hf)
                    S.V(lambda e, m=m, hf=hf, z0=z0: e.memset(zq[m][hf].ap[z0:z0 + 64, :], 0.0), writes=[zq[m][hf]])
            vT = [AR.alloc([S_], BF16, f"vT{i}") for i in range(2)]
            vtok = [AR.alloc([16, 128], BF16, f"vtok{i}") for i in range(2)]
            E = [AR.alloc([512], BF16, f"E{i}") for i in range(8)]
            r12 = AR.alloc([1024], F32, "r12")
            aa = AR.alloc([512], F32, "aa")
            bb = AR.alloc([512], F32, "bb")
            oo = AR.alloc([512], F32, "oo")
            osq = AR.alloc([512], BF16, "osq")
            rs = AR.alloc([512], F32, "ars")
            catst = [AR.alloc([S_], BF16, f"acat{i}") for i in range(2)]
            sbank = {"i": 0}
            spair = {"i": 0}
            ei = {"i": 0}
            sc = 64.0 ** -0.5
            BO = (4, 5)
            BL = (6, 7)

            def sb_next():
                b = sbank["i"] % 4
                sbank["i"] += 1
                return b

            for hp in range(6):
                k2t = kk[hp % 2]
                S.dma("sp", k2t[0].ap, kT_d[hp], reads=[KT[hp]], writes=[k2t[0]])
                S.dma("sp", k2t[1].ap, kT_d[6 + hp], reads=[KT[6 + hp]], writes=[k2t[1]])
                for m in range(2):
                    for hf in range(2):
                        q0 = 64 * hf
                        S.dma("sp", zq[m][hf].ap[q0:q0 + 64, :], qT_d[6 * m + hp][q0:q0 + 64, :],
                              reads=[QT[6 * m + hp]], writes=[zq[m][hf]])
                for half in range(2):
                    hd = 2 * hp + half
                    p0 = 64 * half
                    v_T = vT[hd % 2]
                    v_k = vtok[hd % 2]
                    S.dma("sp", v_T.ap, vT_d[hd], reads=[VT[hd]], writes=[v_T])
                    for t4i in range(4):
                        b = sb_next()
                        pb16 = ps[:, b, :].bitcast(BF16)
                        for j in range(4):
                            tt = t4i * 4 + j
                            S.T(lambda e, pb16=pb16, j=j, tt=tt, v_T=v_T: e.transpose(
                                pb16[:, j * 128:(j + 1) * 128], v_T.ap[:, tt * 128:(tt + 1) * 128], cm(C_IDENT)),
                                reads=[v_T, cbf], writes=[PB[b]])
                        S.V(lambda e, pb16=pb16, t4i=t4i, v_k=v_k: e.tensor_copy(
                            v_k.ap[:, t4i * 4:(t4i + 1) * 4, :], pb16[:, 0:512].rearrange("p (j d) -> p j d", j=4)),
                            reads=[PB[b]], writes=[v_k])
                    cs_ = catst[hd % 2]
                    DEPTH = 3
                    queue = []
                    tails = []

                    def issue_S(qb, kb, m, half=half, k2t=k2t):
                        nkb = 4 * qb + 4
                        i_d = kb - 4 * qb
                        c0 = 128 * i_d if i_d > 0 else 0
                        qt = zq[m][half]
                        kt = k2t[m]
                        bS = sb_next()
                        diag = i_d >= 0
                        S.T(lambda e: e.matmul(ps[:, bS, c0:512], kt.ap[:, kb * 128:(kb + 1) * 128],
                                               qt.ap[:, qb * 512 + c0:(qb + 1) * 512], start=True, stop=(not diag)),
                            reads=[kt, qt], writes=[PB[bS]])
                        if diag:
                            S.T(lambda e: e.matmul(ps[:, bS, c0:c0 + 128], cm(C_IDENT), cm(C_TRIB), start=False, stop=True),
                                reads=[cbf], writes=[PB[bS]])
                        Et = E[ei["i"] % len(E)]
                        ei["i"] += 1
                        S.A(lambda e: e.activation(Et.ap[:, c0:512], ps[:, bS, c0:512], AF.Exp, scale=sc),
                            reads=[PB[bS]], writes=[Et])
                        return (qb, kb, m, c0, nkb, Et)

                    def post_a(qb):
                        S.V(lambda e: e.tensor_copy(aa.ap, ps[:, BO[0], :]), reads=[PB[BO[0]]], writes=[aa])
                        S.V(lambda e: e.tensor_copy(bb.ap, ps[:, BO[1], :]), reads=[PB[BO[1]]], writes=[bb])
                        recip_act(r12.ap.rearrange("p (b n) -> p b n", b=2), ps[:, BL[0]:BL[1] + 1, :], [PB[BL[0]], PB[BL[1]]], r12)
                        S.V(lambda e: e.tensor_tensor(aa.ap, aa.ap, r12.ap[:, 0:512], ALU.mult), reads=[aa, r12], writes=[aa])
                        S.V(lambda e: e.tensor_tensor(bb.ap, bb.ap, r12.ap[:, 512:1024], ALU.mult), reads=[bb, r12], writes=[bb])
                        S.V(lambda e: e.scalar_tensor_tensor(oo.ap, bb.ap, NEGLAM, aa.ap, ALU.mult, ALU.add),
                            reads=[bb, aa, small], writes=[oo])
                        S.V(lambda e: e.tensor_tensor(osq.ap, oo.ap, oo.ap, ALU.mult), reads=[oo], writes=[osq])

                    def post_b(qb, cs_=cs_):
                        bs2 = sb_next()
                        S.T(lambda e: e.matmul(ps[:, bs2, :], cm(C_ONESHD), osq.ap, start=True, stop=True),
                            reads=[osq, cbf], writes=[PB[bs2]])
                        rsqrt_act(rs.ap, ps[:, bs2, :], [PB[bs2]], rs)
                        S.V(lambda e: e.scalar_tensor_tensor(cs_.ap[:, qb * 512:(qb + 1) * 512], oo.ap, SUBG, rs.ap, ALU.mult, ALU.mult),
                            reads=[oo, rs, small], writes=[cs_])

                    def issue_PV(info, v_k=v_k):
                        qb, kb, m, c0, nkb, Et = info
                        S.T(lambda e: e.matmul(ps[:, BO[m], c0:512], v_k.ap[:, kb, :], Et.ap[:, c0:512],
                                               start=(kb == 0), stop=(kb == nkb - 1)),
                            reads=[v_k, Et], writes=[PB[BO[m]]])
                        S.T(lambda e: e.matmul(ps[:, BL[m], c0:512], cm(C_ONES1), Et.ap[:, c0:512],
                                               start=(kb == 0), stop=(kb == nkb - 1)),
                            reads=[cbf, Et], writes=[PB[BL[m]]])
                        for t in tails:
                            t[0] -= 1
                        while tails and tails[0][0] <= 0:
                            tails.pop(0)[1]()
                        if kb == nkb - 1 and m == 1:
                            post_a(qb)
                            tails.append([5, lambda qb=qb: post_b(qb)])

                    steps = [(qb, kb, m) for qb in range(4) for kb in range(4 * qb + 4) for m in range(2)]
                    for st_ in steps:
                        queue.append(issue_S(*st_))
                        if len(queue) > DEPTH:
                            issue_PV(queue.pop(0))
                    while queue:
                        issue_PV(queue.pop(0))
                    while tails:
                        tails.pop(0)[1]()
                    S.dma("sp", cat_d[hd], cs_.ap, reads=[cs_], writes=[CAT[hd]])

        S.dma("sp", params.ap, params_d, writes=[params])
        S.dma("sp", cbf.ap, cbf_d.rearrange("p (a b) -> p a b", a=NCB), writes=[cbf])
        S.V(lambda e: e.memset(EPSC, EPS), writes=[small])
        AR.reset()
        lt = AR.alloc([64], F32, "lam_tmp")
        l4 = AR.alloc([4], F32, "lam4")
        dl = lambda i: params.ap[:, P_DL + 64 * i:P_DL + 64 * (i + 1)]
        S.V(lambda e: e.tensor_tensor(lt.ap, dl(0), dl(1), ALU.mult), reads=[params], writes=[lt])
        S.V(lambda e: e.reduce_sum(l4.ap[:, 0:1], lt.ap, axis=AX.X), reads=[lt], writes=[l4])
        S.V(lambda e: e.tensor_tensor(lt.ap, dl(2), dl(3), ALU.mult), reads=[params, l4], writes=[lt])
        S.V(lambda e: e.reduce_sum(l4.ap[:, 1:2], lt.ap, axis=AX.X), reads=[lt], writes=[l4])
        S.A(lambda e: e.activation(l4.ap[:, 2:4], l4.ap[:, 0:2], AF.Exp), reads=[l4], writes=[l4])
        S.V(lambda e: e.tensor_tensor(NEGLAM, l4.ap[:, 3:4], l4.ap[:, 2:3], ALU.subtract), reads=[l4], writes=[small])
        S.V(lambda e: e.tensor_scalar_add(NEGLAM, NEGLAM, -LAM_INIT), reads=[small], writes=[small])
        S.V(lambda e: e.tensor_scalar_mul(SUBG, pcol(P_SUB), 1.0 - LAM_INIT), reads=[params], writes=[small])

        phases = [
            lambda: None,
            lambda: transpose_in(mem_in, memT, MEM, lambda dc, tb: [MEMT]),
            mixer0,
            lambda: out_proj(0),
            lambda: ffn(0),
            kv_phase,
            q_phase,
            attn_phase,
            lambda: out_proj(1),
            lambda: ffn(1, fuse_out=True),
        ]
        for i, ph in enumerate(phases):
            if i > stop_after:
                break
            ph()
        if stop_after < len(phases) - 1:
            transpose_out()

        with nc.Block() as block:
            S.emit(block, engsem, rings)
    return nc


_CACHE = {}


def kernel(**inputs):
    stop_after = int(os.environ.get("MK_STOP", "99"))
    dbg = os.environ.get("MK_DBG", "0") == "1"
    inp = {k: np.asarray(v) for k, v in inputs.items()}
    key = (stop_after, dbg)
    if key not in _CACHE:
        _CACHE[key] = build_nc(stop_after, dbg)
    nc = _CACHE[key]
    params = pack_params(inp)
    cbf = const_bf16()
    cs = rope_cs()
    shared = {
        "pool_w": np.ascontiguousarray(inp["pool_w"][0], np.float32),
        "w_kv": np.ascontiguousarray(inp["w_kv"], np.float32),
        "params": params, "cbf": cbf, "ropecs": cs,
    }
    for l in range(2):
        shared[f"w_in{l}"] = np.ascontiguousarray(inp["w_in"][l], np.float32)
        shared[f"w_out{l}"] = np.ascontiguousarray(inp["w_out"][l], np.float32)
        shared[f"w_mkv{l}"] = np.ascontiguousarray(inp["w_mem_kv"][l], np.float32)
        shared[f"w_ff1{l}"] = np.ascontiguousarray(inp["w_ff1"][l], np.float32)
        shared[f"w_ff2{l}"] = np.ascontiguousarray(inp["w_ff2"][l], np.float32)
    in_maps = []
    for b in range(8):
        m = dict(shared)
        m["x"] = np.ascontiguousarray(inp["x"][b], np.float32)
        m["mem"] = np.ascontiguousarray(inp["mem"][b], np.float32)
        in_maps.append(m)
    ncores = int(os.environ.get("MK_CORES", "8"))
    res = run_bass_kernel_spmd(nc, in_maps[:ncores], core_ids=list(range(ncores)))
    if dbg:
        kernel.last_results = res.results
    out = np.stack([np.asarray(r["out"], np.float32) for r in res.results], axis=0)
    return out
```
